# Optimizing a Trainium2 kernel written in Bass

```python
import jax, jax.numpy as jnp
from jax import lax
import numpy as np

D_MODEL = 1024
BATCH = 16
SEQ = 256
DEPTH = 2
DEC_BATCH = 2
DEC_SEQ = 1024
PAST_LEN = 256

GRID_W = 64
N_EVEN_LAYERS = (DEPTH + 1) // 2
N_ODD_LAYERS = DEPTH // 2
N_SUB = 3
FFN_RESIDUAL = 0.5
EPS = 1e-6
NEG_INF = -1e30

ATTN_HEADS = 8
ATTN_KV_HEADS = 2
ATTN_GROUP = ATTN_HEADS // ATTN_KV_HEADS
HEAD_DIM = 64
WINDOW = 128
ATTN_BLOCK = 128
ROPE_BASE = 10000.0

GLA_HEADS = 4
GLA_DK = 64
GLA_DV = 128
GLA_RANK = 16
GLA_TAU = 16.0
GLA_CHUNK = 32

A_Q = ATTN_HEADS * HEAD_DIM
A_KV = ATTN_KV_HEADS * HEAD_DIM
B_QK = GLA_HEADS * GLA_DK
B_V = GLA_HEADS * GLA_DV
EVEN_SPLITS = (A_Q, A_KV, A_KV, B_QK, B_QK, B_V, B_V, GLA_RANK, GLA_RANK)
EVEN_SPLIT_IDX = tuple(np.cumsum(EVEN_SPLITS)[:-1].tolist())
EVEN_IN = sum(EVEN_SPLITS)
EVEN_OUT = A_Q + B_V

CHUNK = 128
CMLP_WIDTH = D_MODEL
CMLP_GROUPS = 4

D_FF = 2816

kernel_name = 'hybrid_diffusion_prefix_step'


def rmsnorm(x, g):
    xf = x.astype(jnp.float32)
    y = xf * lax.rsqrt(jnp.mean(xf * xf, axis=-1, keepdims=True) + EPS)
    return (y * g.astype(jnp.float32)).astype(x.dtype)


def adaln(cond, w, b):
    return (jax.nn.silu(cond) @ w + b).reshape(-1, N_SUB, 3, D_MODEL)


def sub_in(x, mod, i, g_pre):
    return rmsnorm(x, g_pre) * (1 + mod[:, i, 1][:, None]) + mod[:, i, 0][:, None]


def sub_out(x, y, mod, i, g_post, coef):
    return x + coef * mod[:, i, 2][:, None] * rmsnorm(y, g_post)


def macaron_ffn(x, mod, i, g_pre, g_post, wg, wu, wd):
    h = sub_in(x, mod, i, g_pre)
    y = (jax.nn.silu(h @ wg) * (h @ wu)) @ wd
    return sub_out(x, y, mod, i, g_post, FFN_RESIDUAL)


def axial_rope(x):
    n = x.shape[1]
    rows = n // GRID_W
    row = jnp.repeat(jnp.arange(rows), GRID_W)
    col = jnp.tile(jnp.arange(GRID_W), rows)
    half = HEAD_DIM // 2
    nf = half // 2
    inv = ROPE_BASE ** (-jnp.arange(nf, dtype=jnp.float32) * 2.0 / half)
    ang = jnp.stack([row[:, None] * inv, col[:, None] * inv], axis=1)
    cos = jnp.cos(ang)[None, :, None].astype(x.dtype)
    sin = jnp.sin(ang)[None, :, None].astype(x.dtype)
    xr = x.reshape(x.shape[:-1] + (2, 2, nf))
    x1, x2 = xr[..., 0, :], xr[..., 1, :]
    out = jnp.stack([x1 * cos - x2 * sin, x2 * cos + x1 * sin], axis=-2)
    return out.reshape(x.shape)


def context_attention(q, k, v, sink):
    B, P = q.shape[:2]
    s = jnp.einsum('bpkgd,bskd->bkgps', q, k).astype(jnp.float32)
    sk = jnp.broadcast_to(sink.astype(jnp.float32)[None, :, :, None, None], s.shape[:-1] + (1,))
    p = jax.nn.softmax(jnp.concatenate([s, sk], axis=-1), axis=-1)[..., :-1].astype(v.dtype)
    o = jnp.einsum('bkgps,bskd->bpkgd', p, v)
    return o.reshape(B, P, A_Q)


def banded_attention(q, k, v, k_ctx, v_ctx, sink):
    B, L = q.shape[:2]
    nb = L // ATTN_BLOCK
    P = k_ctx.shape[1]
    qb = q.reshape(B, nb, ATTN_BLOCK, ATTN_KV_HEADS, ATTN_GROUP, HEAD_DIM)

    def windows(t):
        tp = jnp.pad(t, ((0, 0), (ATTN_BLOCK, ATTN_BLOCK), (0, 0), (0, 0)))
        tp = tp.reshape(B, nb + 2, ATTN_BLOCK, ATTN_KV_HEADS, HEAD_DIM)
        return jnp.concatenate([tp[:, :-2], tp[:, 1:-1], tp[:, 2:]], axis=2)

    kw, vw = windows(k), windows(v)
    qi = jnp.arange(nb)[:, None] * ATTN_BLOCK + jnp.arange(ATTN_BLOCK)[None, :]
    kj = (jnp.arange(nb)[:, None] - 1) * ATTN_BLOCK + jnp.arange(3 * ATTN_BLOCK)[None, :]
    mask = ((jnp.abs(qi[:, :, None] - kj[:, None, :]) <= WINDOW)
            & (kj[:, None, :] >= 0) & (kj[:, None, :] < L))
    s_loc = jnp.einsum('bnqkgd,bnskd->bkgnqs', qb, kw).astype(jnp.float32)
    s_loc = jnp.where(mask, s_loc, NEG_INF)
    s_ctx = jnp.einsum('bnqkgd,bpkd->bkgnqp', qb, k_ctx).astype(jnp.float32)
    sk = jnp.broadcast_to(sink.astype(jnp.float32)[None, :, :, None, None, None], s_loc.shape[:-1] + (1,))
    p = jax.nn.softmax(jnp.concatenate([s_loc, s_ctx, sk], axis=-1), axis=-1)
    n_loc = 3 * ATTN_BLOCK
    p_loc = p[..., :n_loc].astype(v.dtype)
    p_ctx = p[..., n_loc:n_loc + P].astype(v.dtype)
    o = (jnp.einsum('bkgnqs,bnskd->bnqkgd', p_loc, vw)
         + jnp.einsum('bkgnqp,bpkd->bnqkgd', p_ctx, v_ctx))
    return o.reshape(B, L, A_Q)


def gla_chunked(q, k, v, logd, s0):
    B, L, H, K = q.shape
    V = v.shape[-1]
    C = GLA_CHUNK
    n = L // C
    f32 = jnp.float32
    qc = q.astype(f32).reshape(B, n, C, H, K)
    kc = k.astype(f32).reshape(B, n, C, H, K)
    vc = v.astype(f32).reshape(B, n, C, H, V)
    b = jnp.cumsum(logd.astype(f32).reshape(B, n, C, H, K), axis=2)
    b_last = b[:, :, -1]
    causal = jnp.tril(jnp.ones((C, C), dtype=bool))
    diff = b[:, :, :, None] - b[:, :, None, :]
    dec = jnp.exp(jnp.where(causal[:, :, None, None], diff, -jnp.inf))
    a = jnp.einsum('bnihk,bnjhk,bnijhk->bnhij', qc, kc, dec)
    o_intra = jnp.einsum('bnhij,bnjhv->bnihv', a, vc)
    u = jnp.einsum('bnjhk,bnjhv->bnhkv', kc * jnp.exp(b_last[:, :, None] - b), vc)

    def step(s, xs):
        d, uu = xs
        return d[..., None] * s + uu, s

    s_fin, s_prev = lax.scan(step, s0.astype(f32),
                             (jnp.moveaxis(jnp.exp(b_last), 1, 0), jnp.moveaxis(u, 1, 0)))
    s_prev = jnp.moveaxis(s_prev, 0, 1)
    o_inter = jnp.einsum('bnihk,bnhkv->bnihv', qc * jnp.exp(b), s_prev)
    o = (o_intra + o_inter).reshape(B, L, H, V).astype(v.dtype)
    return o, s_fin


def bi_gla(q, k, v, ld_f, ld_b, s_f0, s_b0):
    o_f, s_f = gla_chunked(q, k, v, ld_f, s_f0)
    o_b, s_b = gla_chunked(jnp.flip(q, 1), jnp.flip(k, 1), jnp.flip(v, 1), jnp.flip(ld_b, 1), s_b0)
    return o_f + jnp.flip(o_b, 1), s_f, s_b


def even_project(h, w_in, wa_f, ba_f, wa_b, ba_b):
    B, L, _ = h.shape
    qa, ka, va, qb, kb, vb, gb, lrf, lrb = jnp.split(h @ w_in, EVEN_SPLIT_IDX, axis=-1)
    qa = qa.reshape(B, L, ATTN_HEADS, HEAD_DIM)
    ka = ka.reshape(B, L, ATTN_KV_HEADS, HEAD_DIM)
    va = va.reshape(B, L, ATTN_KV_HEADS, HEAD_DIM)
    qb = qb.reshape(B, L, GLA_HEADS, GLA_DK) * (GLA_DK ** -0.5)
    kb = kb.reshape(B, L, GLA_HEADS, GLA_DK)
    vb = vb.reshape(B, L, GLA_HEADS, GLA_DV)
    ld_f = jax.nn.log_sigmoid((lrf @ wa_f + ba_f).astype(jnp.float32)).reshape(B, L, GLA_HEADS, GLA_DK) / GLA_TAU
    ld_b = jax.nn.log_sigmoid((lrb @ wa_b + ba_b).astype(jnp.float32)).reshape(B, L, GLA_HEADS, GLA_DK) / GLA_TAU
    return qa, ka, va, qb, kb, vb, gb, ld_f, ld_b


def even_output(attn_o, gla_o, gb, gla_g, w_out):
    B, L = attn_o.shape[:2]
    g_o = rmsnorm(gla_o, gla_g.reshape(GLA_HEADS, GLA_DV)).reshape(B, L, B_V) * jax.nn.silu(gb)
    return jnp.concatenate([attn_o, g_o], axis=-1) @ w_out


def even_mixer_context(h, w_in, w_out, sink, wa_f, ba_f, wa_b, ba_b, gla_g):
    B, P, _ = h.shape
    qa, ka, va, qb, kb, vb, gb, ld_f, ld_b = even_project(h, w_in, wa_f, ba_f, wa_b, ba_b)
    qa = (qa * (HEAD_DIM ** -0.5)).reshape(B, P, ATTN_KV_HEADS, ATTN_GROUP, HEAD_DIM)
    attn = context_attention(qa, ka, va, sink.reshape(ATTN_KV_HEADS, ATTN_GROUP))
    zero = jnp.zeros((B, GLA_HEADS, GLA_DK, GLA_DV), jnp.float32)
    g_o, s_f, s_b = bi_gla(qb, kb, vb, ld_f, ld_b, zero, zero)
    return even_output(attn, g_o, gb, gla_g, w_out), ka, va, s_f, s_b


def even_mixer_latent(h, k_ctx, v_ctx, s_f0, s_b0, w_in, w_out, sink, wa_f, ba_f, wa_b, ba_b, gla_g):
    B, L, _ = h.shape
    qa, ka, va, qb, kb, vb, gb, ld_f, ld_b = even_project(h, w_in, wa_f, ba_f, wa_b, ba_b)
    qa = (axial_rope(qa) * (HEAD_DIM ** -0.5)).reshape(B, L, ATTN_KV_HEADS, ATTN_GROUP, HEAD_DIM)
    ka = axial_rope(ka)
    attn = banded_attention(qa, ka, va, k_ctx, v_ctx, sink.reshape(ATTN_KV_HEADS, ATTN_GROUP))
    g_o, _, _ = bi_gla(qb, kb, vb, ld_f, ld_b, s_f0, s_b0)
    return even_output(attn, g_o, gb, gla_g, w_out)


def chunk_mlp(h, w_in, v_gain, v_bias, w_s, b_s, w_out):
    B, L, _ = h.shape
    n = L // CHUNK
    z = jax.nn.gelu(h @ w_in, approximate=False)
    u, v = jnp.split(z, 2, axis=-1)
    vf = v.astype(jnp.float32)
    mu = jnp.mean(vf, axis=-1, keepdims=True)
    var = jnp.mean(jnp.square(vf - mu), axis=-1, keepdims=True)
    v = ((vf - mu) * lax.rsqrt(var + EPS) * v_gain.astype(jnp.float32) + v_bias.astype(jnp.float32)).astype(h.dtype)
    v = v.reshape(B, n, CHUNK, CMLP_GROUPS, CMLP_WIDTH // CMLP_GROUPS)
    mixed = jnp.einsum('gts,bnsgc->bntgc', w_s, v) + b_s.T[:, :, None]
    return (u * mixed.reshape(B, L, CMLP_WIDTH)) @ w_out


def setup_inputs(seed: int = 0) -> dict:
    key = jax.random.key(seed)
    ks = iter(jax.random.split(key, 32))

    def nrm(shape, scale=1.0):
        return jax.random.normal(next(ks), shape, jnp.float32) * scale

    return {
        'x_prompt': nrm((BATCH, SEQ, D_MODEL)),
        'x_sample': nrm((DEC_BATCH, DEC_SEQ, D_MODEL)),
        'cache_k': nrm((DEC_BATCH, N_EVEN_LAYERS, PAST_LEN, ATTN_KV_HEADS, HEAD_DIM)),
        'cache_v': nrm((DEC_BATCH, N_EVEN_LAYERS, PAST_LEN, ATTN_KV_HEADS, HEAD_DIM)),
        'state_gla_fwd': nrm((DEC_BATCH, N_EVEN_LAYERS, GLA_HEADS, GLA_DK, GLA_DV), 2.0),
        'state_gla_bwd': nrm((DEC_BATCH, N_EVEN_LAYERS, GLA_HEADS, GLA_DK, GLA_DV), 2.0),
        'c': nrm((DEC_BATCH, D_MODEL)),
        'c_ctx': nrm((D_MODEL,)),
        'w_mod': nrm((DEPTH, D_MODEL, N_SUB * 3 * D_MODEL), 0.5 * D_MODEL ** -0.5),
        'b_mod': nrm((DEPTH, N_SUB * 3 * D_MODEL), 0.01),
        'norm_pre': 1.0 + nrm((DEPTH, N_SUB, D_MODEL), 0.05),
        'norm_post': 1.0 + nrm((DEPTH, N_SUB, D_MODEL), 0.05),
        'ffn_w_gate': nrm((DEPTH, 2, D_MODEL, D_FF), D_MODEL ** -0.5),
        'ffn_w_up': nrm((DEPTH, 2, D_MODEL, D_FF), D_MODEL ** -0.5),
        'ffn_w_down': nrm((DEPTH, 2, D_FF, D_MODEL), D_FF ** -0.5),
        'ev_w_in': nrm((N_EVEN_LAYERS, D_MODEL, EVEN_IN), D_MODEL ** -0.5),
        'ev_w_out': nrm((N_EVEN_LAYERS, EVEN_OUT, D_MODEL), EVEN_OUT ** -0.5),
        'ev_sink': nrm((N_EVEN_LAYERS, ATTN_HEADS)),
        'gla_wa_f': nrm((N_EVEN_LAYERS, GLA_RANK, B_QK), GLA_RANK ** -0.5),
        'gla_ba_f': nrm((N_EVEN_LAYERS, B_QK), 0.01),
        'gla_wa_b': nrm((N_EVEN_LAYERS, GLA_RANK, B_QK), GLA_RANK ** -0.5),
        'gla_ba_b': nrm((N_EVEN_LAYERS, B_QK), 0.01),
        'gla_norm': 1.0 + nrm((N_EVEN_LAYERS, B_V), 0.05),
        'cm_w_in': nrm((N_ODD_LAYERS, D_MODEL, 2 * CMLP_WIDTH), D_MODEL ** -0.5),
        'cm_v_gain': 1.0 + nrm((N_ODD_LAYERS, CMLP_WIDTH), 0.05),
        'cm_v_bias': nrm((N_ODD_LAYERS, CMLP_WIDTH), 0.01),
        'cm_w_s': nrm((N_ODD_LAYERS, CMLP_GROUPS, CHUNK, CHUNK), CHUNK ** -0.5),
        'cm_b_s': 1.0 + nrm((N_ODD_LAYERS, CMLP_GROUPS, CHUNK), 0.05),
        'cm_w_out': nrm((N_ODD_LAYERS, CMLP_WIDTH, D_MODEL), CMLP_WIDTH ** -0.5),
    }


def reference(x_prompt, x_sample, cache_k, cache_v, state_gla_fwd, state_gla_bwd, c, c_ctx,
              w_mod, b_mod, norm_pre, norm_post, ffn_w_gate, ffn_w_up, ffn_w_down,
              ev_w_in, ev_w_out, ev_sink, gla_wa_f, gla_ba_f, gla_wa_b, gla_ba_b, gla_norm,
              cm_w_in, cm_v_gain, cm_v_bias, cm_w_s, cm_b_s, cm_w_out):
    xp, xs = x_prompt, x_sample
    new_k, new_v, new_sf, new_sb = [], [], [], []
    for layer in range(DEPTH):
        mp = adaln(c_ctx[None, :], w_mod[layer], b_mod[layer])
        ms = adaln(c, w_mod[layer], b_mod[layer])
        g_pre, g_post = norm_pre[layer], norm_post[layer]
        xp = macaron_ffn(xp, mp, 0, g_pre[0], g_post[0], ffn_w_gate[layer, 0], ffn_w_up[layer, 0], ffn_w_down[layer, 0])
        xs = macaron_ffn(xs, ms, 0, g_pre[0], g_post[0], ffn_w_gate[layer, 0], ffn_w_up[layer, 0], ffn_w_down[layer, 0])
        hp = sub_in(xp, mp, 1, g_pre[1])
        hs = sub_in(xs, ms, 1, g_pre[1])
        if layer % 2 == 0:
            e = layer // 2
            ev = (ev_w_in[e], ev_w_out[e], ev_sink[e], gla_wa_f[e], gla_ba_f[e], gla_wa_b[e], gla_ba_b[e], gla_norm[e])
            yp, k_c, v_c, s_f, s_b = even_mixer_context(hp, *ev)
            ys = even_mixer_latent(hs, cache_k[:, e], cache_v[:, e], state_gla_fwd[:, e], state_gla_bwd[:, e], *ev)
            new_k.append(k_c)
            new_v.append(v_c)
            new_sf.append(s_f.astype(x_prompt.dtype))
            new_sb.append(s_b.astype(x_prompt.dtype))
        else:
            o = layer // 2
            cm = (cm_w_in[o], cm_v_gain[o], cm_v_bias[o], cm_w_s[o], cm_b_s[o], cm_w_out[o])
            yp = chunk_mlp(hp, *cm)
            ys = chunk_mlp(hs, *cm)
        xp = sub_out(xp, yp, mp, 1, g_post[1], 1.0)
        xs = sub_out(xs, ys, ms, 1, g_post[1], 1.0)
        xp = macaron_ffn(xp, mp, 2, g_pre[2], g_post[2], ffn_w_gate[layer, 1], ffn_w_up[layer, 1], ffn_w_down[layer, 1])
        xs = macaron_ffn(xs, ms, 2, g_pre[2], g_post[2], ffn_w_gate[layer, 1], ffn_w_up[layer, 1], ffn_w_down[layer, 1])
    new_cache_k = jnp.stack(new_k, axis=1)
    new_cache_v = jnp.stack(new_v, axis=1)
    new_state_gla_fwd = jnp.stack(new_sf, axis=1)
    new_state_gla_bwd = jnp.stack(new_sb, axis=1)
    return (xp, xs, new_cache_k, new_cache_v, new_state_gla_fwd, new_state_gla_bwd)
```

```python
import os
import numpy as np
from contextlib import ExitStack
import concourse.bass as bass
import concourse.mybir as mybir
from concourse.bass_utils import run_bass_kernel_spmd

F32 = mybir.dt.float32
F32R = mybir.dt.float32r
AF = mybir.ActivationFunctionType
ALU = mybir.AluOpType
SAME_ENGINE_SYNC = True

D = 1024
KC = 8
T = 768
FF = 2816
FC = 22
EPS = 1e-6
NCORE = 8
GROUPS_MAIN = [(0, 512, 0), (512, 256, 1)]
GROUPS_OTH = [(0, 512, 1), (512, 256, 1)]


class Trk:
    __slots__ = ("writer", "readers")

    def __init__(self):
        self.writer = None
        self.readers = {}


class Tile:
    __slots__ = ("ap", "trks", "split", "name")

    def __init__(self, ap, name="", split=None):
        self.ap = ap
        self.name = name
        self.split = split
        self.trks = [Trk()] if split is None else [Trk(), Trk()]

    def _sel(self, idx):
        if self.split is None:
            return self.trks
        if isinstance(idx, tuple) and len(idx) >= 2 and isinstance(idx[1], slice):
            a, b = idx[1].start, idx[1].stop
            if a is not None and b is not None:
                if b <= self.split:
                    return [self.trks[0]]
                if a >= self.split:
                    return [self.trks[1]]
        return self.trks

    def __getitem__(self, idx):
        return V(self._sel(idx), self.ap[idx])

    def v(self):
        return V(self.trks, self.ap)


class V:
    __slots__ = ("tiles", "ap")

    def __init__(self, tiles, ap):
        self.tiles = tiles
        self.ap = ap

    def __getitem__(self, idx):
        return V(self.tiles, self.ap[idx])

    def r(self):
        return V(self.tiles, self.ap.bitcast(F32R))

    def f(self):
        return V(self.tiles, self.ap.bitcast(F32))

    def re(self, s, **kw):
        return V(self.tiles, self.ap.rearrange(s, **kw))

    def bc(self, shape):
        return V(self.tiles, self.ap.to_broadcast(shape))


def _tiles(vs):
    out = []
    for v in vs:
        if v is None:
            continue
        if isinstance(v, Tile):
            out.extend(v.trks)
        else:
            out.extend(v.tiles)
    return out


class Ctx:
    def __init__(self, nc, dry=False):
        self.nc = nc
        self.dry = dry
        self.engs = {}
        self.sems = {}
        self.count = {}
        self.waited = {}
        self.stack = None
        self.n_dma_sem = 0
        self.n_ps = 0
        self.n_tp = 0
        self.log = {}

    def setup(self, stack):
        self.stack = stack
        nc = self.nc
        self.engs = {"pe": nc.tensor, "act": nc.scalar, "dve": nc.vector,
                     "pool": nc.gpsimd, "sp": nc.sync}
        for k in self.engs:
            self.sems[k] = stack.enter_context(nc.semaphore("s_" + k))
            self.count[k] = 0
            self.waited[k] = {}

    def new_dma_sem(self, name):
        key = "dma_%s_%d" % (name, self.n_dma_sem)
        self.n_dma_sem += 1
        self.sems[key] = self.stack.enter_context(self.nc.semaphore(key))
        self.count[key] = 0
        return key

    def _deps(self, ek, reads, writes):
        deps = {}

        def add(w):
            if w is None:
                return
            k, c = w
            if deps.get(k, 0) < c:
                deps[k] = c
        for t in reads:
            add(t.writer)
        for t in writes:
            add(t.writer)
            for k, c in t.readers.items():
                add((k, c))
        eng = self.engs[ek]
        for k, c in deps.items():
            if k == ek and (ek == "pe" or not SAME_ENGINE_SYNC):
                continue
            if k.startswith("dma_"):
                c = self.count[k]
            if self.waited[ek].get(k, 0) >= c:
                continue
            eng.wait_ge(self.sems[k], c)
            self.waited[ek][k] = c
            self.log.setdefault(ek, []).append(("w", k, c))

    def op(self, ek, fn, reads=(), writes=(), inc=True):
        rt = _tiles(reads)
        wt = _tiles(writes)
        if self.dry:
            return
        self._deps(ek, rt, wt)
        ins = fn(self.engs[ek])
        idx = self.count[ek] + 1
        if inc:
            ins.then_inc(self.sems[ek], 1)
            self.count[ek] = idx
        self.log.setdefault(ek, []).append(("i", ek if inc else None, 1))
        for t in wt:
            t.writer = (ek, idx)
            t.readers = {}
        for t in rt:
            if t.readers.get(ek, 0) < idx:
                t.readers[ek] = idx

    def dma(self, qk, out, in_, semkey, reads=(), writes=()):
        rt = _tiles(reads)
        wt = _tiles(writes)
        if self.dry:
            return
        self._deps(qk, rt, wt)
        self.count[semkey] += 16
        c = self.count[semkey]
        self.engs[qk].dma_start(out=out, in_=in_).then_inc(self.sems[semkey], 16)
        self.log.setdefault(qk, []).append(("i", semkey, 16))
        for t in wt:
            t.writer = (semkey, c)
            t.readers = {}
        for t in rt:
            t.readers[semkey] = c

    def group_fix(self, semkey, tiles):
        if self.dry:
            return
        for t in _tiles(tiles):
            t.writer = (semkey, self.count[semkey])

    def finish(self, ek, semkeys):
        if self.dry:
            return
        for k in semkeys:
            if self.count[k] > 0:
                self.engs[ek].wait_ge(self.sems[k], self.count[k])

    def mm(self, out, lhsT, rhs, start, stop, inc=None):
        self.op("pe", lambda e: e.matmul(out.ap, lhsT=lhsT.ap, rhs=rhs.ap, start=start, stop=stop),
                reads=[lhsT, rhs], writes=[out], inc=(True if inc is None else inc))

    def act(self, out, in_, func, bias=None, scale=1.0, eng="act"):
        rd = [in_]
        kw = {}
        if bias is not None:
            if isinstance(bias, V):
                rd.append(bias)
                kw["bias"] = bias.ap
            else:
                kw["bias"] = bias
        if isinstance(scale, V):
            rd.append(scale)
            kw["scale"] = scale.ap
        else:
            kw["scale"] = scale
        self.op("act", lambda e: e.activation(out=out.ap, in_=in_.ap, func=func, **kw), reads=rd, writes=[out])

    def tt(self, eng, out, a, b, op):
        self.op(eng, lambda e: e.tensor_tensor(out=out.ap, in0=a.ap, in1=b.ap, op=op), reads=[a, b], writes=[out])

    def ts(self, eng, out, a, s1, op0, s2=None, op1=None):
        rd = [a]
        if isinstance(s1, V):
            rd.append(s1)
        if isinstance(s2, V):
            rd.append(s2)
        a1 = s1.ap if isinstance(s1, V) else s1
        a2 = s2.ap if isinstance(s2, V) else s2
        if op1 is None:
            self.op(eng, lambda e: e.tensor_scalar(out=out.ap, in0=a.ap, scalar1=a1, scalar2=None, op0=op0),
                    reads=rd, writes=[out])
        else:
            self.op(eng, lambda e: e.tensor_scalar(out=out.ap, in0=a.ap, scalar1=a1, scalar2=a2, op0=op0, op1=op1),
                    reads=rd, writes=[out])

    def stt(self, eng, out, a, s, b, op0, op1):
        rd = [a, b]
        if isinstance(s, V):
            rd.append(s)
        sa = s.ap if isinstance(s, V) else s
        self.op(eng, lambda e: e.scalar_tensor_tensor(out=out.ap, in0=a.ap, scalar=sa, in1=b.ap, op0=op0, op1=op1),
                reads=rd, writes=[out])

    def copy(self, eng, out, in_):
        if eng == "act":
            self.op("act", lambda e: e.copy(out=out.ap, in_=in_.ap), reads=[in_], writes=[out])
        else:
            self.op(eng, lambda e: e.tensor_copy(out=out.ap, in_=in_.ap), reads=[in_], writes=[out])

    def recip(self, out, in_):
        self.op("dve", lambda e: e.reciprocal(out=out.ap, in_=in_.ap), reads=[in_], writes=[out])

    def memset(self, eng, out, val):
        self.op(eng, lambda e: e.memset(out.ap, val), writes=[out])


class WStream:
    NS = 5
    SLOT = 2048

    def __init__(self, cx, nc, stack, plan):
        self.cx = cx
        self.plan = plan if plan is not None else []
        self.record = plan is None
        self.cur = 0
        self.issued = 0
        self.hold = 1
        self.slots = []
        self.semk = []
        for i in range(self.NS):
            t = stack.enter_context(nc.sbuf_tensor("wslot%d" % i, [128, self.SLOT], F32R))
            self.slots.append(Tile(t[:], "wslot%d" % i))
            self.semk.append(cx.new_dma_sem("w%d" % i))

    def _view(self, i, shape):
        s = self.slots[i % self.NS]
        n = 1
        for d in shape[1:]:
            n *= d
        assert n <= self.SLOT, shape
        v = s[0:shape[0], 0:n]
        if len(shape) == 3:
            v = v.re("p (a b) -> p a b", a=shape[1])
        return v

    def get(self, dram_ap, shape):
        i = self.cur
        self.cur += 1
        if self.record:
            self.plan.append((dram_ap, tuple(shape)))
            return self._view(i, shape)
        assert self.plan[i][1] == tuple(shape), (i, self.plan[i][1], shape)
        lim = min(len(self.plan), i + self.NS - self.hold + 1)
        while self.issued < lim:
            j = self.issued
            ap_j, shp_j = self.plan[j]
            v = self._view(j, shp_j)
            self.cx.dma("sp", v.ap, ap_j.bitcast(F32R), self.semk[j % self.NS], writes=[v])
            self.issued += 1
        return self._view(i, shape)


class Prog:
    def __init__(self, nc, dry, plan, stage):
        self.nc = nc
        self.dry = dry
        self.plan = plan
        self.stage = stage
        self.cut = int(os.environ.get('MK_CUT', '999'))

    def dram_in(self, name, shape):
        return self.nc.dram_tensor(name, list(shape), F32, kind="ExternalInput").ap()

    def dram_out(self, name, shape):
        return self.nc.dram_tensor(name, list(shape), F32, kind="ExternalOutput").ap()

    def sb(self, name, shape, dt=F32):
        return self.st.enter_context(self.nc.sbuf_tensor("sb_" + name, list(shape), dt))

    def psum(self):
        t = self.ps[self.cx.n_ps % self.ps_rr_n]
        self.cx.n_ps += 1
        return t

    def psum_stat(self, gi):
        return self.ps[6 + gi]

    def tmp(self):
        t = self.tp[self.cx.n_tp % len(self.tp)]
        self.cx.n_tp += 1
        return t

    def tmpr(self):
        t = self.tpr[self.n_tpr % len(self.tpr)]
        self.n_tpr += 1
        return t

    def build(self):
        nc = self.nc
        with ExitStack() as st:
            self.st = st
            cx = self.cx = Ctx(nc, self.dry)
            cx.setup(st)
            self.W = WStream(cx, nc, st, self.plan)
            self.declare_io()
            self.alloc()
            self.load_consts()
            self.mod_pending = [(l, nb) for l in range(2) for nb in range(36)]
            self.prepass()
            G = GROUPS_MAIN
            self.load_x(self.d_xm)
            self.boundary(None, (0, 0), G)
            self.ffn(0, 0, G)
            self.boundary((0, 0), (0, 1), G)
            self.even_mixer()
            self.boundary((0, 1), (0, 2), G)
            self.ffn_feed_l1 = True
            self.ffn(0, 1, G)
            self.boundary((0, 2), (1, 0), G)
            self.ffn(1, 0, G)
            self.boundary((1, 0), (1, 1), G)
            self.odd_mixer()
            self.boundary((1, 1), (1, 2), G)
            self.ffn(1, 1, G)
            self.boundary((1, 2), None, G)
            self.store_out()
            cx.finish("sp", [self.dsem_out])
        return self.W.plan

    def declare_io(self):
        di = self.dram_in
        self.d_xm = di("xm", [128, KC, T])
        self.d_xo = di("xo", [128, KC, T])
        self.d_cond = di("cond", [128, KC, 2])
        self.d_wmod = di("w_mod", [2, D, 9 * D])
        self.d_bmod = di("b_mod", [128, 2 * 72])
        self.d_npre = di("npre", [128, 2 * 3 * KC])
        self.d_npost = di("npost", [128, 2 * 3 * KC])
        self.d_wg = di("ffn_w_gate", [2, 2, D, FF])
        self.d_wu = di("ffn_w_up", [2, 2, D, FF])
        self.d_wd = di("ffn_w_down", [2, 2, FF, D])
        self.d_ident = di("ident", [128, 128])
        self.d_winfm = di("w_in_fm", [D, 2688])
        self.d_wintm = di("w_in_tm", [D, 1280])
        self.d_wout = di("ev_w_out", [D, D])
        self.d_ropec = di("rope_c", [128, 512])
        self.d_ropes = di("rope_s", [128, 512])
        self.d_tri = di("tri", [128, 4 * 128])
        self.d_mm = di("gmask", [128, 4 * 128])
        self.d_fl = di("flags", [128, 6])
        self.d_ck = di("ctx_k", [128, 2 * 256])
        self.d_cv = di("ctx_v", [128, 2 * 2 * 128])
        self.d_s0f = di("s0f", [128, 256])
        self.d_s0b = di("s0b", [128, 256])
        self.d_sink = di("sinkT", [128, 4])
        self.d_gg = di("gla_g", [128, 4])
        self.d_wa = di("gla_wa", [64, 256])
        self.d_ba = di("gla_ba", [64, 256])
        self.d_cmwin = di("cm_w_in", [D, 2 * D])
        self.d_cmwout = di("cm_w_out", [D, D])
        self.d_cmgb = di("cm_gb", [128, 2 * D])
        self.d_cmws = di("cm_ws", [128, 512])
        self.d_cmbs = di("cm_bs", [128, 512])
        self.o_kv = self.dram_out("kv_out", [128, 4, 256])
        self.o_sf = self.dram_out("sf_out", [128, 2, 256])
        self.o_sb = self.dram_out("sb_out", [128, 2, 256])
        self.o_y = self.dram_out("ym", [128, KC, T])

    def alloc(self):
        nc, st, cx = self.nc, self.st, self.cx
        xt = self.sb("xT", [128, KC, T], F32)
        self.X = [Tile(xt[:, k, :], "x%d" % k, split=512) for k in range(KC)]
        ar = self.sb("arena", [128, 30, T], F32R)
        self.PG = [Tile(ar[:, i, :], "pg%d" % i, split=512) for i in range(30)]
        self.H = self.PG[0:8]
        self.ACT = self.PG[8:30]
        self.tp = [Tile(self.sb("tp%d" % i, [128, 512], F32)[:], "tp%d" % i) for i in range(3)]
        self.tpr = [Tile(self.sb("tpr%d" % i, [128, 512], F32R)[:], "tpr%d" % i) for i in range(4)]
        self.n_tpr = 0
        self.ps = [Tile(st.enter_context(nc.psum_tensor("ps%d" % i, [128, 512], F32))[:], "ps%d" % i) for i in range(8)]
        self.rstd = Tile(self.sb("rstd", [128, T], F32)[:], "rstd", split=512)
        self.cond = Tile(self.sb("cond", [128, KC, 2], F32)[:], "cond")
        self.scT = Tile(self.sb("scT", [128, KC, 2], F32R)[:], "scT")
        self.bm = Tile(self.sb("bm", [128, 2, 72], F32)[:], "bm")
        self.npre = Tile(self.sb("npre", [128, 2, 3, KC], F32)[:], "npre")
        self.npost = Tile(self.sb("npost", [128, 2, 3, KC], F32)[:], "npost")
        self.ident = Tile(self.sb("ident", [128, 128], F32)[:], "ident")
        self.onesF = Tile(self.sb("onesF", [128, 128], F32R)[:], "onesF")
        self.epsb = Tile(self.sb("epsb", [128, 1], F32)[:], "epsb")
        self.modT = Tile(self.sb("modT", [128, 2, 72, 2], F32)[:], "modT")
        mrt = self.sb("modrow", [2, 2, 256], F32)
        self.modrow = [Tile(mrt[:, i, :], "modrow%d" % i) for i in range(2)]
        self.mod_unfinished = None
        self.ffn_feed_l1 = False
        self.defer_B = None
        self.y_pending = []
        self.tab = Tile(self.sb("tab", [128, 2, 3, 2, 3, KC], F32)[:], "tab")
        self.tri = Tile(self.sb("tri", [128, 4, 128], F32)[:], "tri")
        self.MM = Tile(self.sb("gmask", [128, 4, 128], F32R)[:], "gmask")
        self.FL = Tile(self.sb("flags", [128, 6], F32)[:], "flags")
        self.CK = Tile(self.sb("ctxk", [128, 2, 256], F32R)[:], "ctxk")
        self.CV = Tile(self.sb("ctxv", [128, 2, 2, 128], F32R)[:], "ctxv")
        self.HK = Tile(self.sb("hk", [128, 2, 2, 128], F32R)[:], "hk")
        self.HV = Tile(self.sb("hv", [128, 2, 256], F32R)[:], "hv")
        self.S = [Tile(self.sb("st%d" % i, [128, 2, 128], F32R)[:], "st%d" % i) for i in range(7)]
        self.FIN = [Tile(self.sb("fin%d" % i, [128, 2, 2, 128], F32)[:], "fin%d" % i) for i in range(2)]
        self.sinkT = Tile(self.sb("sinkT", [128, 4], F32)[:], "sinkT")
        self.esink = Tile(self.sb("esink", [128, 4], F32)[:], "esink")
        self.gg = Tile(self.sb("gg", [128, 4], F32)[:], "gg")
        self.WA = Tile(self.sb("wa", [64, 256], F32R)[:], "wa")
        self.BA = Tile(self.sb("ba", [64, 256], F32R)[:], "ba")
        self.DEC = Tile(self.sb("dec", [128, 12, 2], F32)[:], "dec")
        self.DM = Tile(self.sb("dm", [128, 12, 2], F32)[:], "dm")
        lt = self.sb("lnst", [128, 6, 40], F32)
        self.lnst = [Tile(lt[:, i, :], "lnst%d" % i) for i in range(6)]
        self.lnst_all = lt[:]
        self.ps_rr_n = 6
        self.n_ev = 0
        self.dsem_c = cx.new_dma_sem("const")
        self.dsem_c2 = cx.new_dma_sem("const2")
        self.dsem_cs = cx.new_dma_sem("const_sp")
        self.dsem_x = cx.new_dma_sem("x")
        self.dsem_out = cx.new_dma_sem("out")

    def load_consts(self):
        cx = self.cx
        q = "pool"
        dsem_cond = cx.new_dma_sem("cond")
        cx.dma("sp", self.cond.ap, self.d_cond, dsem_cond, writes=[self.cond])
        cx.act(self.scT.v(), self.cond.v(), AF.Silu)
        cx.dma(q, self.npre.ap, self.d_npre.rearrange("p (l i k) -> p l i k", l=2, i=3), self.dsem_c, writes=[self.npre])
        cx.dma(q, self.npost.ap, self.d_npost.rearrange("p (l i k) -> p l i k", l=2, i=3), self.dsem_c, writes=[self.npost])
        cx.dma(q, self.ident.ap, self.d_ident, self.dsem_c, writes=[self.ident])
        cx.dma(q, self.bm.ap, self.d_bmod.rearrange("p (l c) -> p l c", l=2), self.dsem_c, writes=[self.bm])
        for (t, d, shp) in [(self.tri, self.d_tri, "p (a b) -> p a b"), (self.FL, self.d_fl, None), (self.sinkT, self.d_sink, None),
                            (self.gg, self.d_gg, None)]:
            src = d if shp is None else d.rearrange(shp, a=4)
            cx.dma(q, t.ap, src, self.dsem_c, writes=[t])
        cx.dma("sp", self.MM.ap, self.d_mm.rearrange("p (a b) -> p a b", a=4).bitcast(F32R), self.dsem_cs, writes=[self.MM])
        cx.dma("sp", self.CK.ap, self.d_ck.rearrange("p (a b) -> p a b", a=2).bitcast(F32R), self.dsem_cs, writes=[self.CK])
        cx.dma("sp", self.CV.ap, self.d_cv.rearrange("p (a b c) -> p a b c", a=2, b=2).bitcast(F32R), self.dsem_cs, writes=[self.CV])
        cx.dma("sp", self.S[0].ap, self.d_s0f.rearrange("p (a b) -> p a b", a=2).bitcast(F32R), self.dsem_cs, writes=[self.S[0]])
        cx.dma("sp", self.S[1].ap, self.d_s0b.rearrange("p (a b) -> p a b", a=2).bitcast(F32R), self.dsem_cs, writes=[self.S[1]])
        cx.dma("sp", self.WA.ap, self.d_wa.bitcast(F32R), self.dsem_cs, writes=[self.WA])
        cx.dma("sp", self.BA.ap, self.d_ba.bitcast(F32R), self.dsem_cs, writes=[self.BA])
        cx.group_fix(self.dsem_c, [self.npre, self.npost, self.ident, self.bm, self.tri, self.FL, self.sinkT, self.gg])
        cx.group_fix(self.dsem_cs, [self.MM, self.CK, self.CV, self.S[0], self.S[1], self.WA, self.BA])
        cx.memset("dve", self.FIN[0].v(), 0.0)
        cx.copy("dve", self.S[6].v(), self.FIN[0][:, 0, :, :])
        cx.act(self.esink.v(), self.sinkT.v(), AF.Exp)
        cx.memset("dve", self.tp[0][:, 0:128], 1.0)
        cx.copy("dve", self.onesF.v(), self.tp[0][:, 0:128])
        cx.memset("dve", self.epsb.v(), EPS)

    def mod_step(self, n=1):
        for _ in range(n):
            if not self.mod_pending:
                return
            l, nb = self.mod_pending.pop(0)
            self.mod_block(l, nb)
            if nb % 12 == 11:
                self.mod_tab(l, nb // 12)

    def mod_step_ffn(self):
        if self.mod_pending and (self.mod_pending[0][0] == 0 or self.ffn_feed_l1):
            self.mod_step()

    def mod_need(self, l, i):
        while self.mod_pending and (self.mod_pending[0][0] < l or
                                    (self.mod_pending[0][0] == l and self.mod_pending[0][1] < 12 * (i + 1))):
            self.mod_step()

    def mod_flush(self):
        while self.mod_pending:
            self.mod_step()

    def mod_block(self, l, nb):
        cx = self.cx
        Wt = self.W.get(self.d_wmod[l].rearrange("(kc p) n -> p kc n", p=128)[:, :, nb * 256:(nb + 1) * 256], [128, KC, 256])
        ps = self.psum()
        for kc in range(KC):
            cx.mm(ps[0:2, 0:256], self.scT[:, kc, :], Wt[:, kc, :], start=(kc == 0), stop=(kc == KC - 1), inc=(kc == KC - 1))
        mr = self.modrow[nb % 2]
        cx.copy("act", mr.v(), ps[0:2, 0:256])
        self.mod_finish()
        self.mod_unfinished = (l, nb)
        if nb % 12 == 11:
            self.mod_finish()

    def mod_finish(self):
        cx = self.cx
        if self.mod_unfinished is None:
            return
        l, nb = self.mod_unfinished
        self.mod_unfinished = None
        mr = self.modrow[nb % 2]
        pt = self.psum()
        for hf in range(2):
            cx.op("pe", lambda e, hf=hf, pt=pt: e.transpose(out=pt.ap[:, hf * 2:hf * 2 + 2],
                                                        in_=mr.ap[0:2, hf * 128:(hf + 1) * 128],
                                                        identity=self.ident.ap[0:2, 0:2]),
                  reads=[mr, self.ident], writes=[pt], inc=(hf == 1))
        for r in range(2):
            cx.tt("dve", self.modT[:, l, nb * 2:nb * 2 + 2, r], pt[:, 0:4].re("p (a b) -> p a b", a=2)[:, :, r],
                  self.bm[:, l, nb * 2:nb * 2 + 2], ALU.add)

    def mod_tab(self, l, i):
        cx = self.cx
        if True:
            coef = 1.0 if i == 1 else 0.5
            for r in range(2):
                sh = self.modT[:, l, i * 24 + 0:i * 24 + 8, r]
                sc = self.modT[:, l, i * 24 + 8:i * 24 + 16, r]
                gt = self.modT[:, l, i * 24 + 16:i * 24 + 24, r]
                cx.stt("dve", self.tab[:, l, i, r, 0, :], sc, 1.0, self.npre[:, l, i, :], ALU.add, ALU.mult)
                cx.copy("dve", self.tab[:, l, i, r, 1, :], sh)
                cx.stt("dve", self.tab[:, l, i, r, 2, :], gt, coef, self.npost[:, l, i, :], ALU.mult, ALU.mult)

    def load_x(self, d_x):
        cx = self.cx
        for k in range(KC):
            cx.dma("pool", self.X[k].ap, d_x[:, k, :], self.dsem_x, writes=[self.X[k]])
        cx.group_fix(self.dsem_x, self.X)

    def boundary(self, out_li, in_li, groups):
        if in_li is not None:
            self.mod_need(*in_li)
        self.run_deferred()
        for gi in range(2):
            self._bg_out(gi, out_li, in_li, groups)
        for gi in range(2):
            self._bg_in(gi, in_li, groups)

    def run_deferred(self):
        pass

    def _bg_out(self, gi, out_li, in_li, groups):
        cx = self.cx
        SQ = self.PG[8:16]
        t0, tn, r = groups[gi]
        if out_li is not None:
            prev = None
            for kc in range(KC):
                tq = self.tmp()
                cx.tt("dve", tq[:, 0:tn], self.H[kc][:, t0:t0 + tn].f(), self.rstd[:, t0:t0 + tn], ALU.mult)
                if prev is not None:
                    pk, ptq = prev
                    cx.tt("dve", self.X[pk][:, t0:t0 + tn], ptq[:, 0:tn], self.X[pk][:, t0:t0 + tn], ALU.add)
                prev = (kc, tq)
            pk, ptq = prev
            cx.tt("dve", self.X[pk][:, t0:t0 + tn], ptq[:, 0:tn], self.X[pk][:, t0:t0 + tn], ALU.add)
        if in_li is not None:
            for kc in range(KC):
                cx.act(SQ[kc][:, t0:t0 + tn], self.X[kc][:, t0:t0 + tn], AF.Square)

    def _bg_in(self, gi, in_li, groups):
        cx = self.cx
        if in_li is None:
            return
        SQ = self.PG[8:16]
        t0, tn, r = groups[gi]
        li, ii = in_li
        pss = self.psum_stat(gi)
        for kc in range(KC):
            cx.mm(pss[:, 0:tn], self.onesF.v(), SQ[kc][:, t0:t0 + tn], start=(kc == 0), stop=(kc == KC - 1))
        self.rsqrt(self.rstd[:, t0:t0 + tn], pss[:, 0:tn], 1.0 / D, tn)
        for kc in range(KC):
            tq = self.tmp()
            cx.tt("dve", tq[:, 0:tn], self.X[kc][:, t0:t0 + tn], self.rstd[:, t0:t0 + tn], ALU.mult)
            cx.act(self.H[kc][:, t0:t0 + tn], tq[:, 0:tn], AF.Identity, bias=self.tab[:, li, ii, r, 1, kc:kc + 1],
                   scale=self.tab[:, li, ii, r, 0, kc:kc + 1])

    def rsqrt(self, out, src, scale, tn):
        cx = self.cx
        tq = self.tmp()
        cx.act(tq[:, 0:tn], src, AF.Ln, bias=self.epsb[:, 0:1], scale=scale)
        cx.act(out, tq[:, 0:tn], AF.Exp, scale=-0.5)

    def ffn(self, l, s, groups):
        cx = self.cx
        self.cur_out = (l, 0 if s == 0 else 2)
        self.cur_groups = groups
        self.mod_need(l, 0 if s == 0 else 2)
        wg = self.d_wg[l, s].rearrange("(kc p) n -> p kc n", p=128)
        wu = self.d_wu[l, s].rearrange("(kc p) n -> p kc n", p=128)
        wd = self.d_wd[l, s].rearrange("(fc p) n -> p fc n", p=128)
        self.ps_rr_n = 8
        self.W.hold = 2
        for blk in range(11):
            Wg = self.W.get(wg[:, :, blk * 256:(blk + 1) * 256], [128, KC, 256])
            Wu = self.W.get(wu[:, :, blk * 256:(blk + 1) * 256], [128, KC, 256])
            for jj in range(2):
                fj = blk * 2 + jj
                for (t0, tn, _) in groups:
                    pg = self.psum()
                    pu = self.psum()
                    for kc in range(KC):
                        cx.mm(pg[:, 0:tn], Wg[:, kc, jj * 128:(jj + 1) * 128], self.H[kc][:, t0:t0 + tn],
                              start=(kc == 0), stop=(kc == KC - 1), inc=(kc == KC - 1))
                    for kc in range(KC):
                        cx.mm(pu[:, 0:tn], Wu[:, kc, jj * 128:(jj + 1) * 128], self.H[kc][:, t0:t0 + tn],
                              start=(kc == 0), stop=(kc == KC - 1), inc=(kc == KC - 1))
                    sl = self.tmp()
                    cx.act(sl[:, 0:tn], pg[:, 0:tn], AF.Silu)
                    cx.tt("dve", self.ACT[fj][:, t0:t0 + tn], sl[:, 0:tn], pu[:, 0:tn], ALU.mult)
            self.mod_step_ffn()
        self.ps_rr_n = 6
        Y = self.H
        for dc in range(KC):
            Wd0 = self.W.get(wd[:, 0:11, dc * 128:(dc + 1) * 128], [128, 11, 128])
            Wd1 = self.W.get(wd[:, 11:22, dc * 128:(dc + 1) * 128], [128, 11, 128])
            for gi, (t0, tn, _) in enumerate(groups):
                py = self.psum()
                for fc in range(FC):
                    wv_ = Wd0[:, fc, :] if fc < 11 else Wd1[:, fc - 11, :]
                    cx.mm(py[:, 0:tn], wv_, self.ACT[fc][:, t0:t0 + tn], start=(fc == 0), stop=(fc == FC - 1),
                          inc=(fc == FC - 1))
                self.y_evac(dc, t0, tn, gi, py[:, 0:tn], dc == 0, dc == KC - 1)
            self.mod_step_ffn()
        self.W.hold = 1
        self.y_finish(groups)


    def fm_linear(self, wv, col0, nchunks, src, tok_ranges, evac):
        cx = self.cx
        nk = len(src)
        j = 0
        while j < nchunks:
            nb = min(2, nchunks - j)
            Wt = self.W.get(wv[:, :, col0 + j * 128: col0 + (j + nb) * 128], [128, nk, nb * 128])
            for jj in range(nb):
                for (t0, tn) in tok_ranges:
                    ps = self.psum()
                    for kc in range(nk):
                        cx.mm(ps[:, 0:tn], Wt[:, kc, jj * 128:(jj + 1) * 128], src[kc][:, t0:t0 + tn],
                              start=(kc == 0), stop=(kc == nk - 1), inc=(kc == nk - 1))
                    evac(j + jj, t0, tn, ps[:, 0:tn])
            j += nb
            self.mod_step(1)

    def tm_linear(self, wv, col0, ncols, src, tiles, evac):
        cx = self.cx
        nk = len(src)
        c = 0
        while c < ncols:
            bw = min(256, ncols - c)
            Wt = self.W.get(wv[:, :, col0 + c: col0 + c + bw], [128, nk, bw])
            for tt in tiles:
                if tt >= 4:
                    self.run_deferred()
                ps = self.psum()
                for kc in range(nk):
                    cx.mm(ps[:, 0:bw], src[kc][:, tt * 128:(tt + 1) * 128], Wt[:, kc, :],
                          start=(kc == 0), stop=(kc == nk - 1), inc=(kc == nk - 1))
                evac(c, bw, tt, ps[:, 0:bw])
            c += bw
            self.mod_step(1)

    def ev_copy(self, dst, src):
        self.n_ev += 1
        self.cx.copy("act" if self.n_ev % 2 else "dve", dst, src)

    def y_evac(self, dc, t0, tn, gi, ps, first, last):
        cx = self.cx
        Y = self.H
        lo, io = self.cur_out
        r = self.cur_groups[gi][2]
        cx.act(Y[dc][:, t0:t0 + tn], ps, AF.Copy, scale=self.tab[:, lo, io, r, 2, dc:dc + 1])
        sq = self.tmpr()
        cx.act(sq[:, 0:tn], ps, AF.Square)
        self.y_pending.append((gi, sq, tn, first, last))
        while len(self.y_pending) > 3:
            self.y_flush1()

    def y_flush1(self):
        gi, sq, tn, first, last = self.y_pending.pop(0)
        self.cx.mm(self.psum_stat(gi)[:, 0:tn], self.onesF.v(), sq[:, 0:tn], start=first, stop=last)

    def y_finish(self, groups):
        cx = self.cx
        while self.y_pending:
            self.y_flush1()
        for gi, (t0, tn, _) in enumerate(groups):
            self.rsqrt(self.rstd[:, t0:t0 + tn], self.psum_stat(gi)[:, 0:tn], 1.0 / D, tn)

    def out_linear(self, wdram, src, l):
        wv = wdram.rearrange("(kc p) n -> p kc n", p=128)
        gidx = {512: 0, 256: 1}
        self.cur_out = (l, 1)
        self.cur_groups = GROUPS_MAIN
        self.mod_need(l, 1)

        def ev(j, t0, tn, ps):
            self.y_evac(j, t0, tn, gidx[tn], ps, j == 0, j == KC - 1)
        self.fm_linear(wv, 0, KC, src, [(0, 512), (512, 256)], ev)
        self.y_finish(GROUPS_MAIN)

    def bcast(self, v, n):
        a = v.ap
        return V(v.tiles, bass.AP(a.tensor, a.offset, [list(a.ap[0]), [0, n], list(a.ap[1])]))

    def gla_ld(self, LR, tt, ldpage):
        cx = self.cx
        for dr in range(2):
            ps = self.psum()
            rows = slice(32 * dr, 32 * dr + 16)
            r1 = slice(32 * dr, 32 * dr + 1)
            cx.mm(ps[:, 0:256], LR[rows, tt * 128:(tt + 1) * 128], self.WA[rows, :], True, False, inc=False)
            cx.mm(ps[:, 0:256], self.onesF[r1, 0:128], self.BA[r1, :], False, True)
            e = self.tmp()
            cx.act(e[:, 0:256], ps[:, 0:256], AF.Exp, scale=-1.0)
            cx.act(ldpage[:, dr * 256:(dr + 1) * 256], e[:, 0:256], AF.Ln, bias=1.0)

    def gla_prep(self, dr, kvp, ld, gp, slot, need_q, QB=None, KB=None, tok0=0, flag=None):
        cx = self.cx
        mc = 1 if dr == 0 else 3
        mb = 0 if dr == 0 else 2
        ldv = ld[:, dr * 256:(dr + 1) * 256]
        ps = self.psum()
        cx.mm(ps[:, 0:256], self.MM[:, mc, :], ldv, True, True)
        ec = self.tmp()
        cx.act(ec[:, 0:256], ps[:, 0:256], AF.Exp, scale=-1.0 / 16)
        if flag is not None:
            cx.stt("dve", gp[:, 0:256], kvp[:, 0:256].f(), flag, ec[:, 0:256], ALU.mult, ALU.mult)
        else:
            cx.tt("dve", gp[:, 0:256], kvp[:, 0:256].f(), ec[:, 0:256], ALU.mult)
        ps2 = self.psum()
        if need_q:
            for pc in range(2):
                cx.mm(ps2[:, pc * 128:(pc + 1) * 128], ldv[:, pc * 128:(pc + 1) * 128], self.MM[:, mb, :], True, True)
            eb = self.tmp()
            cx.act(eb[:, 0:256], ps2[:, 0:256], AF.Exp, scale=-1.0 / 16, bias=float(np.log(0.125)))
            for pc in range(2):
                cx.tt("dve", gp[:, 256 + pc * 128:256 + (pc + 1) * 128], QB[pc][:, tok0:tok0 + 128].f(),
                      eb[:, pc * 128:(pc + 1) * 128], ALU.mult)
            enb = self.tmp()
            cx.act(enb[:, 0:256], ps2[:, 0:256], AF.Exp, scale=1.0 / 16)
            for pc in range(2):
                cx.tt("dve", gp[:, 512 + pc * 128:512 + (pc + 1) * 128], KB[pc][:, tok0:tok0 + 128].f(),
                      enb[:, pc * 128:(pc + 1) * 128], ALU.mult)
            col = 127 if dr == 0 else 0
            cx.act(self.DEC[:, slot, :], ps2[:, 0:256].re("p (a b) -> p a b", a=2)[:, :, col], AF.Exp, scale=-1.0 / 16)
        else:
            for pc in range(2):
                cx.mm(ps2[:, pc * 2:pc * 2 + 2], ldv[:, pc * 128:(pc + 1) * 128], self.onesF[:, 0:2], True, True)
            cx.act(self.DEC[:, slot, :], ps2[:, 0:4].re("p (a b) -> p a b", a=2)[:, :, 0], AF.Exp, scale=-1.0 / 16)

    def gla_update(self, Sprev, Snew, gp, kvp, dec, flag=None, out_f32=None):
        cx = self.cx
        U = self.psum()
        for pc in range(2):
            for e in range(2):
                h = 2 * pc + e
                cx.mm(U[:, h * 128:(h + 1) * 128], gp[:, pc * 128:(pc + 1) * 128], kvp[:, 256 + h * 128:256 + (h + 1) * 128], True, True)
        for pc in range(2):
            for e in range(2):
                h = 2 * pc + e
                rows = slice(64 * e, 64 * e + 64)
                dst = Snew[rows, pc, :] if out_f32 is None else out_f32[rows, pc, :]
                cx.stt("dve", dst, Sprev[rows, pc, :].f(), dec[rows, pc:pc + 1], U[rows, h * 128:(h + 1) * 128], ALU.mult, ALU.add)

    def gla_out_a(self, gpf, gpb):
        cx = self.cx
        ats = []
        for dr, gp in ((0, gpf), (1, gpb)):
            mb = 0 if dr == 0 else 2
            Ae = [self.psum(), self.psum()]
            for h in range(4):
                pc, e = h // 2, h % 2
                rows = slice(64 * e, 64 * e + 64)
                cx.mm(Ae[e][:, pc * 128:(pc + 1) * 128], gp[rows, 512 + pc * 128:512 + (pc + 1) * 128],
                      gp[rows, 256 + pc * 128:256 + (pc + 1) * 128], True, True)
            at = self.tmpr()
            atv = at[:, 0:512].re("p (a e b) -> p a e b", a=2, e=2)
            for e in range(2):
                cx.tt("dve", atv[:, :, e, :], Ae[e][:, 0:256].re("p (a b) -> p a b", a=2),
                      self.bcast(self.MM[:, mb, :].f(), 2), ALU.mult)
            ats.append(at)
        return ats

    def gla_out(self, kvp, gpf, gpb, Sf, Sb, dst_pages, tok0, ats):
        cx = self.cx
        PI = self.psum()
        for h in range(4):
            o = PI[:, h * 128:(h + 1) * 128]
            vb = kvp[:, 256 + h * 128:256 + (h + 1) * 128]
            cx.mm(o, vb, ats[0][:, h * 128:(h + 1) * 128], True, False, inc=False)
            cx.mm(o, vb, ats[1][:, h * 128:(h + 1) * 128], False, True)
        PX = [self.psum(), self.psum()]
        for h in range(4):
            pc, e = h // 2, h % 2
            rows = slice(64 * e, 64 * e + 64)
            o = PX[e][:, pc * 128:(pc + 1) * 128]
            cx.mm(o, Sf[rows, pc, :], gpf[rows, 256 + pc * 128:256 + (pc + 1) * 128], True, False, inc=False)
            cx.mm(o, Sb[rows, pc, :], gpb[rows, 256 + pc * 128:256 + (pc + 1) * 128], False, True)
        for h in range(4):
            pc, e = h // 2, h % 2
            tc_ = self.tmp()
            cx.copy("act", tc_[:, 0:128], PI[:, h * 128:(h + 1) * 128])
            cx.tt("dve", dst_pages[h][:, tok0:tok0 + 128], tc_[:, 0:128], PX[e][:, pc * 128:(pc + 1) * 128], ALU.add)

    def load_rope(self, pc_page, ps_page):
        cx = self.cx
        cx.dma("sp", pc_page[:, 0:512].ap, self.d_ropec.bitcast(F32R), self.dsem_c2, writes=[pc_page])
        cx.dma("sp", ps_page[:, 0:512].ap, self.d_ropes.bitcast(F32R), self.dsem_c2, writes=[ps_page])
        cx.group_fix(self.dsem_c2, [pc_page, ps_page])

    def prepass(self):
        cx = self.cx
        PG = self.PG
        self.load_x(self.d_xo)
        self.boundary(None, (0, 0), GROUPS_OTH)
        self.ffn(0, 0, GROUPS_OTH)
        self.boundary((0, 0), (0, 1), GROUPS_OTH)
        if self.cut <= 1:
            return
        wfm = self.d_winfm.rearrange("(kc p) n -> p kc n", p=128)
        wtm = self.d_wintm.rearrange("(kc p) n -> p kc n", p=128)
        LR, KDh, KSh, RC, RS = PG[8], PG[9], PG[10], PG[14], PG[15]
        KVp = PG[16:22]
        LD = PG[22:28]
        GP = [PG[11], PG[12], PG[13], PG[28]]
        self.load_rope(RC, RS)
        allr = [(0, 512), (512, 256)]
        halo = [(0, 128), (640, 128)]
        self.fm_linear(wfm, 10 * 128, 1, self.H, allr, lambda j, t0, tn, ps: self.ev_copy(LR[:, t0:t0 + tn], ps))

        def ev_h(dst):
            def f(j, t0, tn, ps):
                hs = 0 if t0 == 0 else 1
                self.ev_copy(dst[:, j * 256 + hs * 128: j * 256 + (hs + 1) * 128], ps)
            return f
        self.fm_linear(wfm, 4 * 128, 2, self.H, halo, ev_h(KDh))
        self.fm_linear(wfm, 19 * 128, 2, self.H, halo, ev_h(KSh))
        if self.cut <= 2:
            return
        for kv in range(2):
            t1 = self.tmp()
            t2 = self.tmp()
            cx.tt("dve", t1[:, 0:256], KDh[:, kv * 256:(kv + 1) * 256].f(), RC[:, 256:512].f(), ALU.mult)
            cx.tt("dve", t2[:, 0:256], KSh[:, kv * 256:(kv + 1) * 256].f(), RS[:, 256:512].f(), ALU.mult)
            cx.tt("dve", self.HK[:, kv, :, :], t1[:, 0:256].re("p (a b) -> p a b", a=2), t2[:, 0:256].re("p (a b) -> p a b", a=2), ALU.add)
        self.tm_linear(wtm, 0, 256, self.H, [0, 5],
                       lambda c, bw, tt, ps: self.ev_copy(self.HV[:, 0 if tt == 0 else 1, :], ps))
        if self.cut <= 3:
            return
        self.tm_linear(wtm, 512, 768, self.H, list(range(6)),
                       lambda c, bw, tt, ps: self.ev_copy(KVp[tt][:, c:c + bw], ps))
        for tt in range(6):
            self.gla_ld(LR, tt, LD[tt])
        if self.cut <= 4:
            return
        GPs = [PG[11], PG[12], PG[13], PG[28]]

        def kslot(n):
            return GPs[n // 3][:, (n % 3) * 256:(n % 3 + 1) * 256]
        jobs = []
        for dr in range(2):
            order = [0, 1, 2, 3, 4, 5] if dr == 0 else [5, 4, 3, 2, 1, 0]
            for n, tt in enumerate(order):
                jobs.append((dr, n, tt))
        for (dr, n, tt) in jobs:
            sl = dr * 6 + n
            flag = self.FL[:, dr * 3 + tt // 2: dr * 3 + tt // 2 + 1]
            self.gla_prep(dr, KVp[tt], LD[tt], kslot(sl), sl, False, flag=flag)
            cx.ts("dve", self.DM[:, sl, :], self.DEC[:, sl, :], -1.0, ALU.add, flag, ALU.mult)
            cx.ts("dve", self.DM[:, sl, :], self.DM[:, sl, :], 1.0, ALU.add)
        for dr in range(2):
            Sprev = self.S[dr]
            pp = [self.S[4], self.S[5]]
            for (d2, n, tt) in jobs:
                if d2 != dr:
                    continue
                sl = dr * 6 + n
                Snew = self.S[2 + dr] if n == 5 else pp[n % 2]
                self.gla_update(Sprev, Snew, kslot(sl), KVp[tt], self.DM[:, sl, :])
                Sprev = Snew

    def even_mixer(self):
        cx = self.cx
        PG = self.PG
        wfm = self.d_winfm.rearrange("(kc p) n -> p kc n", p=128)
        wtm = self.d_wintm.rearrange("(kc p) n -> p kc n", p=128)
        CAT = PG[8:16]
        QA, KD, SW, VT, PT = PG[16:20], PG[20:22], PG[22:24], PG[24:26], PG[26:29]
        RC, RS = PG[14], PG[15]
        allr = [(0, 512), (512, 256)]
        samp = [(512, 256)]
        self.load_rope(RC, RS)
        self.fm_linear(wfm, 0, 4, self.H, allr, lambda j, t0, tn, ps: self.ev_copy(QA[j][:, t0:t0 + tn], ps))
        self.fm_linear(wfm, 4 * 128, 2, self.H, allr, lambda j, t0, tn, ps: self.ev_copy(KD[j][:, t0:t0 + tn], ps))

        def swv(i):
            return SW[i // 3][:, (i % 3) * 256:(i % 3 + 1) * 256]
        self.fm_linear(wfm, 15 * 128, 4, self.H, samp, lambda j, t0, tn, ps: self.ev_copy(swv(j), ps))
        self.fm_linear(wfm, 19 * 128, 2, self.H, samp, lambda j, t0, tn, ps: self.ev_copy(swv(4 + j), ps))
        if self.cut <= 10:
            return
        for i in range(6):
            dst = (QA[i] if i < 4 else KD[i - 4])[:, 512:768]
            t1 = self.tmp()
            t2 = self.tmp()
            cx.tt("dve", t1[:, 0:256], dst.f(), RC[:, 0:256].f(), ALU.mult)
            cx.tt("dve", t2[:, 0:256], swv(i).f(), RS[:, 0:256].f(), ALU.mult)
            cx.tt("dve", dst, t1[:, 0:256], t2[:, 0:256], ALU.add)

        def vtv(tt):
            return VT[tt // 3][:, (tt % 3) * 256:(tt % 3 + 1) * 256]

        def ev_tm_v(c, bw, tt, ps):
            self.ev_copy(vtv(tt), ps)
            if tt < 4:
                for kv in range(2):
                    cx.dma("sp", self.o_kv[:, tt, 128 + kv * 64:128 + (kv + 1) * 64], vtv(tt)[:, kv * 128:kv * 128 + 64].f().ap,
                           self.dsem_out, reads=[vtv(tt)])

        def ev_tm_k(c, bw, tt, ps):
            tk = self.tmp()
            self.ev_copy(tk[:, 0:128], ps[:, 0:128])
            cx.dma("sp", self.o_kv[:, tt, 0:128], tk[:, 0:128].ap, self.dsem_out, reads=[tk])
        self.tm_linear(wtm, 0, 256, self.H, list(range(6)), ev_tm_v)
        self.tm_linear(wtm, 256, 256, self.H, list(range(4)), ev_tm_k)
        if self.cut <= 12:
            return

        self.ps_rr_n = 4
        ring = [PG[26], PG[27], PG[28], PG[29], PG[12], PG[13], PG[22], PG[23]]
        rn = [0]

        def ptile():
            t = ring[rn[0] % len(ring)]
            rn[0] += 1
            return t

        def finalize(c, c0, po, den):
            for e in range(2):
                rows = slice(64 * e, 64 * e + 64)
                t1 = self.tmp()
                cx.act(t1[rows, 0:256], den[e][rows, 0:256], AF.Ln, bias=self.esink[rows, c:c + 1])
                cx.act(t1[rows, 0:256], t1[rows, 0:256], AF.Exp, scale=-1.0)
                cx.tt("dve", CAT[c][rows, c0:c0 + 256], po[e][rows, 0:256], t1[rows, 0:256], ALU.mult)

        units = [(sq, c) for sq in range(2) for c in range(4)]

        def scores(u):
            sq, c = units[u]
            c0, kv = sq * 256, c // 2
            Pk = [ptile(), ptile()]
            for kt in range(2):
                for e in range(2):
                    rows = slice(64 * e, 64 * e + 64)
                    Sp = self.psum()
                    cx.mm(Sp[:, 0:256], KD[kv][rows, c0 + kt * 128:c0 + (kt + 1) * 128], QA[c][rows, c0:c0 + 256], True, True)
                    cx.act(Pk[kt][:, e * 256:(e + 1) * 256], Sp[:, 0:256], AF.Exp, scale=0.125)
            return Pk

        def pv(u, Pk):
            sq, c = units[u]
            c0, kv = sq * 256, c // 2
            bank = self.ps[4 + 2 * (u % 2)]
            bank2 = self.ps[5 + 2 * (u % 2)]
            po = [bank[:, 0:256], bank[:, 256:512]]
            den = [bank2[:, 0:256], bank2[:, 256:512]]
            for e in range(2):
                for kt in range(2):
                    vt = c0 // 128 + kt
                    cx.mm(po[e], vtv(vt)[:, kv * 128:(kv + 1) * 128], Pk[kt][:, e * 256:(e + 1) * 256], kt == 0, kt == 1)
            for e in range(2):
                for kt in range(2):
                    cx.mm(den[e], self.onesF.v(), Pk[kt][:, e * 256:(e + 1) * 256], kt == 0, kt == 1)
            finalize(c, c0, po, den)

        pend = scores(0)
        for u in range(len(units)):
            nxt = scores(u + 1) if u + 1 < len(units) else None
            pv(u, pend)
            pend = nxt

        c0 = 512
        for c in range(4):
            kv = c // 2
            po = [self.ps[4][:, 0:256], self.ps[5][:, 0:256]]
            den = [self.ps[6][:, 0:256], self.ps[7][:, 0:256]]
            Ps = []
            for kt in range(6):
                P = ptile()
                Ps.append(P)
                q0, qn = (0, 128) if kt == 2 else ((128, 128) if kt == 5 else (0, 256))
                for e in range(2):
                    rows = slice(64 * e, 64 * e + 64)
                    if kt < 2:
                        kl = self.CK[rows, kv, kt * 128:(kt + 1) * 128]
                    elif kt == 2:
                        kl = self.HK[rows, kv, 1, :]
                    elif kt == 5:
                        kl = self.HK[rows, kv, 0, :]
                    else:
                        kl = KD[kv][rows, c0 + (kt - 3) * 128:c0 + (kt - 2) * 128]
                    Sp = self.psum()
                    cx.mm(Sp[:, 0:qn], kl, QA[c][rows, c0 + q0:c0 + q0 + qn], True, True)
                    cx.act(P[:, e * 256 + q0:e * 256 + q0 + qn], Sp[:, 0:qn], AF.Exp, scale=0.125)
                Pv = P[:, 0:512].re("p (a b) -> p a b", a=2)
                lo, hi = Pv[:, :, 0:128], Pv[:, :, 128:256]
                if kt == 2:
                    cx.tt("pool", lo, lo.f(), self.bcast(self.tri[:, 2, :], 2), ALU.mult)
                elif kt == 3:
                    cx.tt("pool", hi, hi.f(), self.bcast(self.tri[:, 0, :], 2), ALU.mult)
                elif kt == 4:
                    cx.tt("pool", lo, lo.f(), self.bcast(self.tri[:, 1, :], 2), ALU.mult)
                elif kt == 5:
                    cx.tt("pool", hi, hi.f(), self.bcast(self.tri[:, 3, :], 2), ALU.mult)
            for kt in range(6):
                P = Ps[kt]
                q0, qn = (0, 128) if kt == 2 else ((128, 128) if kt == 5 else (0, 256))
                if kt < 2:
                    vl = self.CV[:, kt, kv, :]
                elif kt == 2:
                    vl = self.HV[:, 1, kv * 128:(kv + 1) * 128]
                elif kt == 5:
                    vl = self.HV[:, 0, kv * 128:(kv + 1) * 128]
                else:
                    vl = vtv(4 + kt - 3)[:, kv * 128:(kv + 1) * 128]
                for e in range(2):
                    pe_ = P[:, e * 256 + q0:e * 256 + q0 + qn]
                    cx.mm(V(po[e].tiles, po[e].ap[:, q0:q0 + qn]), vl, pe_, kt == 0, kt == 5)
                    cx.mm(V(den[e].tiles, den[e].ap[:, q0:q0 + qn]), self.onesF.v(), pe_, kt == 0, kt == 5)
            finalize(c, c0, po, den)
        self.ps_rr_n = 6

        if self.cut <= 14:
            return
        QB, KB, LR = PG[16:18], PG[18:20], PG[20]
        KVp, LD, GP = PG[21:23], PG[23:25], PG[25:29]
        self.fm_linear(wfm, 6 * 128, 2, self.H, allr, lambda j, t0, tn, ps: self.ev_copy(QB[j][:, t0:t0 + tn], ps))
        self.fm_linear(wfm, 8 * 128, 2, self.H, allr, lambda j, t0, tn, ps: self.ev_copy(KB[j][:, t0:t0 + tn], ps))
        self.fm_linear(wfm, 10 * 128, 1, self.H, allr, lambda j, t0, tn, ps: self.ev_copy(LR[:, t0:t0 + tn], ps))
        ZS = self.S[6]
        if self.cut <= 15:
            return
        for sg in range(3):
            tiles = [2 * sg, 2 * sg + 1]
            self.tm_linear(wtm, 512, 768, self.H, tiles, lambda c, bw, tt, ps: self.ev_copy(KVp[tt % 2][:, c:c + bw], ps))
            for ci in range(2):
                self.gla_ld(LR, tiles[ci], LD[ci])
            if self.cut == 151:
                return
            for (ci, dr) in [(0, 0), (1, 0), (1, 1), (0, 1)]:
                self.gla_prep(dr, KVp[ci], LD[ci], GP[ci * 2 + dr], ci * 2 + dr, True, QB, KB, tiles[ci] * 128)
            if self.cut == 152:
                return
            Sf0 = ZS if sg < 2 else self.S[2]
            Sb0 = ZS if sg < 2 else self.S[3]
            Sf1, Sb1 = self.S[4], self.S[5]
            self.gla_update(Sf0, Sf1, GP[0], KVp[0], self.DEC[:, 0, :])
            self.gla_update(Sb0, Sb1, GP[3], KVp[1], self.DEC[:, 3, :])
            if self.cut == 153:
                return
            ats0 = self.gla_out_a(GP[0], GP[1])
            ats1 = self.gla_out_a(GP[2], GP[3])
            self.gla_out(KVp[0], GP[0], GP[1], Sf0, Sb1, CAT[4:8], tiles[0] * 128, ats0)
            self.gla_out(KVp[1], GP[2], GP[3], Sf1, Sb0, CAT[4:8], tiles[1] * 128, ats1)
            if sg < 2:
                self.gla_update(Sf1, None, GP[2], KVp[1], self.DEC[:, 2, :], out_f32=self.FIN[0][:, sg, :, :])
                self.gla_update(Sb1, None, GP[1], KVp[0], self.DEC[:, 1, :], out_f32=self.FIN[1][:, sg, :, :])
            if self.cut == 155:
                return
        cx.dma("sp", self.o_sf, self.FIN[0].ap.rearrange("p a b c -> p a (b c)"), self.dsem_out, reads=[self.FIN[0]])
        cx.dma("sp", self.o_sb, self.FIN[1].ap.rearrange("p a b c -> p a (b c)"), self.dsem_out, reads=[self.FIN[1]])
        if self.cut <= 16:
            return

        RH = PG[16:20]
        for h in range(4):
            sqp = PG[20 + h]
            cx.act(sqp.v(), CAT[4 + h].v().f(), AF.Square)
            for (t0, tn) in allr:
                ss = self.psum()
                cx.mm(ss[:, 0:tn], self.onesF.v(), sqp[:, t0:t0 + tn], True, True)
                self.rsqrt(RH[h][:, t0:t0 + tn], ss[:, 0:tn], 1.0 / 128, tn)
                oh = CAT[4 + h][:, t0:t0 + tn]
                cx.stt("dve", oh, oh.f(), self.gg[:, h:h + 1], RH[h][:, t0:t0 + tn].f(), ALU.mult, ALU.mult)

        def ev_gate(j, t0, tn, ps):
            o = CAT[4 + j][:, t0:t0 + tn]
            sgt = self.tmp()
            cx.act(sgt[:, 0:tn], ps, AF.Silu)
            cx.tt("dve", o, o.f(), sgt[:, 0:tn], ALU.mult)
        self.fm_linear(wfm, 11 * 128, 4, self.H, allr, ev_gate)
        if self.cut <= 17:
            return
        self.out_linear(self.d_wout, CAT, 0)

    def odd_mixer(self):
        cx = self.cx
        PG = self.PG
        wi = self.d_cmwin.rearrange("(kc p) n -> p kc n", p=128)
        U, VTM, GBp, WSB = PG[8:16], PG[16:24], PG[24:27], PG[27:29]
        allr = [(0, 512), (512, 256)]

        def gbv(which, cb):
            li = which * 4 + cb
            return GBp[li // 3][:, (li % 3) * 256:(li % 3 + 1) * 256]
        for which in range(2):
            for cb in range(4):
                v = gbv(which, cb)
                cx.dma("sp", v.ap, self.d_cmgb[:, which * 1024 + cb * 256: which * 1024 + (cb + 1) * 256].bitcast(F32R),
                       self.dsem_c2, writes=[v])
        WS = WSB[0][:, 0:512].re("p (g t) -> p g t", g=4)
        BS = WSB[1][:, 0:512].re("p (g t) -> p g t", g=4)
        cx.dma("sp", WSB[0][:, 0:512].ap, self.d_cmws.bitcast(F32R), self.dsem_c2, writes=[WSB[0]])
        cx.dma("sp", WSB[1][:, 0:512].ap, self.d_cmbs.bitcast(F32R), self.dsem_c2, writes=[WSB[1]])
        cx.group_fix(self.dsem_c2, list(GBp) + list(WSB))

        def vblk(tt, cb):
            li = tt * 4 + cb
            return VTM[li // 3][:, (li % 3) * 256:(li % 3 + 1) * 256]
        self.tm_linear(wi, 1024, 1024, self.H, list(range(6)), lambda c, bw, tt, ps: cx.act(vblk(tt, c // 256), ps, AF.Gelu))
        for tt in range(6):
            st = self.lnst[tt]
            for cb in range(4):
                v = vblk(tt, cb).f()
                cx.op("dve", lambda e, v=v, cb=cb, st=st: e.bn_stats(out=st.ap[:, 16 + cb * 6:16 + (cb + 1) * 6], in_=v.ap),
                      reads=[v], writes=[st])
            cx.op("dve", lambda e, st=st: e.bn_aggr(out=st.ap[:, 10:12], in_=st.ap[:, 16:40]), reads=[st], writes=[st])
        allst = V([k for t in self.lnst for k in _tiles([t])], self.lnst_all)
        cx.act(allst[:, :, 15], allst[:, :, 11], AF.Ln, bias=self.epsb[:, 0:1], scale=1.0)
        cx.act(allst[:, :, 13], allst[:, :, 15], AF.Exp, scale=-0.5)
        cx.stt("dve", allst[:, :, 14], allst[:, :, 10], -1.0, allst[:, :, 13], ALU.mult, ALU.mult)
        prev = None
        for tt in range(6):
            st = self.lnst[tt]
            for cb in range(4):
                v = vblk(tt, cb)
                t1 = self.tmp()
                cx.ts("dve", t1[:, 0:256], v.f(), st[:, 13:14], ALU.mult, st[:, 14:15], ALU.add)
                cx.tt("pool", t1[:, 0:256], t1[:, 0:256], gbv(0, cb).f(), ALU.mult)
                if prev is not None:
                    cx.tt("dve", prev[0], prev[1][:, 0:256], gbv(1, prev[2]).f(), ALU.add)
                prev = (v, t1, cb)
        cx.tt("dve", prev[0], prev[1][:, 0:256], gbv(1, prev[2]).f(), ALU.add)
        self.fm_linear(wi, 0, 8, self.H, allr, lambda j, t0, tn, ps: cx.act(U[j][:, t0:t0 + tn], ps, AF.Gelu))
        for cc in range(8):
            g = cc // 2
            for tiles in ([0, 1, 2, 3], [4, 5]):
                n = len(tiles)
                ps = self.psum()
                for ti, tt in enumerate(tiles):
                    cx.mm(ps[:, ti * 128:(ti + 1) * 128], vblk(tt, cc // 2)[:, (cc % 2) * 128:(cc % 2 + 1) * 128], WS[:, g, :], True, True)
                t1 = self.tmp()
                cx.tt("dve", t1[:, 0:n * 128].re("p (a b) -> p a b", a=n), ps[:, 0:n * 128].re("p (a b) -> p a b", a=n),
                      self.bcast(BS[:, g, :].f(), n), ALU.add)
                uu = U[cc][:, tiles[0] * 128:(tiles[-1] + 1) * 128]
                cx.tt("pool", uu, t1[:, 0:n * 128], uu.f(), ALU.mult)
        self.out_linear(self.d_cmwout, U, 1)

    def store_out(self):
        cx = self.cx
        for k in range(KC):
            cx.dma("sp", self.o_y[:, k, :], self.X[k].ap, self.dsem_out, reads=[self.X[k]])


def build_program(stage):
    nc0 = bass.Bass("TRN2", target_bir_lowering=False)
    nc0.dge_precook = False
    plan = Prog(nc0, True, None, stage).build()
    nc = bass.Bass("TRN2", target_bir_lowering=False)
    nc.dge_precook = False
    p = Prog(nc, False, plan, stage)
    p.build()
    build_program.last_log = p.cx.log
    return nc


def fm(x):
    t, d = x.shape
    return np.ascontiguousarray(x.reshape(t, d // 128, 128).transpose(2, 1, 0))


def fm_vec(v):
    sh = v.shape
    v2 = v.reshape(-1, sh[-1] // 128, 128)
    return np.ascontiguousarray(v2.transpose(2, 0, 1)).reshape(128, -1)


def _swap_idx():
    d = np.arange(64)
    a, bb, f = d // 32, (d % 32) // 16, d % 16
    return a * 32 + (1 - bb) * 16 + f


def _rope_tables(pos):
    f32 = np.float32
    half, nf = 32, 16
    inv = (f32(10000.0) ** (-np.arange(nf, dtype=f32) * f32(2.0) / f32(half))).astype(f32)
    row = (pos // 64).astype(f32)
    col = (pos % 64).astype(f32)
    d = np.arange(64)
    a, bb, f = d // 32, (d % 32) // 16, d % 16
    p = np.where(a[:, None] == 0, row[None, :], col[None, :]).astype(f32)
    ang = (p * inv[f][:, None]).astype(f32)
    cos = np.cos(ang).astype(f32)
    sin = np.sin(ang).astype(f32)
    sgn = np.where(bb == 0, -1.0, 1.0).astype(f32)[:, None]
    sins = (sin * sgn).astype(f32)
    return np.concatenate([cos, cos], 0), np.concatenate([sins, sins], 0)


def make_in_maps(inp):
    f32 = np.float32
    g = lambda k: np.asarray(inp[k], f32)
    x_prompt, x_sample, c, c_ctx = g("x_prompt"), g("x_sample"), g("c"), g("c_ctx")
    w_in = g("ev_w_in")[0]
    sw = _swap_idx()
    qa, ka, va = w_in[:, 0:512], w_in[:, 512:640], w_in[:, 640:768]
    qb, kb, vb, gb = w_in[:, 768:1024], w_in[:, 1024:1280], w_in[:, 1280:1792], w_in[:, 1792:2304]
    lrf, lrb = w_in[:, 2304:2320], w_in[:, 2320:2336]
    ka_dup = np.concatenate([ka[:, 0:64], ka[:, 0:64], ka[:, 64:128], ka[:, 64:128]], 1)
    qa_sw = qa.reshape(D, 8, 64)[:, :, sw].reshape(D, 512)
    ka_sw = ka.reshape(D, 2, 64)[:, :, sw].reshape(D, 128)
    ka_sw_dup = np.concatenate([ka_sw[:, 0:64], ka_sw[:, 0:64], ka_sw[:, 64:128], ka_sw[:, 64:128]], 1)
    lr = np.concatenate([lrf, lrf, lrb, lrb, lrf, lrf, lrb, lrb], 1)
    w_in_fm = np.ascontiguousarray(np.concatenate([qa, ka_dup, qb, kb, lr, gb, qa_sw, ka_sw_dup], 1))
    assert w_in_fm.shape == (D, 2688)
    va_dup = np.concatenate([va[:, 0:64], va[:, 0:64], va[:, 64:128], va[:, 64:128]], 1)
    w_in_tm = np.ascontiguousarray(np.concatenate([va_dup, ka, ka, kb, vb], 1))
    assert w_in_tm.shape == (D, 1280)
    jj, ii = np.meshgrid(np.arange(128), np.arange(128), indexing="ij")
    gmask = np.stack([(jj <= ii), (jj > ii), (jj >= ii), (jj < ii)], 1).astype(f32).reshape(128, 512)
    tri_prev = (jj >= ii).astype(f32)
    tri_next = (jj <= ii).astype(f32)
    wa = np.zeros((64, 256), f32)
    waf, wab = g("gla_wa_f")[0], g("gla_wa_b")[0]
    wa[0:16], wa[16:32], wa[32:48], wa[48:64] = waf, waf, wab, wab
    ba = np.zeros((64, 256), f32)
    ba[0:32] = g("gla_ba_f")[0][None, :]
    ba[32:64] = g("gla_ba_b")[0][None, :]
    sink = g("ev_sink")[0]
    sinkT = np.ascontiguousarray(np.stack([np.where(np.arange(128) < 64, sink[2 * cc], sink[2 * cc + 1]) for cc in range(4)], 1)).astype(f32)
    cm_gb = np.ascontiguousarray(np.broadcast_to(np.concatenate([g("cm_v_gain")[0], g("cm_v_bias")[0]])[None, :], (128, 2048))).astype(f32)
    cm_ws = np.ascontiguousarray(g("cm_w_s")[0].transpose(2, 0, 1).reshape(128, 512))
    cm_bs = np.ascontiguousarray(np.broadcast_to(g("cm_b_s")[0].reshape(1, 512), (128, 512))).astype(f32)
    shared = {
        "w_mod": np.ascontiguousarray(g("w_mod")),
        "b_mod": fm_vec(g("b_mod")),
        "npre": fm_vec(g("norm_pre")),
        "npost": fm_vec(g("norm_post")),
        "ffn_w_gate": np.ascontiguousarray(g("ffn_w_gate")),
        "ffn_w_up": np.ascontiguousarray(g("ffn_w_up")),
        "ffn_w_down": np.ascontiguousarray(g("ffn_w_down")),
        "ident": np.eye(128, dtype=f32),
        "w_in_fm": w_in_fm, "w_in_tm": w_in_tm,
        "ev_w_out": np.ascontiguousarray(g("ev_w_out")[0]),
        "gmask": gmask,
        "sinkT": sinkT,
        "gla_g": fm_vec(g("gla_norm")),
        "gla_wa": wa, "gla_ba": np.ascontiguousarray(ba),
        "cm_w_in": np.ascontiguousarray(g("cm_w_in")[0]),
        "cm_w_out": np.ascontiguousarray(g("cm_w_out")[0]),
        "cm_gb": cm_gb, "cm_ws": cm_ws, "cm_bs": cm_bs,
    }
    cache_k, cache_v = g("cache_k"), g("cache_v")
    sf, sbw = g("state_gla_fwd"), g("state_gla_bwd")
    maps = []
    for i in range(NCORE):
        b, q = i // 4, i % 4
        own = x_sample[b, 256 * q:256 * (q + 1)]
        main = np.concatenate([x_prompt[2 * i], x_prompt[2 * i + 1], own], axis=0)
        oth = np.concatenate([x_sample[b, 256 * ((q + r) % 4):256 * ((q + r) % 4) + 256] for r in (1, 2, 3)], axis=0)
        cond = np.stack([c_ctx, c[b]], axis=0)
        m = dict(shared)
        m["xm"] = fm(main)
        m["xo"] = fm(oth)
        m["cond"] = np.ascontiguousarray(cond.reshape(2, KC, 128).transpose(2, 1, 0))
        pos = np.concatenate([256 * q + np.arange(256), (256 * (q + 1) + np.arange(128)) % 1024,
                              (256 * q - 128 + np.arange(128)) % 1024])
        rc, rs = _rope_tables(pos)
        m["rope_c"], m["rope_s"] = np.ascontiguousarray(rc), np.ascontiguousarray(rs)
        vp, vn = f32(q > 0), f32(q < 3)
        m["tri"] = np.ascontiguousarray(np.stack([tri_prev, tri_next, tri_prev * vp, tri_next * vn], 1).reshape(128, 512))
        fl = np.zeros((128, 6), f32)
        for r in (1, 2, 3):
            fl[:, r - 1] = f32(r >= 4 - q)
            fl[:, 3 + r - 1] = f32(r <= 3 - q)
        m["flags"] = fl
        ck = cache_k[b, 0]
        ckT = ck.transpose(2, 1, 0)
        m["ctx_k"] = np.ascontiguousarray(np.concatenate([ckT, ckT], 0).reshape(128, 512))
        cv = cache_v[b, 0].reshape(2, 128, 2, 64).transpose(1, 0, 2, 3)
        m["ctx_v"] = np.ascontiguousarray(np.concatenate([cv, cv], 3).reshape(128, 512))
        for nm, stt in (("s0f", sf), ("s0b", sbw)):
            s0 = stt[b, 0]
            m[nm] = np.ascontiguousarray(s0.reshape(2, 2, 64, 128).transpose(1, 2, 0, 3).reshape(128, 256))
        maps.append(m)
    return maps


def assemble(results):
    f32 = np.float32
    y_prompt = np.zeros((16, 256, D), f32)
    y_sample = np.zeros((2, 1024, D), f32)
    nk = np.zeros((16, 1, 256, 2, 64), f32)
    nv = np.zeros((16, 1, 256, 2, 64), f32)
    nsf = np.zeros((16, 1, 4, 64, 128), f32)
    nsb = np.zeros((16, 1, 4, 64, 128), f32)
    for i, r in enumerate(results):
        b, q = i // 4, i % 4
        y = r["ym"].transpose(2, 1, 0).reshape(T, D)
        y_prompt[2 * i] = y[0:256]
        y_prompt[2 * i + 1] = y[256:512]
        y_sample[b, 256 * q:256 * (q + 1)] = y[512:768]
        kvo = r["kv_out"].transpose(1, 0, 2).reshape(512, 256)
        for sq in range(2):
            blk = kvo[sq * 256:(sq + 1) * 256]
            nk[2 * i + sq, 0] = blk[:, 0:128].reshape(256, 2, 64)
            nv[2 * i + sq, 0] = blk[:, 128:256].reshape(256, 2, 64)
        for arr, key in ((nsf, "sf_out"), (nsb, "sb_out")):
            o = r[key].reshape(128, 2, 2, 128)
            for sq in range(2):
                arr[2 * i + sq, 0] = o[:, sq].reshape(2, 64, 2, 128).transpose(2, 0, 1, 3).reshape(4, 64, 128)
    return (y_prompt, y_sample, nk, nv, nsf, nsb)


_NC_CACHE = {}


def kernel(**inputs):
    stage = int(os.environ.get("MK_STAGE", "99"))
    if stage not in _NC_CACHE:
        _NC_CACHE[stage] = build_program(stage)
    nc = _NC_CACHE[stage]
    maps = make_in_maps(inputs)
    res = run_bass_kernel_spmd(nc, maps, core_ids=list(range(NCORE)))
    kernel.last = res.results
    if stage < 99:
        return res.results
    return assemble(res.results)
```

```python
import os
import numpy as np
from contextlib import ExitStack
import concourse.bass as bass
import concourse.mybir as mybir
from concourse.bass_utils import run_bass_kernel_spmd

F32 = mybir.dt.float32
F32R = mybir.dt.float32r
AF = mybir.ActivationFunctionType
ALU = mybir.AluOpType
SAME_ENGINE_SYNC = True

D = 1024
KC = 8
T = 768
FF = 2816
FC = 22
EPS = 1e-6
NCORE = 8
GROUPS_MAIN = [(0, 512, 0), (512, 256, 1)]
GROUPS_OTH = [(0, 512, 1), (512, 256, 1)]


class Trk:
    __slots__ = ("writer", "readers")

    def __init__(self):
        self.writer = None
        self.readers = {}


class Tile:
    __slots__ = ("ap", "trks", "split", "name")

    def __init__(self, ap, name="", split=None):
        self.ap = ap
        self.name = name
        self.split = split
        self.trks = [Trk()] if split is None else [Trk(), Trk()]

    def _sel(self, idx):
        if self.split is None:
            return self.trks
        if isinstance(idx, tuple) and len(idx) >= 2 and isinstance(idx[1], slice):
            a, b = idx[1].start, idx[1].stop
            if a is not None and b is not None:
                if b <= self.split:
                    return [self.trks[0]]
                if a >= self.split:
                    return [self.trks[1]]
        return self.trks

    def __getitem__(self, idx):
        return V(self._sel(idx), self.ap[idx])

    def v(self):
        return V(self.trks, self.ap)


class V:
    __slots__ = ("tiles", "ap")

    def __init__(self, tiles, ap):
        self.tiles = tiles
        self.ap = ap

    def __getitem__(self, idx):
        return V(self.tiles, self.ap[idx])

    def r(self):
        return V(self.tiles, self.ap.bitcast(F32R))

    def f(self):
        return V(self.tiles, self.ap.bitcast(F32))

    def re(self, s, **kw):
        return V(self.tiles, self.ap.rearrange(s, **kw))

    def bc(self, shape):
        return V(self.tiles, self.ap.to_broadcast(shape))


def _tiles(vs):
    out = []
    for v in vs:
        if v is None:
            continue
        if isinstance(v, Tile):
            out.extend(v.trks)
        else:
            out.extend(v.tiles)
    return out


class Ctx:
    def __init__(self, nc, dry=False):
        self.nc = nc
        self.dry = dry
        self.engs = {}
        self.sems = {}
        self.count = {}
        self.waited = {}
        self.stack = None
        self.n_dma_sem = 0
        self.n_ps = 0
        self.n_tp = 0
        self.log = {}

    def setup(self, stack):
        self.stack = stack
        nc = self.nc
        self.engs = {"pe": nc.tensor, "act": nc.scalar, "dve": nc.vector,
                     "pool": nc.gpsimd, "sp": nc.sync}
        for k in self.engs:
            self.sems[k] = stack.enter_context(nc.semaphore("s_" + k))
            self.count[k] = 0
            self.waited[k] = {}

    def new_dma_sem(self, name):
        key = "dma_%s_%d" % (name, self.n_dma_sem)
        self.n_dma_sem += 1
        self.sems[key] = self.stack.enter_context(self.nc.semaphore(key))
        self.count[key] = 0
        return key

    def _deps(self, ek, reads, writes):
        deps = {}

        def add(w):
            if w is None:
                return
            k, c = w
            if deps.get(k, 0) < c:
                deps[k] = c
        for t in reads:
            add(t.writer)
        for t in writes:
            add(t.writer)
            for k, c in t.readers.items():
                add((k, c))
        eng = self.engs[ek]
        for k, c in deps.items():
            if k == ek and (ek == "pe" or not SAME_ENGINE_SYNC):
                continue
            if k.startswith("dma_"):
                c = self.count[k]
            if self.waited[ek].get(k, 0) >= c:
                continue
            eng.wait_ge(self.sems[k], c)
            self.waited[ek][k] = c
            self.log.setdefault(ek, []).append(("w", k, c))

    def op(self, ek, fn, reads=(), writes=(), inc=True):
        rt = _tiles(reads)
        wt = _tiles(writes)
        if self.dry:
            return
        self._deps(ek, rt, wt)
        ins = fn(self.engs[ek])
        idx = self.count[ek] + 1
        if inc:
            ins.then_inc(self.sems[ek], 1)
            self.count[ek] = idx
        self.log.setdefault(ek, []).append(("i", ek if inc else None, 1))
        for t in wt:
            t.writer = (ek, idx)
            t.readers = {}
        for t in rt:
            if t.readers.get(ek, 0) < idx:
                t.readers[ek] = idx

    def dma(self, qk, out, in_, semkey, reads=(), writes=()):
        rt = _tiles(reads)
        wt = _tiles(writes)
        if self.dry:
            return
        self._deps(qk, rt, wt)
        self.count[semkey] += 16
        c = self.count[semkey]
        self.engs[qk].dma_start(out=out, in_=in_).then_inc(self.sems[semkey], 16)
        self.log.setdefault(qk, []).append(("i", semkey, 16))
        for t in wt:
            t.writer = (semkey, c)
            t.readers = {}
        for t in rt:
            t.readers[semkey] = c

    def group_fix(self, semkey, tiles):
        if self.dry:
            return
        for t in _tiles(tiles):
            t.writer = (semkey, self.count[semkey])

    def finish(self, ek, semkeys):
        if self.dry:
            return
        for k in semkeys:
            if self.count[k] > 0:
                self.engs[ek].wait_ge(self.sems[k], self.count[k])

    def mm(self, out, lhsT, rhs, start, stop, inc=None):
        self.op("pe", lambda e: e.matmul(out.ap, lhsT=lhsT.ap, rhs=rhs.ap, start=start, stop=stop),
                reads=[lhsT, rhs], writes=[out], inc=(True if inc is None else inc))

    def act(self, out, in_, func, bias=None, scale=1.0, eng="act"):
        rd = [in_]
        kw = {}
        if bias is not None:
            if isinstance(bias, V):
                rd.append(bias)
                kw["bias"] = bias.ap
            else:
                kw["bias"] = bias
        if isinstance(scale, V):
            rd.append(scale)
            kw["scale"] = scale.ap
        else:
            kw["scale"] = scale
        self.op("act", lambda e: e.activation(out=out.ap, in_=in_.ap, func=func, **kw), reads=rd, writes=[out])

    def tt(self, eng, out, a, b, op):
        self.op(eng, lambda e: e.tensor_tensor(out=out.ap, in0=a.ap, in1=b.ap, op=op), reads=[a, b], writes=[out])

    def ts(self, eng, out, a, s1, op0, s2=None, op1=None):
        rd = [a]
        if isinstance(s1, V):
            rd.append(s1)
        if isinstance(s2, V):
            rd.append(s2)
        a1 = s1.ap if isinstance(s1, V) else s1
        a2 = s2.ap if isinstance(s2, V) else s2
        if op1 is None:
            self.op(eng, lambda e: e.tensor_scalar(out=out.ap, in0=a.ap, scalar1=a1, scalar2=None, op0=op0),
                    reads=rd, writes=[out])
        else:
            self.op(eng, lambda e: e.tensor_scalar(out=out.ap, in0=a.ap, scalar1=a1, scalar2=a2, op0=op0, op1=op1),
                    reads=rd, writes=[out])

    def stt(self, eng, out, a, s, b, op0, op1):
        rd = [a, b]
        if isinstance(s, V):
            rd.append(s)
        sa = s.ap if isinstance(s, V) else s
        self.op(eng, lambda e: e.scalar_tensor_tensor(out=out.ap, in0=a.ap, scalar=sa, in1=b.ap, op0=op0, op1=op1),
                reads=rd, writes=[out])

    def copy(self, eng, out, in_):
        if eng == "act":
            self.op("act", lambda e: e.copy(out=out.ap, in_=in_.ap), reads=[in_], writes=[out])
        else:
            self.op(eng, lambda e: e.tensor_copy(out=out.ap, in_=in_.ap), reads=[in_], writes=[out])

    def recip(self, out, in_):
        self.op("dve", lambda e: e.reciprocal(out=out.ap, in_=in_.ap), reads=[in_], writes=[out])

    def memset(self, eng, out, val):
        self.op(eng, lambda e: e.memset(out.ap, val), writes=[out])


class WStream:
    NS = 5
    SLOT = 2048

    def __init__(self, cx, nc, stack, plan):
        self.cx = cx
        self.plan = plan if plan is not None else []
        self.record = plan is None
        self.cur = 0
        self.issued = 0
        self.hold = 1
        self.slots = []
        self.semk = []
        for i in range(self.NS):
            t = stack.enter_context(nc.sbuf_tensor("wslot%d" % i, [128, self.SLOT], F32R))
            self.slots.append(Tile(t[:], "wslot%d" % i))
            self.semk.append(cx.new_dma_sem("w%d" % i))

    def _view(self, i, shape):
        s = self.slots[i % self.NS]
        n = 1
        for d in shape[1:]:
            n *= d
        assert n <= self.SLOT, shape
        v = s[0:shape[0], 0:n]
        if len(shape) == 3:
            v = v.re("p (a b) -> p a b", a=shape[1])
        return v

    def get(self, dram_ap, shape):
        i = self.cur
        self.cur += 1
        if self.record:
            self.plan.append((dram_ap, tuple(shape)))
            return self._view(i, shape)
        assert self.plan[i][1] == tuple(shape), (i, self.plan[i][1], shape)
        lim = min(len(self.plan), i + self.NS - self.hold + 1)
        while self.issued < lim:
            j = self.issued
            ap_j, shp_j = self.plan[j]
            v = self._view(j, shp_j)
            self.cx.dma("sp", v.ap, ap_j.bitcast(F32R), self.semk[j % self.NS], writes=[v])
            self.issued += 1
        return self._view(i, shape)


class Prog:
    def __init__(self, nc, dry, plan, stage):
        self.nc = nc
        self.dry = dry
        self.plan = plan
        self.stage = stage
        self.cut = int(os.environ.get('MK_CUT', '999'))

    def dram_in(self, name, shape):
        return self.nc.dram_tensor(name, list(shape), F32, kind="ExternalInput").ap()

    def dram_out(self, name, shape):
        return self.nc.dram_tensor(name, list(shape), F32, kind="ExternalOutput").ap()

    def sb(self, name, shape, dt=F32):
        return self.st.enter_context(self.nc.sbuf_tensor("sb_" + name, list(shape), dt))

    def psum(self):
        t = self.ps[self.cx.n_ps % self.ps_rr_n]
        self.cx.n_ps += 1
        return t

    def psum_stat(self, gi):
        return self.ps[6 + gi]

    def tmp(self):
        t = self.tp[self.cx.n_tp % len(self.tp)]
        self.cx.n_tp += 1
        return t

    def tmpr(self):
        t = self.tpr[self.n_tpr % len(self.tpr)]
        self.n_tpr += 1
        return t

    def build(self):
        nc = self.nc
        with ExitStack() as st:
            self.st = st
            cx = self.cx = Ctx(nc, self.dry)
            cx.setup(st)
            self.W = WStream(cx, nc, st, self.plan)
            self.declare_io()
            self.alloc()
            self.load_consts()
            self.mod_pending = [(l, nb) for l in range(2) for nb in range(36)]
            self.prepass()
            G = GROUPS_MAIN
            self.load_x(self.d_xm)
            self.boundary(None, (0, 0), G)
            self.ffn(0, 0, G)
            self.boundary((0, 0), (0, 1), G)
            self.even_mixer()
            self.boundary((0, 1), (0, 2), G)
            self.ffn_feed_l1 = True
            self.ffn(0, 1, G)
            self.boundary((0, 2), (1, 0), G)
            self.ffn(1, 0, G)
            self.boundary((1, 0), (1, 1), G)
            self.odd_mixer()
            self.boundary((1, 1), (1, 2), G)
            self.ffn(1, 1, G)
            self.boundary((1, 2), None, G)
            self.store_out()
            cx.finish("sp", [self.dsem_out])
        return self.W.plan

    def declare_io(self):
        di = self.dram_in
        self.d_xm = di("xm", [128, KC, T])
        self.d_xo = di("xo", [128, KC, T])
        self.d_cond = di("cond", [128, KC, 2])
        self.d_wmod = di("w_mod", [2, D, 9 * D])
        self.d_bmod = di("b_mod", [128, 2 * 72])
        self.d_npre = di("npre", [128, 2 * 3 * KC])
        self.d_npost = di("npost", [128, 2 * 3 * KC])
        self.d_wg = di("ffn_w_gate", [2, 2, D, FF])
        self.d_wu = di("ffn_w_up", [2, 2, D, FF])
        self.d_wd = di("ffn_w_down", [2, 2, FF, D])
        self.d_ident = di("ident", [128, 128])
        self.d_winfm = di("w_in_fm", [D, 2688])
        self.d_wintm = di("w_in_tm", [D, 1280])
        self.d_wout = di("ev_w_out", [D, D])
        self.d_ropec = di("rope_c", [128, 512])
        self.d_ropes = di("rope_s", [128, 512])
        self.d_tri = di("tri", [128, 4 * 128])
        self.d_mm = di("gmask", [128, 4 * 128])
        self.d_fl = di("flags", [128, 6])
        self.d_ck = di("ctx_k", [128, 2 * 256])
        self.d_cv = di("ctx_v", [128, 2 * 2 * 128])
        self.d_s0f = di("s0f", [128, 256])
        self.d_s0b = di("s0b", [128, 256])
        self.d_sink = di("sinkT", [128, 4])
        self.d_gg = di("gla_g", [128, 4])
        self.d_wa = di("gla_wa", [64, 256])
        self.d_ba = di("gla_ba", [64, 256])
        self.d_cmwin = di("cm_w_in", [D, 2 * D])
        self.d_cmwout = di("cm_w_out", [D, D])
        self.d_cmgb = di("cm_gb", [128, 2 * D])
        self.d_cmws = di("cm_ws", [128, 512])
        self.d_cmbs = di("cm_bs", [128, 512])
        self.o_kv = self.dram_out("kv_out", [128, 4, 256])
        self.o_sf = self.dram_out("sf_out", [128, 2, 256])
        self.o_sb = self.dram_out("sb_out", [128, 2, 256])
        self.o_y = self.dram_out("ym", [128, KC, T])

    def alloc(self):
        nc, st, cx = self.nc, self.st, self.cx
        xt = self.sb("xT", [128, KC, T], F32)
        self.X = [Tile(xt[:, k, :], "x%d" % k, split=512) for k in range(KC)]
        ar = self.sb("arena", [128, 30, T], F32R)
        self.PG = [Tile(ar[:, i, :], "pg%d" % i, split=512) for i in range(30)]
        self.H = self.PG[0:8]
        self.ACT = self.PG[8:30]
        self.tp = [Tile(self.sb("tp%d" % i, [128, 512], F32)[:], "tp%d" % i) for i in range(3)]
        self.tpr = [Tile(self.sb("tpr%d" % i, [128, 512], F32R)[:], "tpr%d" % i) for i in range(4)]
        self.n_tpr = 0
        self.ps = [Tile(st.enter_context(nc.psum_tensor("ps%d" % i, [128, 512], F32))[:], "ps%d" % i) for i in range(8)]
        self.rstd = Tile(self.sb("rstd", [128, T], F32)[:], "rstd", split=512)
        self.cond = Tile(self.sb("cond", [128, KC, 2], F32)[:], "cond")
        self.scT = Tile(self.sb("scT", [128, KC, 2], F32R)[:], "scT")
        self.bm = Tile(self.sb("bm", [128, 2, 72], F32)[:], "bm")
        self.npre = Tile(self.sb("npre", [128, 2, 3, KC], F32)[:], "npre")
        self.npost = Tile(self.sb("npost", [128, 2, 3, KC], F32)[:], "npost")
        self.ident = Tile(self.sb("ident", [128, 128], F32)[:], "ident")
        self.onesF = Tile(self.sb("onesF", [128, 128], F32R)[:], "onesF")
        self.epsb = Tile(self.sb("epsb", [128, 1], F32)[:], "epsb")
        self.modT = Tile(self.sb("modT", [128, 2, 72, 2], F32)[:], "modT")
        mrt = self.sb("modrow", [2, 2, 256], F32)
        self.modrow = [Tile(mrt[:, i, :], "modrow%d" % i) for i in range(2)]
        self.mod_unfinished = None
        self.ffn_feed_l1 = False
        self.defer_B = None
        self.y_pending = []
        self.tab = Tile(self.sb("tab", [128, 2, 3, 2, 3, KC], F32)[:], "tab")
        self.tri = Tile(self.sb("tri", [128, 4, 128], F32)[:], "tri")
        self.MM = Tile(self.sb("gmask", [128, 4, 128], F32R)[:], "gmask")
        self.FL = Tile(self.sb("flags", [128, 6], F32)[:], "flags")
        self.CK = Tile(self.sb("ctxk", [128, 2, 256], F32R)[:], "ctxk")
        self.CV = Tile(self.sb("ctxv", [128, 2, 2, 128], F32R)[:], "ctxv")
        self.HK = Tile(self.sb("hk", [128, 2, 2, 128], F32R)[:], "hk")
        self.HV = Tile(self.sb("hv", [128, 2, 256], F32R)[:], "hv")
        self.S = [Tile(self.sb("st%d" % i, [128, 2, 128], F32R)[:], "st%d" % i) for i in range(7)]
        self.FIN = [Tile(self.sb("fin%d" % i, [128, 2, 2, 128], F32)[:], "fin%d" % i) for i in range(2)]
        self.sinkT = Tile(self.sb("sinkT", [128, 4], F32)[:], "sinkT")
        self.esink = Tile(self.sb("esink", [128, 4], F32)[:], "esink")
        self.gg = Tile(self.sb("gg", [128, 4], F32)[:], "gg")
        self.WA = Tile(self.sb("wa", [64, 256], F32R)[:], "wa")
        self.BA = Tile(self.sb("ba", [64, 256], F32R)[:], "ba")
        self.DEC = Tile(self.sb("dec", [128, 12, 2], F32)[:], "dec")
        self.DM = Tile(self.sb("dm", [128, 12, 2], F32)[:], "dm")
        lt = self.sb("lnst", [128, 6, 40], F32)
        self.lnst = [Tile(lt[:, i, :], "lnst%d" % i) for i in range(6)]
        self.lnst_all = lt[:]
        self.ps_rr_n = 6
        self.n_ev = 0
        self.dsem_c = cx.new_dma_sem("const")
        self.dsem_c2 = cx.new_dma_sem("const2")
        self.dsem_cs = cx.new_dma_sem("const_sp")
        self.dsem_x = cx.new_dma_sem("x")
        self.dsem_out = cx.new_dma_sem("out")

    def load_consts(self):
        cx = self.cx
        q = "pool"
        dsem_cond = cx.new_dma_sem("cond")
        cx.dma("sp", self.cond.ap, self.d_cond, dsem_cond, writes=[self.cond])
        cx.act(self.scT.v(), self.cond.v(), AF.Silu)
        cx.dma(q, self.npre.ap, self.d_npre.rearrange("p (l i k) -> p l i k", l=2, i=3), self.dsem_c, writes=[self.npre])
        cx.dma(q, self.npost.ap, self.d_npost.rearrange("p (l i k) -> p l i k", l=2, i=3), self.dsem_c, writes=[self.npost])
        cx.dma(q, self.ident.ap, self.d_ident, self.dsem_c, writes=[self.ident])
        cx.dma(q, self.bm.ap, self.d_bmod.rearrange("p (l c) -> p l c", l=2), self.dsem_c, writes=[self.bm])
        for (t, d, shp) in [(self.tri, self.d_tri, "p (a b) -> p a b"), (self.FL, self.d_fl, None), (self.sinkT, self.d_sink, None),
                            (self.gg, self.d_gg, None)]:
            src = d if shp is None else d.rearrange(shp, a=4)
            cx.dma(q, t.ap, src, self.dsem_c, writes=[t])
        cx.dma("sp", self.MM.ap, self.d_mm.rearrange("p (a b) -> p a b", a=4).bitcast(F32R), self.dsem_cs, writes=[self.MM])
        cx.dma("sp", self.CK.ap, self.d_ck.rearrange("p (a b) -> p a b", a=2).bitcast(F32R), self.dsem_cs, writes=[self.CK])
        cx.dma("sp", self.CV.ap, self.d_cv.rearrange("p (a b c) -> p a b c", a=2, b=2).bitcast(F32R), self.dsem_cs, writes=[self.CV])
        cx.dma("sp", self.S[0].ap, self.d_s0f.rearrange("p (a b) -> p a b", a=2).bitcast(F32R), self.dsem_cs, writes=[self.S[0]])
        cx.dma("sp", self.S[1].ap, self.d_s0b.rearrange("p (a b) -> p a b", a=2).bitcast(F32R), self.dsem_cs, writes=[self.S[1]])
        cx.dma("sp", self.WA.ap, self.d_wa.bitcast(F32R), self.dsem_cs, writes=[self.WA])
        cx.dma("sp", self.BA.ap, self.d_ba.bitcast(F32R), self.dsem_cs, writes=[self.BA])
        cx.group_fix(self.dsem_c, [self.npre, self.npost, self.ident, self.bm, self.tri, self.FL, self.sinkT, self.gg])
        cx.group_fix(self.dsem_cs, [self.MM, self.CK, self.CV, self.S[0], self.S[1], self.WA, self.BA])
        cx.memset("dve", self.FIN[0].v(), 0.0)
        cx.copy("dve", self.S[6].v(), self.FIN[0][:, 0, :, :])
        cx.act(self.esink.v(), self.sinkT.v(), AF.Exp)
        cx.memset("dve", self.tp[0][:, 0:128], 1.0)
        cx.copy("dve", self.onesF.v(), self.tp[0][:, 0:128])
        cx.memset("dve", self.epsb.v(), EPS)

    def mod_step(self, n=1):
        for _ in range(n):
            if not self.mod_pending:
                return
            l, nb = self.mod_pending.pop(0)
            self.mod_block(l, nb)
            if nb % 12 == 11:
                self.mod_tab(l, nb // 12)

    def mod_step_ffn(self):
        if self.mod_pending and (self.mod_pending[0][0] == 0 or self.ffn_feed_l1):
            self.mod_step()

    def mod_need(self, l, i):
        while self.mod_pending and (self.mod_pending[0][0] < l or
                                    (self.mod_pending[0][0] == l and self.mod_pending[0][1] < 12 * (i + 1))):
            self.mod_step()

    def mod_flush(self):
        while self.mod_pending:
            self.mod_step()

    def mod_block(self, l, nb):
        cx = self.cx
        Wt = self.W.get(self.d_wmod[l].rearrange("(kc p) n -> p kc n", p=128)[:, :, nb * 256:(nb + 1) * 256], [128, KC, 256])
        ps = self.psum()
        for kc in range(KC):
            cx.mm(ps[0:2, 0:256], self.scT[:, kc, :], Wt[:, kc, :], start=(kc == 0), stop=(kc == KC - 1), inc=(kc == KC - 1))
        mr = self.modrow[nb % 2]
        cx.copy("act", mr.v(), ps[0:2, 0:256])
        self.mod_finish()
        self.mod_unfinished = (l, nb)
        if nb % 12 == 11:
            self.mod_finish()

    def mod_finish(self):
        cx = self.cx
        if self.mod_unfinished is None:
            return
        l, nb = self.mod_unfinished
        self.mod_unfinished = None
        mr = self.modrow[nb % 2]
        pt = self.psum()
        for hf in range(2):
            cx.op("pe", lambda e, hf=hf, pt=pt: e.transpose(out=pt.ap[:, hf * 2:hf * 2 + 2],
                                                        in_=mr.ap[0:2, hf * 128:(hf + 1) * 128],
                                                        identity=self.ident.ap[0:2, 0:2]),
                  reads=[mr, self.ident], writes=[pt], inc=(hf == 1))
        for r in range(2):
            cx.tt("dve", self.modT[:, l, nb * 2:nb * 2 + 2, r], pt[:, 0:4].re("p (a b) -> p a b", a=2)[:, :, r],
                  self.bm[:, l, nb * 2:nb * 2 + 2], ALU.add)

    def mod_tab(self, l, i):
        cx = self.cx
        if True:
            coef = 1.0 if i == 1 else 0.5
            for r in range(2):
                sh = self.modT[:, l, i * 24 + 0:i * 24 + 8, r]
                sc = self.modT[:, l, i * 24 + 8:i * 24 + 16, r]
                gt = self.modT[:, l, i * 24 + 16:i * 24 + 24, r]
                cx.stt("dve", self.tab[:, l, i, r, 0, :], sc, 1.0, self.npre[:, l, i, :], ALU.add, ALU.mult)
                cx.copy("dve", self.tab[:, l, i, r, 1, :], sh)
                cx.stt("dve", self.tab[:, l, i, r, 2, :], gt, coef, self.npost[:, l, i, :], ALU.mult, ALU.mult)

    def load_x(self, d_x):
        cx = self.cx
        for k in range(KC):
            cx.dma("pool", self.X[k].ap, d_x[:, k, :], self.dsem_x, writes=[self.X[k]])
        cx.group_fix(self.dsem_x, self.X)

    def boundary(self, out_li, in_li, groups):
        if in_li is not None:
            self.mod_need(*in_li)
        self.run_deferred()
        for gi in range(2):
            self._bg_out(gi, out_li, in_li, groups)
        for gi in range(2):
            self._bg_in(gi, in_li, groups, part=0)
        for gi in range(2):
            self._bg_in(gi, in_li, groups, part=1)

    def run_deferred(self):
        pass

    def _bg_out(self, gi, out_li, in_li, groups):
        cx = self.cx
        SQ = self.PG[8:16]
        t0, tn, r = groups[gi]
        if out_li is not None:
            prev = None
            for kc in range(KC):
                tq = self.tmp()
                cx.tt("dve", tq[:, 0:tn], self.H[kc][:, t0:t0 + tn].f(), self.rstd[:, t0:t0 + tn], ALU.mult)
                if prev is not None:
                    pk, ptq = prev
                    cx.tt("dve", self.X[pk][:, t0:t0 + tn], ptq[:, 0:tn], self.X[pk][:, t0:t0 + tn], ALU.add)
                prev = (kc, tq)
            pk, ptq = prev
            cx.tt("dve", self.X[pk][:, t0:t0 + tn], ptq[:, 0:tn], self.X[pk][:, t0:t0 + tn], ALU.add)
        if in_li is not None:
            for kc in range(KC):
                cx.act(SQ[kc][:, t0:t0 + tn], self.X[kc][:, t0:t0 + tn], AF.Square)

    def _bg_in(self, gi, in_li, groups, part):
        cx = self.cx
        if in_li is None:
            return
        SQ = self.PG[8:16]
        t0, tn, r = groups[gi]
        li, ii = in_li
        pss = self.psum_stat(gi)
        if part == 0:
            for kc in range(KC):
                cx.mm(pss[:, 0:tn], self.onesF.v(), SQ[kc][:, t0:t0 + tn], start=(kc == 0), stop=(kc == KC - 1))
            self.rsqrt(self.rstd[:, t0:t0 + tn], pss[:, 0:tn], 1.0 / D, tn)
            return
        for kc in range(KC):
            tq = self.tmp()
            cx.tt("dve", tq[:, 0:tn], self.X[kc][:, t0:t0 + tn], self.rstd[:, t0:t0 + tn], ALU.mult)
            cx.act(self.H[kc][:, t0:t0 + tn], tq[:, 0:tn], AF.Identity, bias=self.tab[:, li, ii, r, 1, kc:kc + 1],
                   scale=self.tab[:, li, ii, r, 0, kc:kc + 1])

    def rsqrt(self, out, src, scale, tn):
        cx = self.cx
        tq = self.tmp()
        cx.act(tq[:, 0:tn], src, AF.Ln, bias=self.epsb[:, 0:1], scale=scale)
        cx.act(out, tq[:, 0:tn], AF.Exp, scale=-0.5)

    def ffn(self, l, s, groups):
        cx = self.cx
        self.cur_out = (l, 0 if s == 0 else 2)
        self.cur_groups = groups
        self.mod_need(l, 0 if s == 0 else 2)
        wg = self.d_wg[l, s].rearrange("(kc p) n -> p kc n", p=128)
        wu = self.d_wu[l, s].rearrange("(kc p) n -> p kc n", p=128)
        wd = self.d_wd[l, s].rearrange("(fc p) n -> p fc n", p=128)
        self.ps_rr_n = 8
        self.W.hold = 2
        for blk in range(11):
            Wg = self.W.get(wg[:, :, blk * 256:(blk + 1) * 256], [128, KC, 256])
            Wu = self.W.get(wu[:, :, blk * 256:(blk + 1) * 256], [128, KC, 256])
            for jj in range(2):
                fj = blk * 2 + jj
                for (t0, tn, _) in groups:
                    pg = self.psum()
                    pu = self.psum()
                    for kc in range(KC):
                        cx.mm(pg[:, 0:tn], Wg[:, kc, jj * 128:(jj + 1) * 128], self.H[kc][:, t0:t0 + tn],
                              start=(kc == 0), stop=(kc == KC - 1), inc=(kc == KC - 1))
                    for kc in range(KC):
                        cx.mm(pu[:, 0:tn], Wu[:, kc, jj * 128:(jj + 1) * 128], self.H[kc][:, t0:t0 + tn],
                              start=(kc == 0), stop=(kc == KC - 1), inc=(kc == KC - 1))
                    sl = self.tmp()
                    cx.act(sl[:, 0:tn], pg[:, 0:tn], AF.Silu)
                    cx.tt("dve", self.ACT[fj][:, t0:t0 + tn], sl[:, 0:tn], pu[:, 0:tn], ALU.mult)
            self.mod_step_ffn()
        self.ps_rr_n = 6
        Y = self.H
        for dc in range(KC):
            Wd0 = self.W.get(wd[:, 0:11, dc * 128:(dc + 1) * 128], [128, 11, 128])
            Wd1 = self.W.get(wd[:, 11:22, dc * 128:(dc + 1) * 128], [128, 11, 128])
            for gi, (t0, tn, _) in enumerate(groups):
                py = self.psum()
                for fc in range(FC):
                    wv_ = Wd0[:, fc, :] if fc < 11 else Wd1[:, fc - 11, :]
                    cx.mm(py[:, 0:tn], wv_, self.ACT[fc][:, t0:t0 + tn], start=(fc == 0), stop=(fc == FC - 1),
                          inc=(fc == FC - 1))
                self.y_evac(dc, t0, tn, gi, py[:, 0:tn], dc == 0, dc == KC - 1)
            self.mod_step_ffn()
        self.W.hold = 1
        self.y_finish(groups)


    def fm_linear(self, wv, col0, nchunks, src, tok_ranges, evac):
        cx = self.cx
        nk = len(src)
        j = 0
        while j < nchunks:
            nb = min(2, nchunks - j)
            Wt = self.W.get(wv[:, :, col0 + j * 128: col0 + (j + nb) * 128], [128, nk, nb * 128])
            for jj in range(nb):
                for (t0, tn) in tok_ranges:
                    ps = self.psum()
                    for kc in range(nk):
                        cx.mm(ps[:, 0:tn], Wt[:, kc, jj * 128:(jj + 1) * 128], src[kc][:, t0:t0 + tn],
                              start=(kc == 0), stop=(kc == nk - 1), inc=(kc == nk - 1))
                    evac(j + jj, t0, tn, ps[:, 0:tn])
            j += nb
            self.mod_step(1)

    def tm_linear(self, wv, col0, ncols, src, tiles, evac):
        cx = self.cx
        nk = len(src)
        c = 0
        while c < ncols:
            bw = min(256, ncols - c)
            Wt = self.W.get(wv[:, :, col0 + c: col0 + c + bw], [128, nk, bw])
            for tt in tiles:
                if tt >= 4:
                    self.run_deferred()
                ps = self.psum()
                for kc in range(nk):
                    cx.mm(ps[:, 0:bw], src[kc][:, tt * 128:(tt + 1) * 128], Wt[:, kc, :],
                          start=(kc == 0), stop=(kc == nk - 1), inc=(kc == nk - 1))
                evac(c, bw, tt, ps[:, 0:bw])
            c += bw
            self.mod_step(1)

    def ev_copy(self, dst, src):
        self.n_ev += 1
        self.cx.copy("act" if self.n_ev % 2 else "dve", dst, src)

    def y_evac(self, dc, t0, tn, gi, ps, first, last):
        cx = self.cx
        Y = self.H
        lo, io = self.cur_out
        r = self.cur_groups[gi][2]
        cx.act(Y[dc][:, t0:t0 + tn], ps, AF.Copy, scale=self.tab[:, lo, io, r, 2, dc:dc + 1])
        sq = self.tmpr()
        cx.act(sq[:, 0:tn], ps, AF.Square)
        self.y_pending.append((gi, sq, tn, first, last))
        while len(self.y_pending) > 3:
            self.y_flush1()

    def y_flush1(self):
        gi, sq, tn, first, last = self.y_pending.pop(0)
        self.cx.mm(self.psum_stat(gi)[:, 0:tn], self.onesF.v(), sq[:, 0:tn], start=first, stop=last)

    def y_finish(self, groups):
        cx = self.cx
        while self.y_pending:
            self.y_flush1()
        for gi, (t0, tn, _) in enumerate(groups):
            self.rsqrt(self.rstd[:, t0:t0 + tn], self.psum_stat(gi)[:, 0:tn], 1.0 / D, tn)

    def out_linear(self, wdram, src, l):
        wv = wdram.rearrange("(kc p) n -> p kc n", p=128)
        gidx = {512: 0, 256: 1}
        self.cur_out = (l, 1)
        self.cur_groups = GROUPS_MAIN
        self.mod_need(l, 1)

        def ev(j, t0, tn, ps):
            self.y_evac(j, t0, tn, gidx[tn], ps, j == 0, j == KC - 1)
        self.fm_linear(wv, 0, KC, src, [(0, 512), (512, 256)], ev)
        self.y_finish(GROUPS_MAIN)

    def bcast(self, v, n):
        a = v.ap
        return V(v.tiles, bass.AP(a.tensor, a.offset, [list(a.ap[0]), [0, n], list(a.ap[1])]))

    def gla_ld(self, LR, tt, ldpage):
        cx = self.cx
        for dr in range(2):
            ps = self.psum()
            rows = slice(32 * dr, 32 * dr + 16)
            r1 = slice(32 * dr, 32 * dr + 1)
            cx.mm(ps[:, 0:256], LR[rows, tt * 128:(tt + 1) * 128], self.WA[rows, :], True, False, inc=False)
            cx.mm(ps[:, 0:256], self.onesF[r1, 0:128], self.BA[r1, :], False, True)
            e = self.tmp()
            cx.act(e[:, 0:256], ps[:, 0:256], AF.Exp, scale=-1.0)
            cx.act(ldpage[:, dr * 256:(dr + 1) * 256], e[:, 0:256], AF.Ln, bias=1.0)

    def gla_prep(self, dr, kvp, ld, gp, slot, need_q, QB=None, KB=None, tok0=0, flag=None):
        cx = self.cx
        mc = 1 if dr == 0 else 3
        mb = 0 if dr == 0 else 2
        ldv = ld[:, dr * 256:(dr + 1) * 256]
        ps = self.psum()
        cx.mm(ps[:, 0:256], self.MM[:, mc, :], ldv, True, True)
        ec = self.tmp()
        cx.act(ec[:, 0:256], ps[:, 0:256], AF.Exp, scale=-1.0 / 16)
        if flag is not None:
            cx.stt("dve", gp[:, 0:256], kvp[:, 0:256].f(), flag, ec[:, 0:256], ALU.mult, ALU.mult)
        else:
            cx.tt("dve", gp[:, 0:256], kvp[:, 0:256].f(), ec[:, 0:256], ALU.mult)
        ps2 = self.psum()
        if need_q:
            for pc in range(2):
                cx.mm(ps2[:, pc * 128:(pc + 1) * 128], ldv[:, pc * 128:(pc + 1) * 128], self.MM[:, mb, :], True, True)
            eb = self.tmp()
            cx.act(eb[:, 0:256], ps2[:, 0:256], AF.Exp, scale=-1.0 / 16, bias=float(np.log(0.125)))
            for pc in range(2):
                cx.tt("dve", gp[:, 256 + pc * 128:256 + (pc + 1) * 128], QB[pc][:, tok0:tok0 + 128].f(),
                      eb[:, pc * 128:(pc + 1) * 128], ALU.mult)
            enb = self.tmp()
            cx.act(enb[:, 0:256], ps2[:, 0:256], AF.Exp, scale=1.0 / 16)
            for pc in range(2):
                cx.tt("dve", gp[:, 512 + pc * 128:512 + (pc + 1) * 128], KB[pc][:, tok0:tok0 + 128].f(),
                      enb[:, pc * 128:(pc + 1) * 128], ALU.mult)
            col = 127 if dr == 0 else 0
            cx.act(self.DEC[:, slot, :], ps2[:, 0:256].re("p (a b) -> p a b", a=2)[:, :, col], AF.Exp, scale=-1.0 / 16)
        else:
            for pc in range(2):
                cx.mm(ps2[:, pc * 2:pc * 2 + 2], ldv[:, pc * 128:(pc + 1) * 128], self.onesF[:, 0:2], True, True)
            cx.act(self.DEC[:, slot, :], ps2[:, 0:4].re("p (a b) -> p a b", a=2)[:, :, 0], AF.Exp, scale=-1.0 / 16)

    def gla_update(self, Sprev, Snew, gp, kvp, dec, flag=None, out_f32=None):
        cx = self.cx
        U = self.psum()
        for pc in range(2):
            for e in range(2):
                h = 2 * pc + e
                cx.mm(U[:, h * 128:(h + 1) * 128], gp[:, pc * 128:(pc + 1) * 128], kvp[:, 256 + h * 128:256 + (h + 1) * 128], True, True)
        for pc in range(2):
            for e in range(2):
                h = 2 * pc + e
                rows = slice(64 * e, 64 * e + 64)
                dst = Snew[rows, pc, :] if out_f32 is None else out_f32[rows, pc, :]
                cx.stt("dve", dst, Sprev[rows, pc, :].f(), dec[rows, pc:pc + 1], U[rows, h * 128:(h + 1) * 128], ALU.mult, ALU.add)

    def gla_out_a(self, gpf, gpb):
        cx = self.cx
        ats = []
        for dr, gp in ((0, gpf), (1, gpb)):
            mb = 0 if dr == 0 else 2
            Ae = [self.psum(), self.psum()]
            for h in range(4):
                pc, e = h // 2, h % 2
                rows = slice(64 * e, 64 * e + 64)
                cx.mm(Ae[e][:, pc * 128:(pc + 1) * 128], gp[rows, 512 + pc * 128:512 + (pc + 1) * 128],
                      gp[rows, 256 + pc * 128:256 + (pc + 1) * 128], True, True)
            at = self.tmpr()
            atv = at[:, 0:512].re("p (a e b) -> p a e b", a=2, e=2)
            for e in range(2):
                cx.tt("dve", atv[:, :, e, :], Ae[e][:, 0:256].re("p (a b) -> p a b", a=2),
                      self.bcast(self.MM[:, mb, :].f(), 2), ALU.mult)
            ats.append(at)
        return ats

    def gla_out(self, kvp, gpf, gpb, Sf, Sb, dst_pages, tok0, ats):
        cx = self.cx
        PI = self.psum()
        for h in range(4):
            o = PI[:, h * 128:(h + 1) * 128]
            vb = kvp[:, 256 + h * 128:256 + (h + 1) * 128]
            cx.mm(o, vb, ats[0][:, h * 128:(h + 1) * 128], True, False, inc=False)
            cx.mm(o, vb, ats[1][:, h * 128:(h + 1) * 128], False, True)
        PX = [self.psum(), self.psum()]
        for h in range(4):
            pc, e = h // 2, h % 2
            rows = slice(64 * e, 64 * e + 64)
            o = PX[e][:, pc * 128:(pc + 1) * 128]
            cx.mm(o, Sf[rows, pc, :], gpf[rows, 256 + pc * 128:256 + (pc + 1) * 128], True, False, inc=False)
            cx.mm(o, Sb[rows, pc, :], gpb[rows, 256 + pc * 128:256 + (pc + 1) * 128], False, True)
        for h in range(4):
            pc, e = h // 2, h % 2
            tc_ = self.tmp()
            cx.copy("act", tc_[:, 0:128], PI[:, h * 128:(h + 1) * 128])
            cx.tt("dve", dst_pages[h][:, tok0:tok0 + 128], tc_[:, 0:128], PX[e][:, pc * 128:(pc + 1) * 128], ALU.add)

    def load_rope(self, pc_page, ps_page):
        cx = self.cx
        cx.dma("sp", pc_page[:, 0:512].ap, self.d_ropec.bitcast(F32R), self.dsem_c2, writes=[pc_page])
        cx.dma("sp", ps_page[:, 0:512].ap, self.d_ropes.bitcast(F32R), self.dsem_c2, writes=[ps_page])
        cx.group_fix(self.dsem_c2, [pc_page, ps_page])

    def prepass(self):
        cx = self.cx
        PG = self.PG
        self.load_x(self.d_xo)
        self.boundary(None, (0, 0), GROUPS_OTH)
        self.ffn(0, 0, GROUPS_OTH)
        self.boundary((0, 0), (0, 1), GROUPS_OTH)
        if self.cut <= 1:
            return
        wfm = self.d_winfm.rearrange("(kc p) n -> p kc n", p=128)
        wtm = self.d_wintm.rearrange("(kc p) n -> p kc n", p=128)
        LR, KDh, KSh, RC, RS = PG[8], PG[9], PG[10], PG[14], PG[15]
        KVp = PG[16:22]
        LD = PG[22:28]
        GP = [PG[11], PG[12], PG[13], PG[28]]
        self.load_rope(RC, RS)
        allr = [(0, 512), (512, 256)]
        halo = [(0, 128), (640, 128)]
        self.fm_linear(wfm, 10 * 128, 1, self.H, allr, lambda j, t0, tn, ps: self.ev_copy(LR[:, t0:t0 + tn], ps))

        def ev_h(dst):
            def f(j, t0, tn, ps):
                hs = 0 if t0 == 0 else 1
                self.ev_copy(dst[:, j * 256 + hs * 128: j * 256 + (hs + 1) * 128], ps)
            return f
        self.fm_linear(wfm, 4 * 128, 2, self.H, halo, ev_h(KDh))
        self.fm_linear(wfm, 19 * 128, 2, self.H, halo, ev_h(KSh))
        if self.cut <= 2:
            return
        for kv in range(2):
            t1 = self.tmp()
            t2 = self.tmp()
            cx.tt("dve", t1[:, 0:256], KDh[:, kv * 256:(kv + 1) * 256].f(), RC[:, 256:512].f(), ALU.mult)
            cx.tt("dve", t2[:, 0:256], KSh[:, kv * 256:(kv + 1) * 256].f(), RS[:, 256:512].f(), ALU.mult)
            cx.tt("dve", self.HK[:, kv, :, :], t1[:, 0:256].re("p (a b) -> p a b", a=2), t2[:, 0:256].re("p (a b) -> p a b", a=2), ALU.add)
        self.tm_linear(wtm, 0, 256, self.H, [0, 5],
                       lambda c, bw, tt, ps: self.ev_copy(self.HV[:, 0 if tt == 0 else 1, :], ps))
        if self.cut <= 3:
            return
        self.tm_linear(wtm, 512, 768, self.H, list(range(6)),
                       lambda c, bw, tt, ps: self.ev_copy(KVp[tt][:, c:c + bw], ps))
        for tt in range(6):
            self.gla_ld(LR, tt, LD[tt])
        if self.cut <= 4:
            return
        GPs = [PG[11], PG[12], PG[13], PG[28]]

        def kslot(n):
            return GPs[n // 3][:, (n % 3) * 256:(n % 3 + 1) * 256]
        jobs = []
        for dr in range(2):
            order = [0, 1, 2, 3, 4, 5] if dr == 0 else [5, 4, 3, 2, 1, 0]
            for n, tt in enumerate(order):
                jobs.append((dr, n, tt))
        for (dr, n, tt) in jobs:
            sl = dr * 6 + n
            flag = self.FL[:, dr * 3 + tt // 2: dr * 3 + tt // 2 + 1]
            self.gla_prep(dr, KVp[tt], LD[tt], kslot(sl), sl, False, flag=flag)
            cx.ts("dve", self.DM[:, sl, :], self.DEC[:, sl, :], -1.0, ALU.add, flag, ALU.mult)
            cx.ts("dve", self.DM[:, sl, :], self.DM[:, sl, :], 1.0, ALU.add)
        for dr in range(2):
            Sprev = self.S[dr]
            pp = [self.S[4], self.S[5]]
            for (d2, n, tt) in jobs:
                if d2 != dr:
                    continue
                sl = dr * 6 + n
                Snew = self.S[2 + dr] if n == 5 else pp[n % 2]
                self.gla_update(Sprev, Snew, kslot(sl), KVp[tt], self.DM[:, sl, :])
                Sprev = Snew

    def even_mixer(self):
        cx = self.cx
        PG = self.PG
        wfm = self.d_winfm.rearrange("(kc p) n -> p kc n", p=128)
        wtm = self.d_wintm.rearrange("(kc p) n -> p kc n", p=128)
        CAT = PG[8:16]
        QA, KD, SW, VT, PT = PG[16:20], PG[20:22], PG[22:24], PG[24:26], PG[26:29]
        RC, RS = PG[14], PG[15]
        allr = [(0, 512), (512, 256)]
        samp = [(512, 256)]
        self.load_rope(RC, RS)
        self.fm_linear(wfm, 0, 4, self.H, allr, lambda j, t0, tn, ps: self.ev_copy(QA[j][:, t0:t0 + tn], ps))
        self.fm_linear(wfm, 4 * 128, 2, self.H, allr, lambda j, t0, tn, ps: self.ev_copy(KD[j][:, t0:t0 + tn], ps))

        def swv(i):
            return SW[i // 3][:, (i % 3) * 256:(i % 3 + 1) * 256]
        self.fm_linear(wfm, 15 * 128, 4, self.H, samp, lambda j, t0, tn, ps: self.ev_copy(swv(j), ps))
        self.fm_linear(wfm, 19 * 128, 2, self.H, samp, lambda j, t0, tn, ps: self.ev_copy(swv(4 + j), ps))
        if self.cut <= 10:
            return
        for i in range(6):
            dst = (QA[i] if i < 4 else KD[i - 4])[:, 512:768]
            t1 = self.tmp()
            t2 = self.tmp()
            cx.tt("dve", t1[:, 0:256], dst.f(), RC[:, 0:256].f(), ALU.mult)
            cx.tt("dve", t2[:, 0:256], swv(i).f(), RS[:, 0:256].f(), ALU.mult)
            cx.tt("dve", dst, t1[:, 0:256], t2[:, 0:256], ALU.add)

        def vtv(tt):
            return VT[tt // 3][:, (tt % 3) * 256:(tt % 3 + 1) * 256]

        def ev_tm_v(c, bw, tt, ps):
            self.ev_copy(vtv(tt), ps)
            if tt < 4:
                for kv in range(2):
                    cx.dma("sp", self.o_kv[:, tt, 128 + kv * 64:128 + (kv + 1) * 64], vtv(tt)[:, kv * 128:kv * 128 + 64].f().ap,
                           self.dsem_out, reads=[vtv(tt)])

        def ev_tm_k(c, bw, tt, ps):
            tk = self.tmp()
            self.ev_copy(tk[:, 0:128], ps[:, 0:128])
            cx.dma("sp", self.o_kv[:, tt, 0:128], tk[:, 0:128].ap, self.dsem_out, reads=[tk])
        self.tm_linear(wtm, 0, 256, self.H, list(range(6)), ev_tm_v)
        self.tm_linear(wtm, 256, 256, self.H, list(range(4)), ev_tm_k)
        if self.cut <= 12:
            return

        self.ps_rr_n = 4
        ring = [PG[26], PG[27], PG[28], PG[29], PG[12], PG[13], PG[22], PG[23]]
        rn = [0]

        def ptile():
            t = ring[rn[0] % len(ring)]
            rn[0] += 1
            return t

        def finalize(c, c0, po, den):
            for e in range(2):
                rows = slice(64 * e, 64 * e + 64)
                t1 = self.tmp()
                cx.act(t1[rows, 0:256], den[e][rows, 0:256], AF.Ln, bias=self.esink[rows, c:c + 1])
                cx.act(t1[rows, 0:256], t1[rows, 0:256], AF.Exp, scale=-1.0)
                cx.tt("dve", CAT[c][rows, c0:c0 + 256], po[e][rows, 0:256], t1[rows, 0:256], ALU.mult)

        units = [(sq, c) for sq in range(2) for c in range(4)]

        def scores(u):
            sq, c = units[u]
            c0, kv = sq * 256, c // 2
            Pk = [ptile(), ptile()]
            for kt in range(2):
                for e in range(2):
                    rows = slice(64 * e, 64 * e + 64)
                    Sp = self.psum()
                    cx.mm(Sp[:, 0:256], KD[kv][rows, c0 + kt * 128:c0 + (kt + 1) * 128], QA[c][rows, c0:c0 + 256], True, True)
                    cx.act(Pk[kt][:, e * 256:(e + 1) * 256], Sp[:, 0:256], AF.Exp, scale=0.125)
            return Pk

        def pv(u, Pk):
            sq, c = units[u]
            c0, kv = sq * 256, c // 2
            bank = self.ps[4 + 2 * (u % 2)]
            bank2 = self.ps[5 + 2 * (u % 2)]
            po = [bank[:, 0:256], bank[:, 256:512]]
            den = [bank2[:, 0:256], bank2[:, 256:512]]
            for e in range(2):
                for kt in range(2):
                    vt = c0 // 128 + kt
                    cx.mm(po[e], vtv(vt)[:, kv * 128:(kv + 1) * 128], Pk[kt][:, e * 256:(e + 1) * 256], kt == 0, kt == 1)
            for e in range(2):
                for kt in range(2):
                    cx.mm(den[e], self.onesF.v(), Pk[kt][:, e * 256:(e + 1) * 256], kt == 0, kt == 1)
            finalize(c, c0, po, den)

        pend = scores(0)
        for u in range(len(units)):
            nxt = scores(u + 1) if u + 1 < len(units) else None
            pv(u, pend)
            pend = nxt

        c0 = 512
        for c in range(4):
            kv = c // 2
            po = [self.ps[4][:, 0:256], self.ps[5][:, 0:256]]
            den = [self.ps[6][:, 0:256], self.ps[7][:, 0:256]]
            Ps = []
            for kt in range(6):
                P = ptile()
                Ps.append(P)
                q0, qn = (0, 128) if kt == 2 else ((128, 128) if kt == 5 else (0, 256))
                for e in range(2):
                    rows = slice(64 * e, 64 * e + 64)
                    if kt < 2:
                        kl = self.CK[rows, kv, kt * 128:(kt + 1) * 128]
                    elif kt == 2:
                        kl = self.HK[rows, kv, 1, :]
                    elif kt == 5:
                        kl = self.HK[rows, kv, 0, :]
                    else:
                        kl = KD[kv][rows, c0 + (kt - 3) * 128:c0 + (kt - 2) * 128]
                    Sp = self.psum()
                    cx.mm(Sp[:, 0:qn], kl, QA[c][rows, c0 + q0:c0 + q0 + qn], True, True)
                    cx.act(P[:, e * 256 + q0:e * 256 + q0 + qn], Sp[:, 0:qn], AF.Exp, scale=0.125)
                Pv = P[:, 0:512].re("p (a b) -> p a b", a=2)
                lo, hi = Pv[:, :, 0:128], Pv[:, :, 128:256]
                if kt == 2:
                    cx.tt("pool", lo, lo.f(), self.bcast(self.tri[:, 2, :], 2), ALU.mult)
                elif kt == 3:
                    cx.tt("pool", hi, hi.f(), self.bcast(self.tri[:, 0, :], 2), ALU.mult)
                elif kt == 4:
                    cx.tt("pool", lo, lo.f(), self.bcast(self.tri[:, 1, :], 2), ALU.mult)
                elif kt == 5:
                    cx.tt("pool", hi, hi.f(), self.bcast(self.tri[:, 3, :], 2), ALU.mult)
            for kt in range(6):
                P = Ps[kt]
                q0, qn = (0, 128) if kt == 2 else ((128, 128) if kt == 5 else (0, 256))
                if kt < 2:
                    vl = self.CV[:, kt, kv, :]
                elif kt == 2:
                    vl = self.HV[:, 1, kv * 128:(kv + 1) * 128]
                elif kt == 5:
                    vl = self.HV[:, 0, kv * 128:(kv + 1) * 128]
                else:
                    vl = vtv(4 + kt - 3)[:, kv * 128:(kv + 1) * 128]
                for e in range(2):
                    pe_ = P[:, e * 256 + q0:e * 256 + q0 + qn]
                    cx.mm(V(po[e].tiles, po[e].ap[:, q0:q0 + qn]), vl, pe_, kt == 0, kt == 5)
                    cx.mm(V(den[e].tiles, den[e].ap[:, q0:q0 + qn]), self.onesF.v(), pe_, kt == 0, kt == 5)
            finalize(c, c0, po, den)
        self.ps_rr_n = 6

        if self.cut <= 14:
            return
        QB, KB, LR = PG[16:18], PG[18:20], PG[20]
        KVp, LD, GP = PG[21:23], PG[23:25], PG[25:29]
        self.fm_linear(wfm, 6 * 128, 2, self.H, allr, lambda j, t0, tn, ps: self.ev_copy(QB[j][:, t0:t0 + tn], ps))
        self.fm_linear(wfm, 8 * 128, 2, self.H, allr, lambda j, t0, tn, ps: self.ev_copy(KB[j][:, t0:t0 + tn], ps))
        self.fm_linear(wfm, 10 * 128, 1, self.H, allr, lambda j, t0, tn, ps: self.ev_copy(LR[:, t0:t0 + tn], ps))
        ZS = self.S[6]
        if self.cut <= 15:
            return
        for sg in range(3):
            tiles = [2 * sg, 2 * sg + 1]
            self.tm_linear(wtm, 512, 768, self.H, tiles, lambda c, bw, tt, ps: self.ev_copy(KVp[tt % 2][:, c:c + bw], ps))
            for ci in range(2):
                self.gla_ld(LR, tiles[ci], LD[ci])
            if self.cut == 151:
                return
            for (ci, dr) in [(0, 0), (1, 0), (1, 1), (0, 1)]:
                self.gla_prep(dr, KVp[ci], LD[ci], GP[ci * 2 + dr], ci * 2 + dr, True, QB, KB, tiles[ci] * 128)
            if self.cut == 152:
                return
            Sf0 = ZS if sg < 2 else self.S[2]
            Sb0 = ZS if sg < 2 else self.S[3]
            Sf1, Sb1 = self.S[4], self.S[5]
            self.gla_update(Sf0, Sf1, GP[0], KVp[0], self.DEC[:, 0, :])
            self.gla_update(Sb0, Sb1, GP[3], KVp[1], self.DEC[:, 3, :])
            if self.cut == 153:
                return
            ats0 = self.gla_out_a(GP[0], GP[1])
            ats1 = self.gla_out_a(GP[2], GP[3])
            self.gla_out(KVp[0], GP[0], GP[1], Sf0, Sb1, CAT[4:8], tiles[0] * 128, ats0)
            self.gla_out(KVp[1], GP[2], GP[3], Sf1, Sb0, CAT[4:8], tiles[1] * 128, ats1)
            if sg < 2:
                self.gla_update(Sf1, None, GP[2], KVp[1], self.DEC[:, 2, :], out_f32=self.FIN[0][:, sg, :, :])
                self.gla_update(Sb1, None, GP[1], KVp[0], self.DEC[:, 1, :], out_f32=self.FIN[1][:, sg, :, :])
            if self.cut == 155:
                return
        cx.dma("sp", self.o_sf, self.FIN[0].ap.rearrange("p a b c -> p a (b c)"), self.dsem_out, reads=[self.FIN[0]])
        cx.dma("sp", self.o_sb, self.FIN[1].ap.rearrange("p a b c -> p a (b c)"), self.dsem_out, reads=[self.FIN[1]])
        if self.cut <= 16:
            return

        RH = PG[16:20]
        for h in range(4):
            sqp = PG[20 + h]
            cx.act(sqp.v(), CAT[4 + h].v().f(), AF.Square)
            for (t0, tn) in allr:
                ss = self.psum()
                cx.mm(ss[:, 0:tn], self.onesF.v(), sqp[:, t0:t0 + tn], True, True)
                self.rsqrt(RH[h][:, t0:t0 + tn], ss[:, 0:tn], 1.0 / 128, tn)
                oh = CAT[4 + h][:, t0:t0 + tn]
                cx.stt("dve", oh, oh.f(), self.gg[:, h:h + 1], RH[h][:, t0:t0 + tn].f(), ALU.mult, ALU.mult)

        def ev_gate(j, t0, tn, ps):
            o = CAT[4 + j][:, t0:t0 + tn]
            sgt = self.tmp()
            cx.act(sgt[:, 0:tn], ps, AF.Silu)
            cx.tt("dve", o, o.f(), sgt[:, 0:tn], ALU.mult)
        self.fm_linear(wfm, 11 * 128, 4, self.H, allr, ev_gate)
        if self.cut <= 17:
            return
        self.out_linear(self.d_wout, CAT, 0)

    def odd_mixer(self):
        cx = self.cx
        PG = self.PG
        wi = self.d_cmwin.rearrange("(kc p) n -> p kc n", p=128)
        U, VTM, GBp, WSB = PG[8:16], PG[16:24], PG[24:27], PG[27:29]
        allr = [(0, 512), (512, 256)]

        def gbv(which, cb):
            li = which * 4 + cb
            return GBp[li // 3][:, (li % 3) * 256:(li % 3 + 1) * 256]
        for which in range(2):
            for cb in range(4):
                v = gbv(which, cb)
                cx.dma("sp", v.ap, self.d_cmgb[:, which * 1024 + cb * 256: which * 1024 + (cb + 1) * 256].bitcast(F32R),
                       self.dsem_c2, writes=[v])
        WS = WSB[0][:, 0:512].re("p (g t) -> p g t", g=4)
        BS = WSB[1][:, 0:512].re("p (g t) -> p g t", g=4)
        cx.dma("sp", WSB[0][:, 0:512].ap, self.d_cmws.bitcast(F32R), self.dsem_c2, writes=[WSB[0]])
        cx.dma("sp", WSB[1][:, 0:512].ap, self.d_cmbs.bitcast(F32R), self.dsem_c2, writes=[WSB[1]])
        cx.group_fix(self.dsem_c2, list(GBp) + list(WSB))

        def vblk(tt, cb):
            li = tt * 4 + cb
            return VTM[li // 3][:, (li % 3) * 256:(li % 3 + 1) * 256]
        self.tm_linear(wi, 1024, 1024, self.H, list(range(6)), lambda c, bw, tt, ps: cx.act(vblk(tt, c // 256), ps, AF.Gelu))
        for tt in range(6):
            st = self.lnst[tt]
            for cb in range(4):
                v = vblk(tt, cb).f()
                cx.op("dve", lambda e, v=v, cb=cb, st=st: e.bn_stats(out=st.ap[:, 16 + cb * 6:16 + (cb + 1) * 6], in_=v.ap),
                      reads=[v], writes=[st])
            cx.op("dve", lambda e, st=st: e.bn_aggr(out=st.ap[:, 10:12], in_=st.ap[:, 16:40]), reads=[st], writes=[st])
        allst = V([k for t in self.lnst for k in _tiles([t])], self.lnst_all)
        cx.act(allst[:, :, 15], allst[:, :, 11], AF.Ln, bias=self.epsb[:, 0:1], scale=1.0)
        cx.act(allst[:, :, 13], allst[:, :, 15], AF.Exp, scale=-0.5)
        cx.stt("dve", allst[:, :, 14], allst[:, :, 10], -1.0, allst[:, :, 13], ALU.mult, ALU.mult)
        prev = None
        for tt in range(6):
            st = self.lnst[tt]
            for cb in range(4):
                v = vblk(tt, cb)
                t1 = self.tmp()
                cx.ts("dve", t1[:, 0:256], v.f(), st[:, 13:14], ALU.mult, st[:, 14:15], ALU.add)
                cx.tt("pool", t1[:, 0:256], t1[:, 0:256], gbv(0, cb).f(), ALU.mult)
                if prev is not None:
                    cx.tt("dve", prev[0], prev[1][:, 0:256], gbv(1, prev[2]).f(), ALU.add)
                prev = (v, t1, cb)
        cx.tt("dve", prev[0], prev[1][:, 0:256], gbv(1, prev[2]).f(), ALU.add)
        self.fm_linear(wi, 0, 8, self.H, allr, lambda j, t0, tn, ps: cx.act(U[j][:, t0:t0 + tn], ps, AF.Gelu))
        for cc in range(8):
            g = cc // 2
            for tiles in ([0, 1, 2, 3], [4, 5]):
                n = len(tiles)
                ps = self.psum()
                for ti, tt in enumerate(tiles):
                    cx.mm(ps[:, ti * 128:(ti + 1) * 128], vblk(tt, cc // 2)[:, (cc % 2) * 128:(cc % 2 + 1) * 128], WS[:, g, :], True, True)
                t1 = self.tmp()
                cx.tt("dve", t1[:, 0:n * 128].re("p (a b) -> p a b", a=n), ps[:, 0:n * 128].re("p (a b) -> p a b", a=n),
                      self.bcast(BS[:, g, :].f(), n), ALU.add)
                uu = U[cc][:, tiles[0] * 128:(tiles[-1] + 1) * 128]
                cx.tt("pool", uu, t1[:, 0:n * 128], uu.f(), ALU.mult)
        self.out_linear(self.d_cmwout, U, 1)

    def store_out(self):
        cx = self.cx
        for k in range(KC):
            cx.dma("sp", self.o_y[:, k, :], self.X[k].ap, self.dsem_out, reads=[self.X[k]])


def build_program(stage):
    nc0 = bass.Bass("TRN2", target_bir_lowering=False)
    nc0.dge_precook = False
    plan = Prog(nc0, True, None, stage).build()
    nc = bass.Bass("TRN2", target_bir_lowering=False)
    nc.dge_precook = False
    p = Prog(nc, False, plan, stage)
    p.build()
    build_program.last_log = p.cx.log
    return nc


def fm(x):
    t, d = x.shape
    return np.ascontiguousarray(x.reshape(t, d // 128, 128).transpose(2, 1, 0))


def fm_vec(v):
    sh = v.shape
    v2 = v.reshape(-1, sh[-1] // 128, 128)
    return np.ascontiguousarray(v2.transpose(2, 0, 1)).reshape(128, -1)


def _swap_idx():
    d = np.arange(64)
    a, bb, f = d // 32, (d % 32) // 16, d % 16
    return a * 32 + (1 - bb) * 16 + f


def _rope_tables(pos):
    f32 = np.float32
    half, nf = 32, 16
    inv = (f32(10000.0) ** (-np.arange(nf, dtype=f32) * f32(2.0) / f32(half))).astype(f32)
    row = (pos // 64).astype(f32)
    col = (pos % 64).astype(f32)
    d = np.arange(64)
    a, bb, f = d // 32, (d % 32) // 16, d % 16
    p = np.where(a[:, None] == 0, row[None, :], col[None, :]).astype(f32)
    ang = (p * inv[f][:, None]).astype(f32)
    cos = np.cos(ang).astype(f32)
    sin = np.sin(ang).astype(f32)
    sgn = np.where(bb == 0, -1.0, 1.0).astype(f32)[:, None]
    sins = (sin * sgn).astype(f32)
    return np.concatenate([cos, cos], 0), np.concatenate([sins, sins], 0)


def make_in_maps(inp):
    f32 = np.float32
    g = lambda k: np.asarray(inp[k], f32)
    x_prompt, x_sample, c, c_ctx = g("x_prompt"), g("x_sample"), g("c"), g("c_ctx")
    w_in = g("ev_w_in")[0]
    sw = _swap_idx()
    qa, ka, va = w_in[:, 0:512], w_in[:, 512:640], w_in[:, 640:768]
    qb, kb, vb, gb = w_in[:, 768:1024], w_in[:, 1024:1280], w_in[:, 1280:1792], w_in[:, 1792:2304]
    lrf, lrb = w_in[:, 2304:2320], w_in[:, 2320:2336]
    ka_dup = np.concatenate([ka[:, 0:64], ka[:, 0:64], ka[:, 64:128], ka[:, 64:128]], 1)
    qa_sw = qa.reshape(D, 8, 64)[:, :, sw].reshape(D, 512)
    ka_sw = ka.reshape(D, 2, 64)[:, :, sw].reshape(D, 128)
    ka_sw_dup = np.concatenate([ka_sw[:, 0:64], ka_sw[:, 0:64], ka_sw[:, 64:128], ka_sw[:, 64:128]], 1)
    lr = np.concatenate([lrf, lrf, lrb, lrb, lrf, lrf, lrb, lrb], 1)
    w_in_fm = np.ascontiguousarray(np.concatenate([qa, ka_dup, qb, kb, lr, gb, qa_sw, ka_sw_dup], 1))
    assert w_in_fm.shape == (D, 2688)
    va_dup = np.concatenate([va[:, 0:64], va[:, 0:64], va[:, 64:128], va[:, 64:128]], 1)
    w_in_tm = np.ascontiguousarray(np.concatenate([va_dup, ka, ka, kb, vb], 1))
    assert w_in_tm.shape == (D, 1280)
    jj, ii = np.meshgrid(np.arange(128), np.arange(128), indexing="ij")
    gmask = np.stack([(jj <= ii), (jj > ii), (jj >= ii), (jj < ii)], 1).astype(f32).reshape(128, 512)
    tri_prev = (jj >= ii).astype(f32)
    tri_next = (jj <= ii).astype(f32)
    wa = np.zeros((64, 256), f32)
    waf, wab = g("gla_wa_f")[0], g("gla_wa_b")[0]
    wa[0:16], wa[16:32], wa[32:48], wa[48:64] = waf, waf, wab, wab
    ba = np.zeros((64, 256), f32)
    ba[0:32] = g("gla_ba_f")[0][None, :]
    ba[32:64] = g("gla_ba_b")[0][None, :]
    sink = g("ev_sink")[0]
    sinkT = np.ascontiguousarray(np.stack([np.where(np.arange(128) < 64, sink[2 * cc], sink[2 * cc + 1]) for cc in range(4)], 1)).astype(f32)
    cm_gb = np.ascontiguousarray(np.broadcast_to(np.concatenate([g("cm_v_gain")[0], g("cm_v_bias")[0]])[None, :], (128, 2048))).astype(f32)
    cm_ws = np.ascontiguousarray(g("cm_w_s")[0].transpose(2, 0, 1).reshape(128, 512))
    cm_bs = np.ascontiguousarray(np.broadcast_to(g("cm_b_s")[0].reshape(1, 512), (128, 512))).astype(f32)
    shared = {
        "w_mod": np.ascontiguousarray(g("w_mod")),
        "b_mod": fm_vec(g("b_mod")),
        "npre": fm_vec(g("norm_pre")),
        "npost": fm_vec(g("norm_post")),
        "ffn_w_gate": np.ascontiguousarray(g("ffn_w_gate")),
        "ffn_w_up": np.ascontiguousarray(g("ffn_w_up")),
        "ffn_w_down": np.ascontiguousarray(g("ffn_w_down")),
        "ident": np.eye(128, dtype=f32),
        "w_in_fm": w_in_fm, "w_in_tm": w_in_tm,
        "ev_w_out": np.ascontiguousarray(g("ev_w_out")[0]),
        "gmask": gmask,
        "sinkT": sinkT,
        "gla_g": fm_vec(g("gla_norm")),
        "gla_wa": wa, "gla_ba": np.ascontiguousarray(ba),
        "cm_w_in": np.ascontiguousarray(g("cm_w_in")[0]),
        "cm_w_out": np.ascontiguousarray(g("cm_w_out")[0]),
        "cm_gb": cm_gb, "cm_ws": cm_ws, "cm_bs": cm_bs,
    }
    cache_k, cache_v = g("cache_k"), g("cache_v")
    sf, sbw = g("state_gla_fwd"), g("state_gla_bwd")
    maps = []
    for i in range(NCORE):
        b, q = i // 4, i % 4
        own = x_sample[b, 256 * q:256 * (q + 1)]
        main = np.concatenate([x_prompt[2 * i], x_prompt[2 * i + 1], own], axis=0)
        oth = np.concatenate([x_sample[b, 256 * ((q + r) % 4):256 * ((q + r) % 4) + 256] for r in (1, 2, 3)], axis=0)
        cond = np.stack([c_ctx, c[b]], axis=0)
        m = dict(shared)
        m["xm"] = fm(main)
        m["xo"] = fm(oth)
        m["cond"] = np.ascontiguousarray(cond.reshape(2, KC, 128).transpose(2, 1, 0))
        pos = np.concatenate([256 * q + np.arange(256), (256 * (q + 1) + np.arange(128)) % 1024,
                              (256 * q - 128 + np.arange(128)) % 1024])
        rc, rs = _rope_tables(pos)
        m["rope_c"], m["rope_s"] = np.ascontiguousarray(rc), np.ascontiguousarray(rs)
        vp, vn = f32(q > 0), f32(q < 3)
        m["tri"] = np.ascontiguousarray(np.stack([tri_prev, tri_next, tri_prev * vp, tri_next * vn], 1).reshape(128, 512))
        fl = np.zeros((128, 6), f32)
        for r in (1, 2, 3):
            fl[:, r - 1] = f32(r >= 4 - q)
            fl[:, 3 + r - 1] = f32(r <= 3 - q)
        m["flags"] = fl
        ck = cache_k[b, 0]
        ckT = ck.transpose(2, 1, 0)
        m["ctx_k"] = np.ascontiguousarray(np.concatenate([ckT, ckT], 0).reshape(128, 512))
        cv = cache_v[b, 0].reshape(2, 128, 2, 64).transpose(1, 0, 2, 3)
        m["ctx_v"] = np.ascontiguousarray(np.concatenate([cv, cv], 3).reshape(128, 512))
        for nm, stt in (("s0f", sf), ("s0b", sbw)):
            s0 = stt[b, 0]
            m[nm] = np.ascontiguousarray(s0.reshape(2, 2, 64, 128).transpose(1, 2, 0, 3).reshape(128, 256))
        maps.append(m)
    return maps


def assemble(results):
    f32 = np.float32
    y_prompt = np.zeros((16, 256, D), f32)
    y_sample = np.zeros((2, 1024, D), f32)
    nk = np.zeros((16, 1, 256, 2, 64), f32)
    nv = np.zeros((16, 1, 256, 2, 64), f32)
    nsf = np.zeros((16, 1, 4, 64, 128), f32)
    nsb = np.zeros((16, 1, 4, 64, 128), f32)
    for i, r in enumerate(results):
        b, q = i // 4, i % 4
        y = r["ym"].transpose(2, 1, 0).reshape(T, D)
        y_prompt[2 * i] = y[0:256]
        y_prompt[2 * i + 1] = y[256:512]
        y_sample[b, 256 * q:256 * (q + 1)] = y[512:768]
        kvo = r["kv_out"].transpose(1, 0, 2).reshape(512, 256)
        for sq in range(2):
            blk = kvo[sq * 256:(sq + 1) * 256]
            nk[2 * i + sq, 0] = blk[:, 0:128].reshape(256, 2, 64)
            nv[2 * i + sq, 0] = blk[:, 128:256].reshape(256, 2, 64)
        for arr, key in ((nsf, "sf_out"), (nsb, "sb_out")):
            o = r[key].reshape(128, 2, 2, 128)
            for sq in range(2):
                arr[2 * i + sq, 0] = o[:, sq].reshape(2, 64, 2, 128).transpose(2, 0, 1, 3).reshape(4, 64, 128)
    return (y_prompt, y_sample, nk, nv, nsf, nsb)


_NC_CACHE = {}


def kernel(**inputs):
    stage = int(os.environ.get("MK_STAGE", "99"))
    if stage not in _NC_CACHE:
        _NC_CACHE[stage] = build_program(stage)
    nc = _NC_CACHE[stage]
    maps = make_in_maps(inputs)
    res = run_bass_kernel_spmd(nc, maps, core_ids=list(range(NCORE)))
    kernel.last = res.results
    if stage < 99:
        return res.results
    return assemble(res.results)
```

```python
import os
import numpy as np
from contextlib import ExitStack
import concourse.bass as bass
import concourse.mybir as mybir
from concourse.bass_utils import run_bass_kernel_spmd

F32 = mybir.dt.float32
F32R = mybir.dt.float32r
AF = mybir.ActivationFunctionType
ALU = mybir.AluOpType
SAME_ENGINE_SYNC = True

D = 1024
KC = 8
T = 768
FF = 2816
FC = 22
EPS = 1e-6
NCORE = 8
GROUPS_MAIN = [(0, 512, 0), (512, 256, 1)]
GROUPS_OTH = [(0, 512, 1), (512, 256, 1)]


class Trk:
    __slots__ = ("writer", "readers")

    def __init__(self):
        self.writer = None
        self.readers = {}


class Tile:
    __slots__ = ("ap", "trks", "split", "name")

    def __init__(self, ap, name="", split=None):
        self.ap = ap
        self.name = name
        self.split = split
        self.trks = [Trk()] if split is None else [Trk(), Trk()]

    def _sel(self, idx):
        if self.split is None:
            return self.trks
        if isinstance(idx, tuple) and len(idx) >= 2 and isinstance(idx[1], slice):
            a, b = idx[1].start, idx[1].stop
            if a is not None and b is not None:
                if b <= self.split:
                    return [self.trks[0]]
                if a >= self.split:
                    return [self.trks[1]]
        return self.trks

    def __getitem__(self, idx):
        return V(self._sel(idx), self.ap[idx])

    def v(self):
        return V(self.trks, self.ap)


class V:
    __slots__ = ("tiles", "ap")

    def __init__(self, tiles, ap):
        self.tiles = tiles
        self.ap = ap

    def __getitem__(self, idx):
        return V(self.tiles, self.ap[idx])

    def r(self):
        return V(self.tiles, self.ap.bitcast(F32R))

    def f(self):
        return V(self.tiles, self.ap.bitcast(F32))

    def re(self, s, **kw):
        return V(self.tiles, self.ap.rearrange(s, **kw))

    def bc(self, shape):
        return V(self.tiles, self.ap.to_broadcast(shape))


def _tiles(vs):
    out = []
    for v in vs:
        if v is None:
            continue
        if isinstance(v, Tile):
            out.extend(v.trks)
        else:
            out.extend(v.tiles)
    return out


class Ctx:
    def __init__(self, nc, dry=False):
        self.nc = nc
        self.dry = dry
        self.engs = {}
        self.sems = {}
        self.count = {}
        self.waited = {}
        self.stack = None
        self.n_dma_sem = 0
        self.n_ps = 0
        self.n_tp = 0
        self.log = {}

    def setup(self, stack):
        self.stack = stack
        nc = self.nc
        self.engs = {"pe": nc.tensor, "act": nc.scalar, "dve": nc.vector,
                     "pool": nc.gpsimd, "sp": nc.sync}
        for k in self.engs:
            self.sems[k] = stack.enter_context(nc.semaphore("s_" + k))
            self.count[k] = 0
            self.waited[k] = {}

    def new_dma_sem(self, name):
        key = "dma_%s_%d" % (name, self.n_dma_sem)
        self.n_dma_sem += 1
        self.sems[key] = self.stack.enter_context(self.nc.semaphore(key))
        self.count[key] = 0
        return key

    def _deps(self, ek, reads, writes):
        deps = {}

        def add(w):
            if w is None:
                return
            k, c = w
            if deps.get(k, 0) < c:
                deps[k] = c
        for t in reads:
            add(t.writer)
        for t in writes:
            add(t.writer)
            for k, c in t.readers.items():
                add((k, c))
        eng = self.engs[ek]
        for k, c in deps.items():
            if k == ek and (ek == "pe" or not SAME_ENGINE_SYNC):
                continue
            if k.startswith("dma_"):
                c = self.count[k]
            if self.waited[ek].get(k, 0) >= c:
                continue
            eng.wait_ge(self.sems[k], c)
            self.waited[ek][k] = c
            self.log.setdefault(ek, []).append(("w", k, c))

    def op(self, ek, fn, reads=(), writes=(), inc=True):
        rt = _tiles(reads)
        wt = _tiles(writes)
        if self.dry:
            return
        self._deps(ek, rt, wt)
        ins = fn(self.engs[ek])
        idx = self.count[ek] + 1
        if inc:
            ins.then_inc(self.sems[ek], 1)
            self.count[ek] = idx
        self.log.setdefault(ek, []).append(("i", ek if inc else None, 1))
        for t in wt:
            t.writer = (ek, idx)
            t.readers = {}
        for t in rt:
            if t.readers.get(ek, 0) < idx:
                t.readers[ek] = idx

    def dma(self, qk, out, in_, semkey, reads=(), writes=()):
        rt = _tiles(reads)
        wt = _tiles(writes)
        if self.dry:
            return
        self._deps(qk, rt, wt)
        self.count[semkey] += 16
        c = self.count[semkey]
        self.engs[qk].dma_start(out=out, in_=in_).then_inc(self.sems[semkey], 16)
        self.log.setdefault(qk, []).append(("i", semkey, 16))
        for t in wt:
            t.writer = (semkey, c)
            t.readers = {}
        for t in rt:
            t.readers[semkey] = c

    def group_fix(self, semkey, tiles):
        if self.dry:
            return
        for t in _tiles(tiles):
            t.writer = (semkey, self.count[semkey])

    def finish(self, ek, semkeys):
        if self.dry:
            return
        for k in semkeys:
            if self.count[k] > 0:
                self.engs[ek].wait_ge(self.sems[k], self.count[k])

    def mm(self, out, lhsT, rhs, start, stop, inc=None):
        self.op("pe", lambda e: e.matmul(out.ap, lhsT=lhsT.ap, rhs=rhs.ap, start=start, stop=stop),
                reads=[lhsT, rhs], writes=[out], inc=(True if inc is None else inc))

    def act(self, out, in_, func, bias=None, scale=1.0, eng="act"):
        rd = [in_]
        kw = {}
        if bias is not None:
            if isinstance(bias, V):
                rd.append(bias)
                kw["bias"] = bias.ap
            else:
                kw["bias"] = bias
        if isinstance(scale, V):
            rd.append(scale)
            kw["scale"] = scale.ap
        else:
            kw["scale"] = scale
        self.op("act", lambda e: e.activation(out=out.ap, in_=in_.ap, func=func, **kw), reads=rd, writes=[out])

    def tt(self, eng, out, a, b, op):
        self.op(eng, lambda e: e.tensor_tensor(out=out.ap, in0=a.ap, in1=b.ap, op=op), reads=[a, b], writes=[out])

    def ts(self, eng, out, a, s1, op0, s2=None, op1=None):
        rd = [a]
        if isinstance(s1, V):
            rd.append(s1)
        if isinstance(s2, V):
            rd.append(s2)
        a1 = s1.ap if isinstance(s1, V) else s1
        a2 = s2.ap if isinstance(s2, V) else s2
        if op1 is None:
            self.op(eng, lambda e: e.tensor_scalar(out=out.ap, in0=a.ap, scalar1=a1, scalar2=None, op0=op0),
                    reads=rd, writes=[out])
        else:
            self.op(eng, lambda e: e.tensor_scalar(out=out.ap, in0=a.ap, scalar1=a1, scalar2=a2, op0=op0, op1=op1),
                    reads=rd, writes=[out])

    def stt(self, eng, out, a, s, b, op0, op1):
        rd = [a, b]
        if isinstance(s, V):
            rd.append(s)
        sa = s.ap if isinstance(s, V) else s
        self.op(eng, lambda e: e.scalar_tensor_tensor(out=out.ap, in0=a.ap, scalar=sa, in1=b.ap, op0=op0, op1=op1),
                reads=rd, writes=[out])

    def copy(self, eng, out, in_):
        if eng == "act":
            self.op("act", lambda e: e.copy(out=out.ap, in_=in_.ap), reads=[in_], writes=[out])
        else:
            self.op(eng, lambda e: e.tensor_copy(out=out.ap, in_=in_.ap), reads=[in_], writes=[out])

    def recip(self, out, in_):
        self.op("dve", lambda e: e.reciprocal(out=out.ap, in_=in_.ap), reads=[in_], writes=[out])

    def memset(self, eng, out, val):
        self.op(eng, lambda e: e.memset(out.ap, val), writes=[out])


class WStream:
    NS = 5
    SLOT = 2048

    def __init__(self, cx, nc, stack, plan):
        self.cx = cx
        self.plan = plan if plan is not None else []
        self.record = plan is None
        self.cur = 0
        self.issued = 0
        self.hold = 1
        self.slots = []
        self.semk = []
        for i in range(self.NS):
            t = stack.enter_context(nc.sbuf_tensor("wslot%d" % i, [128, self.SLOT], F32R))
            self.slots.append(Tile(t[:], "wslot%d" % i))
            self.semk.append(cx.new_dma_sem("w%d" % i))

    def _view(self, i, shape):
        s = self.slots[i % self.NS]
        n = 1
        for d in shape[1:]:
            n *= d
        assert n <= self.SLOT, shape
        v = s[0:shape[0], 0:n]
        if len(shape) == 3:
            v = v.re("p (a b) -> p a b", a=shape[1])
        return v

    def get(self, dram_ap, shape):
        i = self.cur
        self.cur += 1
        if self.record:
            self.plan.append((dram_ap, tuple(shape)))
            return self._view(i, shape)
        assert self.plan[i][1] == tuple(shape), (i, self.plan[i][1], shape)
        lim = min(len(self.plan), i + self.NS - self.hold + 1)
        while self.issued < lim:
            j = self.issued
            ap_j, shp_j = self.plan[j]
            v = self._view(j, shp_j)
            self.cx.dma("sp", v.ap, ap_j.bitcast(F32R), self.semk[j % self.NS], writes=[v])
            self.issued += 1
        return self._view(i, shape)


class Prog:
    def __init__(self, nc, dry, plan, stage):
        self.nc = nc
        self.dry = dry
        self.plan = plan
        self.stage = stage
        self.cut = int(os.environ.get('MK_CUT', '999'))

    def dram_in(self, name, shape):
        return self.nc.dram_tensor(name, list(shape), F32, kind="ExternalInput").ap()

    def dram_out(self, name, shape):
        return self.nc.dram_tensor(name, list(shape), F32, kind="ExternalOutput").ap()

    def sb(self, name, shape, dt=F32):
        return self.st.enter_context(self.nc.sbuf_tensor("sb_" + name, list(shape), dt))

    def psum(self):
        t = self.ps[self.cx.n_ps % self.ps_rr_n]
        self.cx.n_ps += 1
        return t

    def psum_stat(self, gi):
        return self.ps[6 + gi]

    def tmp(self):
        t = self.tp[self.cx.n_tp % len(self.tp)]
        self.cx.n_tp += 1
        return t

    def tmpr(self):
        t = self.tpr[self.n_tpr % len(self.tpr)]
        self.n_tpr += 1
        return t

    def build(self):
        nc = self.nc
        with ExitStack() as st:
            self.st = st
            cx = self.cx = Ctx(nc, self.dry)
            cx.setup(st)
            self.W = WStream(cx, nc, st, self.plan)
            self.declare_io()
            self.alloc()
            self.load_consts()
            self.mod_pending = [(l, nb) for l in range(2) for nb in range(36)]
            self.prepass()
            G = GROUPS_MAIN
            self.load_x(self.d_xm)
            self.boundary(None, (0, 0), G)
            self.ffn(0, 0, G)
            self.boundary((0, 0), (0, 1), G)
            self.even_mixer()
            self.boundary((0, 1), (0, 2), G)
            self.ffn_feed_l1 = True
            self.ffn(0, 1, G)
            self.boundary((0, 2), (1, 0), G)
            self.ffn(1, 0, G)
            self.boundary((1, 0), (1, 1), G)
            self.odd_mixer()
            self.boundary((1, 1), (1, 2), G)
            self.ffn(1, 1, G)
            self.boundary((1, 2), None, G)
            self.store_out()
            cx.finish("sp", [self.dsem_out])
        return self.W.plan

    def declare_io(self):
        di = self.dram_in
        self.d_xm = di("xm", [128, KC, T])
        self.d_xo = di("xo", [128, KC, T])
        self.d_cond = di("cond", [128, KC, 2])
        self.d_wmod = di("w_mod", [2, D, 9 * D])
        self.d_bmod = di("b_mod", [128, 2 * 72])
        self.d_npre = di("npre", [128, 2 * 3 * KC])
        self.d_npost = di("npost", [128, 2 * 3 * KC])
        self.d_wg = di("ffn_w_gate", [2, 2, D, FF])
        self.d_wu = di("ffn_w_up", [2, 2, D, FF])
        self.d_wd = di("ffn_w_down", [2, 2, FF, D])
        self.d_ident = di("ident", [128, 128])
        self.d_winfm = di("w_in_fm", [D, 2688])
        self.d_wintm = di("w_in_tm", [D, 1280])
        self.d_wout = di("ev_w_out", [D, D])
        self.d_ropec = di("rope_c", [128, 512])
        self.d_ropes = di("rope_s", [128, 512])
        self.d_tri = di("tri", [128, 4 * 128])
        self.d_mm = di("gmask", [128, 4 * 128])
        self.d_fl = di("flags", [128, 6])
        self.d_ck = di("ctx_k", [128, 2 * 256])
        self.d_cv = di("ctx_v", [128, 2 * 2 * 128])
        self.d_s0f = di("s0f", [128, 256])
        self.d_s0b = di("s0b", [128, 256])
        self.d_sink = di("sinkT", [128, 4])
        self.d_gg = di("gla_g", [128, 4])
        self.d_wa = di("gla_wa", [64, 256])
        self.d_ba = di("gla_ba", [64, 256])
        self.d_cmwin = di("cm_w_in", [D, 2 * D])
        self.d_cmwout = di("cm_w_out", [D, D])
        self.d_cmgb = di("cm_gb", [128, 2 * D])
        self.d_cmws = di("cm_ws", [128, 512])
        self.d_cmbs = di("cm_bs", [128, 512])
        self.o_kv = self.dram_out("kv_out", [128, 4, 256])
        self.o_sf = self.dram_out("sf_out", [128, 2, 256])
        self.o_sb = self.dram_out("sb_out", [128, 2, 256])
        self.o_y = self.dram_out("ym", [128, KC, T])

    def alloc(self):
        nc, st, cx = self.nc, self.st, self.cx
        xt = self.sb("xT", [128, KC, T], F32)
        self.X = [Tile(xt[:, k, :], "x%d" % k, split=512) for k in range(KC)]
        ar = self.sb("arena", [128, 30, T], F32R)
        self.PG = [Tile(ar[:, i, :], "pg%d" % i, split=512) for i in range(30)]
        self.H = self.PG[0:8]
        self.ACT = self.PG[8:30]
        self.tp = [Tile(self.sb("tp%d" % i, [128, 512], F32)[:], "tp%d" % i) for i in range(3)]
        self.tpr = [Tile(self.sb("tpr%d" % i, [128, 512], F32R)[:], "tpr%d" % i) for i in range(4)]
        self.n_tpr = 0
        self.ps = [Tile(st.enter_context(nc.psum_tensor("ps%d" % i, [128, 512], F32))[:], "ps%d" % i) for i in range(8)]
        self.rstd = Tile(self.sb("rstd", [128, T], F32)[:], "rstd", split=512)
        self.cond = Tile(self.sb("cond", [128, KC, 2], F32)[:], "cond")
        self.scT = Tile(self.sb("scT", [128, KC, 2], F32R)[:], "scT")
        self.bm = Tile(self.sb("bm", [128, 2, 72], F32)[:], "bm")
        self.npre = Tile(self.sb("npre", [128, 2, 3, KC], F32)[:], "npre")
        self.npost = Tile(self.sb("npost", [128, 2, 3, KC], F32)[:], "npost")
        self.ident = Tile(self.sb("ident", [128, 128], F32)[:], "ident")
        self.onesF = Tile(self.sb("onesF", [128, 128], F32R)[:], "onesF")
        self.epsb = Tile(self.sb("epsb", [128, 1], F32)[:], "epsb")
        self.modT = Tile(self.sb("modT", [128, 2, 72, 2], F32)[:], "modT")
        mrt = self.sb("modrow", [2, 2, 256], F32)
        self.modrow = [Tile(mrt[:, i, :], "modrow%d" % i) for i in range(2)]
        self.mod_unfinished = None
        self.ffn_feed_l1 = False
        self.defer_B = None
        self.y_pending = []
        self.tab = Tile(self.sb("tab", [128, 2, 3, 2, 3, KC], F32)[:], "tab")
        self.tri = Tile(self.sb("tri", [128, 4, 128], F32)[:], "tri")
        self.MM = Tile(self.sb("gmask", [128, 4, 128], F32R)[:], "gmask")
        self.FL = Tile(self.sb("flags", [128, 6], F32)[:], "flags")
        self.CK = Tile(self.sb("ctxk", [128, 2, 256], F32R)[:], "ctxk")
        self.CV = Tile(self.sb("ctxv", [128, 2, 2, 128], F32R)[:], "ctxv")
        self.HK = Tile(self.sb("hk", [128, 2, 2, 128], F32R)[:], "hk")
        self.HV = Tile(self.sb("hv", [128, 2, 256], F32R)[:], "hv")
        self.S = [Tile(self.sb("st%d" % i, [128, 2, 128], F32R)[:], "st%d" % i) for i in range(7)]
        self.FIN = [Tile(self.sb("fin%d" % i, [128, 2, 2, 128], F32)[:], "fin%d" % i) for i in range(2)]
        self.sinkT = Tile(self.sb("sinkT", [128, 4], F32)[:], "sinkT")
        self.esink = Tile(self.sb("esink", [128, 4], F32)[:], "esink")
        self.gg = Tile(self.sb("gg", [128, 4], F32)[:], "gg")
        self.WA = Tile(self.sb("wa", [64, 256], F32R)[:], "wa")
        self.BA = Tile(self.sb("ba", [64, 256], F32R)[:], "ba")
        self.DEC = Tile(self.sb("dec", [128, 12, 2], F32)[:], "dec")
        self.DM = Tile(self.sb("dm", [128, 12, 2], F32)[:], "dm")
        lt = self.sb("lnst", [128, 6, 40], F32)
        self.lnst = [Tile(lt[:, i, :], "lnst%d" % i) for i in range(6)]
        self.lnst_all = lt[:]
        self.ps_rr_n = 6
        self.n_ev = 0
        self.dsem_c = cx.new_dma_sem("const")
        self.dsem_c2 = cx.new_dma_sem("const2")
        self.dsem_cs = cx.new_dma_sem("const_sp")
        self.dsem_x = cx.new_dma_sem("x")
        self.dsem_out = cx.new_dma_sem("out")

    def load_consts(self):
        cx = self.cx
        q = "pool"
        dsem_cond = cx.new_dma_sem("cond")
        cx.dma("sp", self.cond.ap, self.d_cond, dsem_cond, writes=[self.cond])
        cx.act(self.scT.v(), self.cond.v(), AF.Silu)
        cx.dma(q, self.npre.ap, self.d_npre.rearrange("p (l i k) -> p l i k", l=2, i=3), self.dsem_c, writes=[self.npre])
        cx.dma(q, self.npost.ap, self.d_npost.rearrange("p (l i k) -> p l i k", l=2, i=3), self.dsem_c, writes=[self.npost])
        cx.dma(q, self.ident.ap, self.d_ident, self.dsem_c, writes=[self.ident])
        cx.dma(q, self.bm.ap, self.d_bmod.rearrange("p (l c) -> p l c", l=2), self.dsem_c, writes=[self.bm])
        for (t, d, shp) in [(self.tri, self.d_tri, "p (a b) -> p a b"), (self.FL, self.d_fl, None), (self.sinkT, self.d_sink, None),
                            (self.gg, self.d_gg, None)]:
            src = d if shp is None else d.rearrange(shp, a=4)
            cx.dma(q, t.ap, src, self.dsem_c, writes=[t])
        cx.dma("sp", self.MM.ap, self.d_mm.rearrange("p (a b) -> p a b", a=4).bitcast(F32R), self.dsem_cs, writes=[self.MM])
        cx.dma("sp", self.CK.ap, self.d_ck.rearrange("p (a b) -> p a b", a=2).bitcast(F32R), self.dsem_cs, writes=[self.CK])
        cx.dma("sp", self.CV.ap, self.d_cv.rearrange("p (a b c) -> p a b c", a=2, b=2).bitcast(F32R), self.dsem_cs, writes=[self.CV])
        cx.dma("sp", self.S[0].ap, self.d_s0f.rearrange("p (a b) -> p a b", a=2).bitcast(F32R), self.dsem_cs, writes=[self.S[0]])
        cx.dma("sp", self.S[1].ap, self.d_s0b.rearrange("p (a b) -> p a b", a=2).bitcast(F32R), self.dsem_cs, writes=[self.S[1]])
        cx.dma("sp", self.WA.ap, self.d_wa.bitcast(F32R), self.dsem_cs, writes=[self.WA])
        cx.dma("sp", self.BA.ap, self.d_ba.bitcast(F32R), self.dsem_cs, writes=[self.BA])
        cx.group_fix(self.dsem_c, [self.npre, self.npost, self.ident, self.bm, self.tri, self.FL, self.sinkT, self.gg])
        cx.group_fix(self.dsem_cs, [self.MM, self.CK, self.CV, self.S[0], self.S[1], self.WA, self.BA])
        cx.memset("dve", self.FIN[0].v(), 0.0)
        cx.copy("dve", self.S[6].v(), self.FIN[0][:, 0, :, :])
        cx.act(self.esink.v(), self.sinkT.v(), AF.Exp)
        cx.memset("dve", self.tp[0][:, 0:128], 1.0)
        cx.copy("dve", self.onesF.v(), self.tp[0][:, 0:128])
        cx.memset("dve", self.epsb.v(), EPS)

    def mod_step(self, n=1):
        for _ in range(n):
            if not self.mod_pending:
                return
            l, nb = self.mod_pending.pop(0)
            self.mod_block(l, nb)
            if nb % 12 == 11:
                self.mod_tab(l, nb // 12)

    def mod_step_ffn(self):
        if self.mod_pending and (self.mod_pending[0][0] == 0 or self.ffn_feed_l1):
            self.mod_step()

    def mod_need(self, l, i):
        while self.mod_pending and (self.mod_pending[0][0] < l or
                                    (self.mod_pending[0][0] == l and self.mod_pending[0][1] < 12 * (i + 1))):
            self.mod_step()

    def mod_flush(self):
        while self.mod_pending:
            self.mod_step()

    def mod_block(self, l, nb):
        cx = self.cx
        Wt = self.W.get(self.d_wmod[l].rearrange("(kc p) n -> p kc n", p=128)[:, :, nb * 256:(nb + 1) * 256], [128, KC, 256])
        ps = self.psum()
        for kc in range(KC):
            cx.mm(ps[0:2, 0:256], self.scT[:, kc, :], Wt[:, kc, :], start=(kc == 0), stop=(kc == KC - 1), inc=(kc == KC - 1))
        mr = self.modrow[nb % 2]
        cx.copy("act", mr.v(), ps[0:2, 0:256])
        self.mod_finish()
        self.mod_unfinished = (l, nb)
        if nb % 12 == 11:
            self.mod_finish()

    def mod_finish(self):
        cx = self.cx
        if self.mod_unfinished is None:
            return
        l, nb = self.mod_unfinished
        self.mod_unfinished = None
        mr = self.modrow[nb % 2]
        pt = self.psum()
        for hf in range(2):
            cx.op("pe", lambda e, hf=hf, pt=pt: e.transpose(out=pt.ap[:, hf * 2:hf * 2 + 2],
                                                        in_=mr.ap[0:2, hf * 128:(hf + 1) * 128],
                                                        identity=self.ident.ap[0:2, 0:2]),
                  reads=[mr, self.ident], writes=[pt], inc=(hf == 1))
        for r in range(2):
            cx.tt("dve", self.modT[:, l, nb * 2:nb * 2 + 2, r], pt[:, 0:4].re("p (a b) -> p a b", a=2)[:, :, r],
                  self.bm[:, l, nb * 2:nb * 2 + 2], ALU.add)

    def mod_tab(self, l, i):
        cx = self.cx
        if True:
            coef = 1.0 if i == 1 else 0.5
            for r in range(2):
                sh = self.modT[:, l, i * 24 + 0:i * 24 + 8, r]
                sc = self.modT[:, l, i * 24 + 8:i * 24 + 16, r]
                gt = self.modT[:, l, i * 24 + 16:i * 24 + 24, r]
                cx.stt("dve", self.tab[:, l, i, r, 0, :], sc, 1.0, self.npre[:, l, i, :], ALU.add, ALU.mult)
                cx.copy("dve", self.tab[:, l, i, r, 1, :], sh)
                cx.stt("dve", self.tab[:, l, i, r, 2, :], gt, coef, self.npost[:, l, i, :], ALU.mult, ALU.mult)

    def load_x(self, d_x):
        cx = self.cx
        for k in range(KC):
            cx.dma("pool", self.X[k].ap, d_x[:, k, :], self.dsem_x, writes=[self.X[k]])
        cx.group_fix(self.dsem_x, self.X)

    def boundary(self, out_li, in_li, groups):
        if in_li is not None:
            self.mod_need(*in_li)
        self.run_deferred()
        for gi in range(2):
            self._bg_out(gi, out_li, in_li, groups)
        for gi in range(2):
            self._bg_in(gi, in_li, groups, part=0)
        for gi in range(2):
            self._bg_in(gi, in_li, groups, part=1)

    def run_deferred(self):
        pass

    def _bg_out(self, gi, out_li, in_li, groups):
        cx = self.cx
        SQ = self.PG[8:16]
        t0, tn, r = groups[gi]
        if out_li is not None:
            prev = None
            for kc in range(KC):
                tq = self.tmp()
                cx.tt("dve", tq[:, 0:tn], self.H[kc][:, t0:t0 + tn].f(), self.rstd[:, t0:t0 + tn], ALU.mult)
                if prev is not None:
                    pk, ptq = prev
                    cx.tt("dve", self.X[pk][:, t0:t0 + tn], ptq[:, 0:tn], self.X[pk][:, t0:t0 + tn], ALU.add)
                prev = (kc, tq)
            pk, ptq = prev
            cx.tt("dve", self.X[pk][:, t0:t0 + tn], ptq[:, 0:tn], self.X[pk][:, t0:t0 + tn], ALU.add)
        if in_li is not None:
            for kc in range(KC):
                cx.act(SQ[kc][:, t0:t0 + tn], self.X[kc][:, t0:t0 + tn], AF.Square)

    def _bg_in(self, gi, in_li, groups, part):
        cx = self.cx
        if in_li is None:
            return
        SQ = self.PG[8:16]
        t0, tn, r = groups[gi]
        li, ii = in_li
        pss = self.psum_stat(gi)
        if part == 0:
            for kc in range(KC):
                cx.mm(pss[:, 0:tn], self.onesF.v(), SQ[kc][:, t0:t0 + tn], start=(kc == 0), stop=(kc == KC - 1),
                      inc=(kc == KC - 1))
            self.rsqrt(self.rstd[:, t0:t0 + tn], pss[:, 0:tn], 1.0 / D, tn)
            return
        for kc in range(KC):
            tq = self.tmp()
            cx.tt("dve", tq[:, 0:tn], self.X[kc][:, t0:t0 + tn], self.rstd[:, t0:t0 + tn], ALU.mult)
            cx.act(self.H[kc][:, t0:t0 + tn], tq[:, 0:tn], AF.Identity, bias=self.tab[:, li, ii, r, 1, kc:kc + 1],
                   scale=self.tab[:, li, ii, r, 0, kc:kc + 1])

    def rsqrt(self, out, src, scale, tn):
        cx = self.cx
        tq = self.tmp()
        cx.act(tq[:, 0:tn], src, AF.Ln, bias=self.epsb[:, 0:1], scale=scale)
        cx.act(out, tq[:, 0:tn], AF.Exp, scale=-0.5)

    def ffn(self, l, s, groups):
        cx = self.cx
        self.cur_out = (l, 0 if s == 0 else 2)
        self.cur_groups = groups
        self.mod_need(l, 0 if s == 0 else 2)
        wg = self.d_wg[l, s].rearrange("(kc p) n -> p kc n", p=128)
        wu = self.d_wu[l, s].rearrange("(kc p) n -> p kc n", p=128)
        wd = self.d_wd[l, s].rearrange("(fc p) n -> p fc n", p=128)
        self.ps_rr_n = 8
        self.W.hold = 2
        for blk in range(11):
            Wg = self.W.get(wg[:, :, blk * 256:(blk + 1) * 256], [128, KC, 256])
            Wu = self.W.get(wu[:, :, blk * 256:(blk + 1) * 256], [128, KC, 256])
            for jj in range(2):
                fj = blk * 2 + jj
                for (t0, tn, _) in groups:
                    pg = self.psum()
                    pu = self.psum()
                    for kc in range(KC):
                        cx.mm(pg[:, 0:tn], Wg[:, kc, jj * 128:(jj + 1) * 128], self.H[kc][:, t0:t0 + tn],
                              start=(kc == 0), stop=(kc == KC - 1), inc=(kc == KC - 1))
                    for kc in range(KC):
                        cx.mm(pu[:, 0:tn], Wu[:, kc, jj * 128:(jj + 1) * 128], self.H[kc][:, t0:t0 + tn],
                              start=(kc == 0), stop=(kc == KC - 1), inc=(kc == KC - 1))
                    sl = self.tmp()
                    cx.act(sl[:, 0:tn], pg[:, 0:tn], AF.Silu)
                    cx.tt("dve", self.ACT[fj][:, t0:t0 + tn], sl[:, 0:tn], pu[:, 0:tn], ALU.mult)
            self.mod_step_ffn()
        self.ps_rr_n = 6
        Y = self.H
        for dc in range(KC):
            Wd0 = self.W.get(wd[:, 0:11, dc * 128:(dc + 1) * 128], [128, 11, 128])
            Wd1 = self.W.get(wd[:, 11:22, dc * 128:(dc + 1) * 128], [128, 11, 128])
            for gi, (t0, tn, _) in enumerate(groups):
                py = self.psum()
                for fc in range(FC):
                    wv_ = Wd0[:, fc, :] if fc < 11 else Wd1[:, fc - 11, :]
                    cx.mm(py[:, 0:tn], wv_, self.ACT[fc][:, t0:t0 + tn], start=(fc == 0), stop=(fc == FC - 1),
                          inc=(fc == FC - 1))
                self.y_evac(dc, t0, tn, gi, py[:, 0:tn], dc == 0, dc == KC - 1)
            self.mod_step_ffn()
        self.W.hold = 1
        self.y_finish(groups)


    def fm_linear(self, wv, col0, nchunks, src, tok_ranges, evac):
        cx = self.cx
        nk = len(src)
        j = 0
        while j < nchunks:
            nb = min(2, nchunks - j)
            Wt = self.W.get(wv[:, :, col0 + j * 128: col0 + (j + nb) * 128], [128, nk, nb * 128])
            for jj in range(nb):
                for (t0, tn) in tok_ranges:
                    ps = self.psum()
                    for kc in range(nk):
                        cx.mm(ps[:, 0:tn], Wt[:, kc, jj * 128:(jj + 1) * 128], src[kc][:, t0:t0 + tn],
                              start=(kc == 0), stop=(kc == nk - 1), inc=(kc == nk - 1))
                    evac(j + jj, t0, tn, ps[:, 0:tn])
            j += nb
            self.mod_step(1)

    def tm_linear(self, wv, col0, ncols, src, tiles, evac):
        cx = self.cx
        nk = len(src)
        c = 0
        while c < ncols:
            bw = min(256, ncols - c)
            Wt = self.W.get(wv[:, :, col0 + c: col0 + c + bw], [128, nk, bw])
            for tt in tiles:
                if tt >= 4:
                    self.run_deferred()
                ps = self.psum()
                for kc in range(nk):
                    cx.mm(ps[:, 0:bw], src[kc][:, tt * 128:(tt + 1) * 128], Wt[:, kc, :],
                          start=(kc == 0), stop=(kc == nk - 1), inc=(kc == nk - 1))
                evac(c, bw, tt, ps[:, 0:bw])
            c += bw
            self.mod_step(1)

    def ev_copy(self, dst, src):
        self.n_ev += 1
        self.cx.copy("act" if self.n_ev % 2 else "dve", dst, src)

    def y_evac(self, dc, t0, tn, gi, ps, first, last):
        cx = self.cx
        Y = self.H
        lo, io = self.cur_out
        r = self.cur_groups[gi][2]
        cx.act(Y[dc][:, t0:t0 + tn], ps, AF.Copy, scale=self.tab[:, lo, io, r, 2, dc:dc + 1])
        sq = self.tmpr()
        cx.act(sq[:, 0:tn], ps, AF.Square)
        self.y_pending.append((gi, sq, tn, first, last))
        while len(self.y_pending) > 3:
            self.y_flush1()

    def y_flush1(self):
        gi, sq, tn, first, last = self.y_pending.pop(0)
        self.cx.mm(self.psum_stat(gi)[:, 0:tn], self.onesF.v(), sq[:, 0:tn], start=first, stop=last)

    def y_finish(self, groups):
        cx = self.cx
        while self.y_pending:
            self.y_flush1()
        for gi, (t0, tn, _) in enumerate(groups):
            self.rsqrt(self.rstd[:, t0:t0 + tn], self.psum_stat(gi)[:, 0:tn], 1.0 / D, tn)

    def out_linear(self, wdram, src, l):
        wv = wdram.rearrange("(kc p) n -> p kc n", p=128)
        gidx = {512: 0, 256: 1}
        self.cur_out = (l, 1)
        self.cur_groups = GROUPS_MAIN
        self.mod_need(l, 1)

        def ev(j, t0, tn, ps):
            self.y_evac(j, t0, tn, gidx[tn], ps, j == 0, j == KC - 1)
        self.fm_linear(wv, 0, KC, src, [(0, 512), (512, 256)], ev)
        self.y_finish(GROUPS_MAIN)

    def bcast(self, v, n):
        a = v.ap
        return V(v.tiles, bass.AP(a.tensor, a.offset, [list(a.ap[0]), [0, n], list(a.ap[1])]))

    def gla_ld(self, LR, tt, ldpage):
        cx = self.cx
        for dr in range(2):
            ps = self.psum()
            rows = slice(32 * dr, 32 * dr + 16)
            r1 = slice(32 * dr, 32 * dr + 1)
            cx.mm(ps[:, 0:256], LR[rows, tt * 128:(tt + 1) * 128], self.WA[rows, :], True, False, inc=False)
            cx.mm(ps[:, 0:256], self.onesF[r1, 0:128], self.BA[r1, :], False, True)
            e = self.tmp()
            cx.act(e[:, 0:256], ps[:, 0:256], AF.Exp, scale=-1.0)
            cx.act(ldpage[:, dr * 256:(dr + 1) * 256], e[:, 0:256], AF.Ln, bias=1.0)

    def gla_prep(self, dr, kvp, ld, gp, slot, need_q, QB=None, KB=None, tok0=0, flag=None):
        cx = self.cx
        mc = 1 if dr == 0 else 3
        mb = 0 if dr == 0 else 2
        ldv = ld[:, dr * 256:(dr + 1) * 256]
        ps = self.psum()
        cx.mm(ps[:, 0:256], self.MM[:, mc, :], ldv, True, True)
        ec = self.tmp()
        cx.act(ec[:, 0:256], ps[:, 0:256], AF.Exp, scale=-1.0 / 16)
        if flag is not None:
            cx.stt("dve", gp[:, 0:256], kvp[:, 0:256].f(), flag, ec[:, 0:256], ALU.mult, ALU.mult)
        else:
            cx.tt("dve", gp[:, 0:256], kvp[:, 0:256].f(), ec[:, 0:256], ALU.mult)
        ps2 = self.psum()
        if need_q:
            for pc in range(2):
                cx.mm(ps2[:, pc * 128:(pc + 1) * 128], ldv[:, pc * 128:(pc + 1) * 128], self.MM[:, mb, :], True, True)
            eb = self.tmp()
            cx.act(eb[:, 0:256], ps2[:, 0:256], AF.Exp, scale=-1.0 / 16, bias=float(np.log(0.125)))
            for pc in range(2):
                cx.tt("dve", gp[:, 256 + pc * 128:256 + (pc + 1) * 128], QB[pc][:, tok0:tok0 + 128].f(),
                      eb[:, pc * 128:(pc + 1) * 128], ALU.mult)
            enb = self.tmp()
            cx.act(enb[:, 0:256], ps2[:, 0:256], AF.Exp, scale=1.0 / 16)
            for pc in range(2):
                cx.tt("dve", gp[:, 512 + pc * 128:512 + (pc + 1) * 128], KB[pc][:, tok0:tok0 + 128].f(),
                      enb[:, pc * 128:(pc + 1) * 128], ALU.mult)
            col = 127 if dr == 0 else 0
            cx.act(self.DEC[:, slot, :], ps2[:, 0:256].re("p (a b) -> p a b", a=2)[:, :, col], AF.Exp, scale=-1.0 / 16)
        else:
            for pc in range(2):
                cx.mm(ps2[:, pc * 2:pc * 2 + 2], ldv[:, pc * 128:(pc + 1) * 128], self.onesF[:, 0:2], True, True)
            cx.act(self.DEC[:, slot, :], ps2[:, 0:4].re("p (a b) -> p a b", a=2)[:, :, 0], AF.Exp, scale=-1.0 / 16)

    def gla_update(self, Sprev, Snew, gp, kvp, dec, flag=None, out_f32=None):
        cx = self.cx
        U = self.psum()
        for pc in range(2):
            for e in range(2):
                h = 2 * pc + e
                cx.mm(U[:, h * 128:(h + 1) * 128], gp[:, pc * 128:(pc + 1) * 128], kvp[:, 256 + h * 128:256 + (h + 1) * 128], True, True)
        for pc in range(2):
            for e in range(2):
                h = 2 * pc + e
                rows = slice(64 * e, 64 * e + 64)
                dst = Snew[rows, pc, :] if out_f32 is None else out_f32[rows, pc, :]
                cx.stt("dve", dst, Sprev[rows, pc, :].f(), dec[rows, pc:pc + 1], U[rows, h * 128:(h + 1) * 128], ALU.mult, ALU.add)

    def gla_out_a(self, gpf, gpb):
        cx = self.cx
        ats = []
        for dr, gp in ((0, gpf), (1, gpb)):
            mb = 0 if dr == 0 else 2
            Ae = [self.psum(), self.psum()]
            for h in range(4):
                pc, e = h // 2, h % 2
                rows = slice(64 * e, 64 * e + 64)
                cx.mm(Ae[e][:, pc * 128:(pc + 1) * 128], gp[rows, 512 + pc * 128:512 + (pc + 1) * 128],
                      gp[rows, 256 + pc * 128:256 + (pc + 1) * 128], True, True)
            at = self.tmpr()
            atv = at[:, 0:512].re("p (a e b) -> p a e b", a=2, e=2)
            for e in range(2):
                cx.tt("dve", atv[:, :, e, :], Ae[e][:, 0:256].re("p (a b) -> p a b", a=2),
                      self.bcast(self.MM[:, mb, :].f(), 2), ALU.mult)
            ats.append(at)
        return ats

    def gla_out(self, kvp, gpf, gpb, Sf, Sb, dst_pages, tok0, ats):
        cx = self.cx
        PI = self.psum()
        for h in range(4):
            o = PI[:, h * 128:(h + 1) * 128]
            vb = kvp[:, 256 + h * 128:256 + (h + 1) * 128]
            cx.mm(o, vb, ats[0][:, h * 128:(h + 1) * 128], True, False, inc=False)
            cx.mm(o, vb, ats[1][:, h * 128:(h + 1) * 128], False, True)
        PX = [self.psum(), self.psum()]
        for h in range(4):
            pc, e = h // 2, h % 2
            rows = slice(64 * e, 64 * e + 64)
            o = PX[e][:, pc * 128:(pc + 1) * 128]
            cx.mm(o, Sf[rows, pc, :], gpf[rows, 256 + pc * 128:256 + (pc + 1) * 128], True, False, inc=False)
            cx.mm(o, Sb[rows, pc, :], gpb[rows, 256 + pc * 128:256 + (pc + 1) * 128], False, True)
        for h in range(4):
            pc, e = h // 2, h % 2
            tc_ = self.tmp()
            cx.copy("act", tc_[:, 0:128], PI[:, h * 128:(h + 1) * 128])
            cx.tt("dve", dst_pages[h][:, tok0:tok0 + 128], tc_[:, 0:128], PX[e][:, pc * 128:(pc + 1) * 128], ALU.add)

    def load_rope(self, pc_page, ps_page):
        cx = self.cx
        cx.dma("sp", pc_page[:, 0:512].ap, self.d_ropec.bitcast(F32R), self.dsem_c2, writes=[pc_page])
        cx.dma("sp", ps_page[:, 0:512].ap, self.d_ropes.bitcast(F32R), self.dsem_c2, writes=[ps_page])
        cx.group_fix(self.dsem_c2, [pc_page, ps_page])

    def prepass(self):
        cx = self.cx
        PG = self.PG
        self.load_x(self.d_xo)
        self.boundary(None, (0, 0), GROUPS_OTH)
        self.ffn(0, 0, GROUPS_OTH)
        self.boundary((0, 0), (0, 1), GROUPS_OTH)
        if self.cut <= 1:
            return
        wfm = self.d_winfm.rearrange("(kc p) n -> p kc n", p=128)
        wtm = self.d_wintm.rearrange("(kc p) n -> p kc n", p=128)
        LR, KDh, KSh, RC, RS = PG[8], PG[9], PG[10], PG[14], PG[15]
        KVp = PG[16:22]
        LD = PG[22:28]
        GP = [PG[11], PG[12], PG[13], PG[28]]
        self.load_rope(RC, RS)
        allr = [(0, 512), (512, 256)]
        halo = [(0, 128), (640, 128)]
        self.fm_linear(wfm, 10 * 128, 1, self.H, allr, lambda j, t0, tn, ps: self.ev_copy(LR[:, t0:t0 + tn], ps))

        def ev_h(dst):
            def f(j, t0, tn, ps):
                hs = 0 if t0 == 0 else 1
                self.ev_copy(dst[:, j * 256 + hs * 128: j * 256 + (hs + 1) * 128], ps)
            return f
        self.fm_linear(wfm, 4 * 128, 2, self.H, halo, ev_h(KDh))
        self.fm_linear(wfm, 19 * 128, 2, self.H, halo, ev_h(KSh))
        if self.cut <= 2:
            return
        for kv in range(2):
            t1 = self.tmp()
            t2 = self.tmp()
            cx.tt("dve", t1[:, 0:256], KDh[:, kv * 256:(kv + 1) * 256].f(), RC[:, 256:512].f(), ALU.mult)
            cx.tt("dve", t2[:, 0:256], KSh[:, kv * 256:(kv + 1) * 256].f(), RS[:, 256:512].f(), ALU.mult)
            cx.tt("dve", self.HK[:, kv, :, :], t1[:, 0:256].re("p (a b) -> p a b", a=2), t2[:, 0:256].re("p (a b) -> p a b", a=2), ALU.add)
        self.tm_linear(wtm, 0, 256, self.H, [0, 5],
                       lambda c, bw, tt, ps: self.ev_copy(self.HV[:, 0 if tt == 0 else 1, :], ps))
        if self.cut <= 3:
            return
        self.tm_linear(wtm, 512, 768, self.H, list(range(6)),
                       lambda c, bw, tt, ps: self.ev_copy(KVp[tt][:, c:c + bw], ps))
        for tt in range(6):
            self.gla_ld(LR, tt, LD[tt])
        if self.cut <= 4:
            return
        GPs = [PG[11], PG[12], PG[13], PG[28]]

        def kslot(n):
            return GPs[n // 3][:, (n % 3) * 256:(n % 3 + 1) * 256]
        jobs = []
        for dr in range(2):
            order = [0, 1, 2, 3, 4, 5] if dr == 0 else [5, 4, 3, 2, 1, 0]
            for n, tt in enumerate(order):
                jobs.append((dr, n, tt))
        for (dr, n, tt) in jobs:
            sl = dr * 6 + n
            flag = self.FL[:, dr * 3 + tt // 2: dr * 3 + tt // 2 + 1]
            self.gla_prep(dr, KVp[tt], LD[tt], kslot(sl), sl, False, flag=flag)
            cx.ts("dve", self.DM[:, sl, :], self.DEC[:, sl, :], -1.0, ALU.add, flag, ALU.mult)
            cx.ts("dve", self.DM[:, sl, :], self.DM[:, sl, :], 1.0, ALU.add)
        for dr in range(2):
            Sprev = self.S[dr]
            pp = [self.S[4], self.S[5]]
            for (d2, n, tt) in jobs:
                if d2 != dr:
                    continue
                sl = dr * 6 + n
                Snew = self.S[2 + dr] if n == 5 else pp[n % 2]
                self.gla_update(Sprev, Snew, kslot(sl), KVp[tt], self.DM[:, sl, :])
                Sprev = Snew

    def even_mixer(self):
        cx = self.cx
        PG = self.PG
        wfm = self.d_winfm.rearrange("(kc p) n -> p kc n", p=128)
        wtm = self.d_wintm.rearrange("(kc p) n -> p kc n", p=128)
        CAT = PG[8:16]
        QA, KD, SW, VT, PT = PG[16:20], PG[20:22], PG[22:24], PG[24:26], PG[26:29]
        RC, RS = PG[14], PG[15]
        allr = [(0, 512), (512, 256)]
        samp = [(512, 256)]
        self.load_rope(RC, RS)
        self.fm_linear(wfm, 0, 4, self.H, allr, lambda j, t0, tn, ps: self.ev_copy(QA[j][:, t0:t0 + tn], ps))
        self.fm_linear(wfm, 4 * 128, 2, self.H, allr, lambda j, t0, tn, ps: self.ev_copy(KD[j][:, t0:t0 + tn], ps))

        def swv(i):
            return SW[i // 3][:, (i % 3) * 256:(i % 3 + 1) * 256]
        self.fm_linear(wfm, 15 * 128, 4, self.H, samp, lambda j, t0, tn, ps: self.ev_copy(swv(j), ps))
        self.fm_linear(wfm, 19 * 128, 2, self.H, samp, lambda j, t0, tn, ps: self.ev_copy(swv(4 + j), ps))
        if self.cut <= 10:
            return
        for i in range(6):
            dst = (QA[i] if i < 4 else KD[i - 4])[:, 512:768]
            t1 = self.tmp()
            t2 = self.tmp()
            cx.tt("dve", t1[:, 0:256], dst.f(), RC[:, 0:256].f(), ALU.mult)
            cx.tt("dve", t2[:, 0:256], swv(i).f(), RS[:, 0:256].f(), ALU.mult)
            cx.tt("dve", dst, t1[:, 0:256], t2[:, 0:256], ALU.add)

        def vtv(tt):
            return VT[tt // 3][:, (tt % 3) * 256:(tt % 3 + 1) * 256]

        def ev_tm_v(c, bw, tt, ps):
            self.ev_copy(vtv(tt), ps)
            if tt < 4:
                for kv in range(2):
                    cx.dma("sp", self.o_kv[:, tt, 128 + kv * 64:128 + (kv + 1) * 64], vtv(tt)[:, kv * 128:kv * 128 + 64].f().ap,
                           self.dsem_out, reads=[vtv(tt)])

        def ev_tm_k(c, bw, tt, ps):
            tk = self.tmp()
            self.ev_copy(tk[:, 0:128], ps[:, 0:128])
            cx.dma("sp", self.o_kv[:, tt, 0:128], tk[:, 0:128].ap, self.dsem_out, reads=[tk])
        self.tm_linear(wtm, 0, 256, self.H, list(range(6)), ev_tm_v)
        self.tm_linear(wtm, 256, 256, self.H, list(range(4)), ev_tm_k)
        if self.cut <= 12:
            return

        self.ps_rr_n = 4
        ring = [PG[26], PG[27], PG[28], PG[29], PG[12], PG[13], PG[22], PG[23]]
        rn = [0]

        def ptile():
            t = ring[rn[0] % len(ring)]
            rn[0] += 1
            return t

        def finalize(c, c0, po, den):
            for e in range(2):
                rows = slice(64 * e, 64 * e + 64)
                t1 = self.tmp()
                cx.act(t1[rows, 0:256], den[e][rows, 0:256], AF.Ln, bias=self.esink[rows, c:c + 1])
                cx.act(t1[rows, 0:256], t1[rows, 0:256], AF.Exp, scale=-1.0)
                cx.tt("dve", CAT[c][rows, c0:c0 + 256], po[e][rows, 0:256], t1[rows, 0:256], ALU.mult)

        units = [(sq, c) for sq in range(2) for c in range(4)]

        def scores(u):
            sq, c = units[u]
            c0, kv = sq * 256, c // 2
            Pk = [ptile(), ptile()]
            for kt in range(2):
                for e in range(2):
                    rows = slice(64 * e, 64 * e + 64)
                    Sp = self.psum()
                    cx.mm(Sp[:, 0:256], KD[kv][rows, c0 + kt * 128:c0 + (kt + 1) * 128], QA[c][rows, c0:c0 + 256], True, True)
                    cx.act(Pk[kt][:, e * 256:(e + 1) * 256], Sp[:, 0:256], AF.Exp, scale=0.125)
            return Pk

        def pv(u, Pk):
            sq, c = units[u]
            c0, kv = sq * 256, c // 2
            bank = self.ps[4 + 2 * (u % 2)]
            bank2 = self.ps[5 + 2 * (u % 2)]
            po = [bank[:, 0:256], bank[:, 256:512]]
            den = [bank2[:, 0:256], bank2[:, 256:512]]
            for e in range(2):
                for kt in range(2):
                    vt = c0 // 128 + kt
                    cx.mm(po[e], vtv(vt)[:, kv * 128:(kv + 1) * 128], Pk[kt][:, e * 256:(e + 1) * 256], kt == 0, kt == 1)
            for e in range(2):
                for kt in range(2):
                    cx.mm(den[e], self.onesF.v(), Pk[kt][:, e * 256:(e + 1) * 256], kt == 0, kt == 1)
            finalize(c, c0, po, den)

        pend = scores(0)
        for u in range(len(units)):
            nxt = scores(u + 1) if u + 1 < len(units) else None
            pv(u, pend)
            pend = nxt

        c0 = 512
        for c in range(4):
            kv = c // 2
            po = [self.ps[4][:, 0:256], self.ps[5][:, 0:256]]
            den = [self.ps[6][:, 0:256], self.ps[7][:, 0:256]]
            Ps = []
            for kt in range(6):
                P = ptile()
                Ps.append(P)
                q0, qn = (0, 128) if kt == 2 else ((128, 128) if kt == 5 else (0, 256))
                for e in range(2):
                    rows = slice(64 * e, 64 * e + 64)
                    if kt < 2:
                        kl = self.CK[rows, kv, kt * 128:(kt + 1) * 128]
                    elif kt == 2:
                        kl = self.HK[rows, kv, 1, :]
                    elif kt == 5:
                        kl = self.HK[rows, kv, 0, :]
                    else:
                        kl = KD[kv][rows, c0 + (kt - 3) * 128:c0 + (kt - 2) * 128]
                    Sp = self.psum()
                    cx.mm(Sp[:, 0:qn], kl, QA[c][rows, c0 + q0:c0 + q0 + qn], True, True)
                    cx.act(P[:, e * 256 + q0:e * 256 + q0 + qn], Sp[:, 0:qn], AF.Exp, scale=0.125)
                Pv = P[:, 0:512].re("p (a b) -> p a b", a=2)
                lo, hi = Pv[:, :, 0:128], Pv[:, :, 128:256]
                if kt == 2:
                    cx.tt("pool", lo, lo.f(), self.bcast(self.tri[:, 2, :], 2), ALU.mult)
                elif kt == 3:
                    cx.tt("pool", hi, hi.f(), self.bcast(self.tri[:, 0, :], 2), ALU.mult)
                elif kt == 4:
                    cx.tt("pool", lo, lo.f(), self.bcast(self.tri[:, 1, :], 2), ALU.mult)
                elif kt == 5:
                    cx.tt("pool", hi, hi.f(), self.bcast(self.tri[:, 3, :], 2), ALU.mult)
            for kt in range(6):
                P = Ps[kt]
                q0, qn = (0, 128) if kt == 2 else ((128, 128) if kt == 5 else (0, 256))
                if kt < 2:
                    vl = self.CV[:, kt, kv, :]
                elif kt == 2:
                    vl = self.HV[:, 1, kv * 128:(kv + 1) * 128]
                elif kt == 5:
                    vl = self.HV[:, 0, kv * 128:(kv + 1) * 128]
                else:
                    vl = vtv(4 + kt - 3)[:, kv * 128:(kv + 1) * 128]
                for e in range(2):
                    pe_ = P[:, e * 256 + q0:e * 256 + q0 + qn]
                    cx.mm(V(po[e].tiles, po[e].ap[:, q0:q0 + qn]), vl, pe_, kt == 0, kt == 5)
                    cx.mm(V(den[e].tiles, den[e].ap[:, q0:q0 + qn]), self.onesF.v(), pe_, kt == 0, kt == 5)
            finalize(c, c0, po, den)
        self.ps_rr_n = 6

        if self.cut <= 14:
            return
        QB, KB, LR = PG[16:18], PG[18:20], PG[20]
        KVp, LD, GP = PG[21:23], PG[23:25], PG[25:29]
        self.fm_linear(wfm, 6 * 128, 2, self.H, allr, lambda j, t0, tn, ps: self.ev_copy(QB[j][:, t0:t0 + tn], ps))
        self.fm_linear(wfm, 8 * 128, 2, self.H, allr, lambda j, t0, tn, ps: self.ev_copy(KB[j][:, t0:t0 + tn], ps))
        self.fm_linear(wfm, 10 * 128, 1, self.H, allr, lambda j, t0, tn, ps: self.ev_copy(LR[:, t0:t0 + tn], ps))
        ZS = self.S[6]
        if self.cut <= 15:
            return
        for sg in range(3):
            tiles = [2 * sg, 2 * sg + 1]
            self.tm_linear(wtm, 512, 768, self.H, tiles, lambda c, bw, tt, ps: self.ev_copy(KVp[tt % 2][:, c:c + bw], ps))
            for ci in range(2):
                self.gla_ld(LR, tiles[ci], LD[ci])
            if self.cut == 151:
                return
            for (ci, dr) in [(0, 0), (1, 0), (1, 1), (0, 1)]:
                self.gla_prep(dr, KVp[ci], LD[ci], GP[ci * 2 + dr], ci * 2 + dr, True, QB, KB, tiles[ci] * 128)
            if self.cut == 152:
                return
            Sf0 = ZS if sg < 2 else self.S[2]
            Sb0 = ZS if sg < 2 else self.S[3]
            Sf1, Sb1 = self.S[4], self.S[5]
            self.gla_update(Sf0, Sf1, GP[0], KVp[0], self.DEC[:, 0, :])
            self.gla_update(Sb0, Sb1, GP[3], KVp[1], self.DEC[:, 3, :])
            if self.cut == 153:
                return
            ats0 = self.gla_out_a(GP[0], GP[1])
            ats1 = self.gla_out_a(GP[2], GP[3])
            self.gla_out(KVp[0], GP[0], GP[1], Sf0, Sb1, CAT[4:8], tiles[0] * 128, ats0)
            self.gla_out(KVp[1], GP[2], GP[3], Sf1, Sb0, CAT[4:8], tiles[1] * 128, ats1)
            if sg < 2:
                self.gla_update(Sf1, None, GP[2], KVp[1], self.DEC[:, 2, :], out_f32=self.FIN[0][:, sg, :, :])
                self.gla_update(Sb1, None, GP[1], KVp[0], self.DEC[:, 1, :], out_f32=self.FIN[1][:, sg, :, :])
            if self.cut == 155:
                return
        cx.dma("sp", self.o_sf, self.FIN[0].ap.rearrange("p a b c -> p a (b c)"), self.dsem_out, reads=[self.FIN[0]])
        cx.dma("sp", self.o_sb, self.FIN[1].ap.rearrange("p a b c -> p a (b c)"), self.dsem_out, reads=[self.FIN[1]])
        if self.cut <= 16:
            return

        RH = PG[16:20]
        for h in range(4):
            sqp = PG[20 + h]
            cx.act(sqp.v(), CAT[4 + h].v().f(), AF.Square)
            for (t0, tn) in allr:
                ss = self.psum()
                cx.mm(ss[:, 0:tn], self.onesF.v(), sqp[:, t0:t0 + tn], True, True)
                self.rsqrt(RH[h][:, t0:t0 + tn], ss[:, 0:tn], 1.0 / 128, tn)
                oh = CAT[4 + h][:, t0:t0 + tn]
                cx.stt("dve", oh, oh.f(), self.gg[:, h:h + 1], RH[h][:, t0:t0 + tn].f(), ALU.mult, ALU.mult)

        def ev_gate(j, t0, tn, ps):
            o = CAT[4 + j][:, t0:t0 + tn]
            sgt = self.tmp()
            cx.act(sgt[:, 0:tn], ps, AF.Silu)
            cx.tt("dve", o, o.f(), sgt[:, 0:tn], ALU.mult)
        self.fm_linear(wfm, 11 * 128, 4, self.H, allr, ev_gate)
        if self.cut <= 17:
            return
        self.out_linear(self.d_wout, CAT, 0)

    def odd_mixer(self):
        cx = self.cx
        PG = self.PG
        wi = self.d_cmwin.rearrange("(kc p) n -> p kc n", p=128)
        U, VTM, GBp, WSB = PG[8:16], PG[16:24], PG[24:27], PG[27:29]
        allr = [(0, 512), (512, 256)]

        def gbv(which, cb):
            li = which * 4 + cb
            return GBp[li // 3][:, (li % 3) * 256:(li % 3 + 1) * 256]
        for which in range(2):
            for cb in range(4):
                v = gbv(which, cb)
                cx.dma("sp", v.ap, self.d_cmgb[:, which * 1024 + cb * 256: which * 1024 + (cb + 1) * 256].bitcast(F32R),
                       self.dsem_c2, writes=[v])
        WS = WSB[0][:, 0:512].re("p (g t) -> p g t", g=4)
        BS = WSB[1][:, 0:512].re("p (g t) -> p g t", g=4)
        cx.dma("sp", WSB[0][:, 0:512].ap, self.d_cmws.bitcast(F32R), self.dsem_c2, writes=[WSB[0]])
        cx.dma("sp", WSB[1][:, 0:512].ap, self.d_cmbs.bitcast(F32R), self.dsem_c2, writes=[WSB[1]])
        cx.group_fix(self.dsem_c2, list(GBp) + list(WSB))

        def vblk(tt, cb):
            li = tt * 4 + cb
            return VTM[li // 3][:, (li % 3) * 256:(li % 3 + 1) * 256]
        self.tm_linear(wi, 1024, 1024, self.H, list(range(6)), lambda c, bw, tt, ps: cx.act(vblk(tt, c // 256), ps, AF.Gelu))
        for tt in range(6):
            st = self.lnst[tt]
            for cb in range(4):
                v = vblk(tt, cb).f()
                cx.op("dve", lambda e, v=v, cb=cb, st=st: e.bn_stats(out=st.ap[:, 16 + cb * 6:16 + (cb + 1) * 6], in_=v.ap),
                      reads=[v], writes=[st])
            cx.op("dve", lambda e, st=st: e.bn_aggr(out=st.ap[:, 10:12], in_=st.ap[:, 16:40]), reads=[st], writes=[st])
        allst = V([k for t in self.lnst for k in _tiles([t])], self.lnst_all)
        cx.act(allst[:, :, 15], allst[:, :, 11], AF.Ln, bias=self.epsb[:, 0:1], scale=1.0)
        cx.act(allst[:, :, 13], allst[:, :, 15], AF.Exp, scale=-0.5)
        cx.stt("dve", allst[:, :, 14], allst[:, :, 10], -1.0, allst[:, :, 13], ALU.mult, ALU.mult)
        prev = None
        for tt in range(6):
            st = self.lnst[tt]
            for cb in range(4):
                v = vblk(tt, cb)
                t1 = self.tmp()
                cx.ts("dve", t1[:, 0:256], v.f(), st[:, 13:14], ALU.mult, st[:, 14:15], ALU.add)
                cx.tt("pool", t1[:, 0:256], t1[:, 0:256], gbv(0, cb).f(), ALU.mult)
                if prev is not None:
                    cx.tt("dve", prev[0], prev[1][:, 0:256], gbv(1, prev[2]).f(), ALU.add)
                prev = (v, t1, cb)
        cx.tt("dve", prev[0], prev[1][:, 0:256], gbv(1, prev[2]).f(), ALU.add)
        self.fm_linear(wi, 0, 8, self.H, allr, lambda j, t0, tn, ps: cx.act(U[j][:, t0:t0 + tn], ps, AF.Gelu))
        for cc in range(8):
            g = cc // 2
            for tiles in ([0, 1, 2, 3], [4, 5]):
                n = len(tiles)
                ps = self.psum()
                for ti, tt in enumerate(tiles):
                    cx.mm(ps[:, ti * 128:(ti + 1) * 128], vblk(tt, cc // 2)[:, (cc % 2) * 128:(cc % 2 + 1) * 128], WS[:, g, :], True, True)
                t1 = self.tmp()
                cx.tt("dve", t1[:, 0:n * 128].re("p (a b) -> p a b", a=n), ps[:, 0:n * 128].re("p (a b) -> p a b", a=n),
                      self.bcast(BS[:, g, :].f(), n), ALU.add)
                uu = U[cc][:, tiles[0] * 128:(tiles[-1] + 1) * 128]
                cx.tt("pool", uu, t1[:, 0:n * 128], uu.f(), ALU.mult)
        self.out_linear(self.d_cmwout, U, 1)

    def store_out(self):
        cx = self.cx
        for k in range(KC):
            cx.dma("sp", self.o_y[:, k, :], self.X[k].ap, self.dsem_out, reads=[self.X[k]])


def build_program(stage):
    nc0 = bass.Bass("TRN2", target_bir_lowering=False)
    nc0.dge_precook = False
    plan = Prog(nc0, True, None, stage).build()
    nc = bass.Bass("TRN2", target_bir_lowering=False)
    nc.dge_precook = False
    p = Prog(nc, False, plan, stage)
    p.build()
    build_program.last_log = p.cx.log
    return nc


def fm(x):
    t, d = x.shape
    return np.ascontiguousarray(x.reshape(t, d // 128, 128).transpose(2, 1, 0))


def fm_vec(v):
    sh = v.shape
    v2 = v.reshape(-1, sh[-1] // 128, 128)
    return np.ascontiguousarray(v2.transpose(2, 0, 1)).reshape(128, -1)


def _swap_idx():
    d = np.arange(64)
    a, bb, f = d // 32, (d % 32) // 16, d % 16
    return a * 32 + (1 - bb) * 16 + f


def _rope_tables(pos):
    f32 = np.float32
    half, nf = 32, 16
    inv = (f32(10000.0) ** (-np.arange(nf, dtype=f32) * f32(2.0) / f32(half))).astype(f32)
    row = (pos // 64).astype(f32)
    col = (pos % 64).astype(f32)
    d = np.arange(64)
    a, bb, f = d // 32, (d % 32) // 16, d % 16
    p = np.where(a[:, None] == 0, row[None, :], col[None, :]).astype(f32)
    ang = (p * inv[f][:, None]).astype(f32)
    cos = np.cos(ang).astype(f32)
    sin = np.sin(ang).astype(f32)
    sgn = np.where(bb == 0, -1.0, 1.0).astype(f32)[:, None]
    sins = (sin * sgn).astype(f32)
    return np.concatenate([cos, cos], 0), np.concatenate([sins, sins], 0)


def make_in_maps(inp):
    f32 = np.float32
    g = lambda k: np.asarray(inp[k], f32)
    x_prompt, x_sample, c, c_ctx = g("x_prompt"), g("x_sample"), g("c"), g("c_ctx")
    w_in = g("ev_w_in")[0]
    sw = _swap_idx()
    qa, ka, va = w_in[:, 0:512], w_in[:, 512:640], w_in[:, 640:768]
    qb, kb, vb, gb = w_in[:, 768:1024], w_in[:, 1024:1280], w_in[:, 1280:1792], w_in[:, 1792:2304]
    lrf, lrb = w_in[:, 2304:2320], w_in[:, 2320:2336]
    ka_dup = np.concatenate([ka[:, 0:64], ka[:, 0:64], ka[:, 64:128], ka[:, 64:128]], 1)
    qa_sw = qa.reshape(D, 8, 64)[:, :, sw].reshape(D, 512)
    ka_sw = ka.reshape(D, 2, 64)[:, :, sw].reshape(D, 128)
    ka_sw_dup = np.concatenate([ka_sw[:, 0:64], ka_sw[:, 0:64], ka_sw[:, 64:128], ka_sw[:, 64:128]], 1)
    lr = np.concatenate([lrf, lrf, lrb, lrb, lrf, lrf, lrb, lrb], 1)
    w_in_fm = np.ascontiguousarray(np.concatenate([qa, ka_dup, qb, kb, lr, gb, qa_sw, ka_sw_dup], 1))
    assert w_in_fm.shape == (D, 2688)
    va_dup = np.concatenate([va[:, 0:64], va[:, 0:64], va[:, 64:128], va[:, 64:128]], 1)
    w_in_tm = np.ascontiguousarray(np.concatenate([va_dup, ka, ka, kb, vb], 1))
    assert w_in_tm.shape == (D, 1280)
    jj, ii = np.meshgrid(np.arange(128), np.arange(128), indexing="ij")
    gmask = np.stack([(jj <= ii), (jj > ii), (jj >= ii), (jj < ii)], 1).astype(f32).reshape(128, 512)
    tri_prev = (jj >= ii).astype(f32)
    tri_next = (jj <= ii).astype(f32)
    wa = np.zeros((64, 256), f32)
    waf, wab = g("gla_wa_f")[0], g("gla_wa_b")[0]
    wa[0:16], wa[16:32], wa[32:48], wa[48:64] = waf, waf, wab, wab
    ba = np.zeros((64, 256), f32)
    ba[0:32] = g("gla_ba_f")[0][None, :]
    ba[32:64] = g("gla_ba_b")[0][None, :]
    sink = g("ev_sink")[0]
    sinkT = np.ascontiguousarray(np.stack([np.where(np.arange(128) < 64, sink[2 * cc], sink[2 * cc + 1]) for cc in range(4)], 1)).astype(f32)
    cm_gb = np.ascontiguousarray(np.broadcast_to(np.concatenate([g("cm_v_gain")[0], g("cm_v_bias")[0]])[None, :], (128, 2048))).astype(f32)
    cm_ws = np.ascontiguousarray(g("cm_w_s")[0].transpose(2, 0, 1).reshape(128, 512))
    cm_bs = np.ascontiguousarray(np.broadcast_to(g("cm_b_s")[0].reshape(1, 512), (128, 512))).astype(f32)
    shared = {
        "w_mod": np.ascontiguousarray(g("w_mod")),
        "b_mod": fm_vec(g("b_mod")),
        "npre": fm_vec(g("norm_pre")),
        "npost": fm_vec(g("norm_post")),
        "ffn_w_gate": np.ascontiguousarray(g("ffn_w_gate")),
        "ffn_w_up": np.ascontiguousarray(g("ffn_w_up")),
        "ffn_w_down": np.ascontiguousarray(g("ffn_w_down")),
        "ident": np.eye(128, dtype=f32),
        "w_in_fm": w_in_fm, "w_in_tm": w_in_tm,
        "ev_w_out": np.ascontiguousarray(g("ev_w_out")[0]),
        "gmask": gmask,
        "sinkT": sinkT,
        "gla_g": fm_vec(g("gla_norm")),
        "gla_wa": wa, "gla_ba": np.ascontiguousarray(ba),
        "cm_w_in": np.ascontiguousarray(g("cm_w_in")[0]),
        "cm_w_out": np.ascontiguousarray(g("cm_w_out")[0]),
        "cm_gb": cm_gb, "cm_ws": cm_ws, "cm_bs": cm_bs,
    }
    cache_k, cache_v = g("cache_k"), g("cache_v")
    sf, sbw = g("state_gla_fwd"), g("state_gla_bwd")
    maps = []
    for i in range(NCORE):
        b, q = i // 4, i % 4
        own = x_sample[b, 256 * q:256 * (q + 1)]
        main = np.concatenate([x_prompt[2 * i], x_prompt[2 * i + 1], own], axis=0)
        oth = np.concatenate([x_sample[b, 256 * ((q + r) % 4):256 * ((q + r) % 4) + 256] for r in (1, 2, 3)], axis=0)
        cond = np.stack([c_ctx, c[b]], axis=0)
        m = dict(shared)
        m["xm"] = fm(main)
        m["xo"] = fm(oth)
        m["cond"] = np.ascontiguousarray(cond.reshape(2, KC, 128).transpose(2, 1, 0))
        pos = np.concatenate([256 * q + np.arange(256), (256 * (q + 1) + np.arange(128)) % 1024,
                              (256 * q - 128 + np.arange(128)) % 1024])
        rc, rs = _rope_tables(pos)
        m["rope_c"], m["rope_s"] = np.ascontiguousarray(rc), np.ascontiguousarray(rs)
        vp, vn = f32(q > 0), f32(q < 3)
        m["tri"] = np.ascontiguousarray(np.stack([tri_prev, tri_next, tri_prev * vp, tri_next * vn], 1).reshape(128, 512))
        fl = np.zeros((128, 6), f32)
        for r in (1, 2, 3):
            fl[:, r - 1] = f32(r >= 4 - q)
            fl[:, 3 + r - 1] = f32(r <= 3 - q)
        m["flags"] = fl
        ck = cache_k[b, 0]
        ckT = ck.transpose(2, 1, 0)
        m["ctx_k"] = np.ascontiguousarray(np.concatenate([ckT, ckT], 0).reshape(128, 512))
        cv = cache_v[b, 0].reshape(2, 128, 2, 64).transpose(1, 0, 2, 3)
        m["ctx_v"] = np.ascontiguousarray(np.concatenate([cv, cv], 3).reshape(128, 512))
        for nm, stt in (("s0f", sf), ("s0b", sbw)):
            s0 = stt[b, 0]
            m[nm] = np.ascontiguousarray(s0.reshape(2, 2, 64, 128).transpose(1, 2, 0, 3).reshape(128, 256))
        maps.append(m)
    return maps


def assemble(results):
    f32 = np.float32
    y_prompt = np.zeros((16, 256, D), f32)
    y_sample = np.zeros((2, 1024, D), f32)
    nk = np.zeros((16, 1, 256, 2, 64), f32)
    nv = np.zeros((16, 1, 256, 2, 64), f32)
    nsf = np.zeros((16, 1, 4, 64, 128), f32)
    nsb = np.zeros((16, 1, 4, 64, 128), f32)
    for i, r in enumerate(results):
        b, q = i // 4, i % 4
        y = r["ym"].transpose(2, 1, 0).reshape(T, D)
        y_prompt[2 * i] = y[0:256]
        y_prompt[2 * i + 1] = y[256:512]
        y_sample[b, 256 * q:256 * (q + 1)] = y[512:768]
        kvo = r["kv_out"].transpose(1, 0, 2).reshape(512, 256)
        for sq in range(2):
            blk = kvo[sq * 256:(sq + 1) * 256]
            nk[2 * i + sq, 0] = blk[:, 0:128].reshape(256, 2, 64)
            nv[2 * i + sq, 0] = blk[:, 128:256].reshape(256, 2, 64)
        for arr, key in ((nsf, "sf_out"), (nsb, "sb_out")):
            o = r[key].reshape(128, 2, 2, 128)
            for sq in range(2):
                arr[2 * i + sq, 0] = o[:, sq].reshape(2, 64, 2, 128).transpose(2, 0, 1, 3).reshape(4, 64, 128)
    return (y_prompt, y_sample, nk, nv, nsf, nsb)


_NC_CACHE = {}


def kernel(**inputs):
    stage = int(os.environ.get("MK_STAGE", "99"))
    if stage not in _NC_CACHE:
        _NC_CACHE[stage] = build_program(stage)
    nc = _NC_CACHE[stage]
    maps = make_in_maps(inputs)
    res = run_bass_kernel_spmd(nc, maps, core_ids=list(range(NCORE)))
    kernel.last = res.results
    if stage < 99:
        return res.results
    return assemble(res.results)
```

```python
import os
import numpy as np
from contextlib import ExitStack
import concourse.bass as bass
import concourse.mybir as mybir
from concourse.bass_utils import run_bass_kernel_spmd

F32 = mybir.dt.float32
F32R = mybir.dt.float32r
AF = mybir.ActivationFunctionType
ALU = mybir.AluOpType
SAME_ENGINE_SYNC = True

D = 1024
KC = 8
T = 768
FF = 2816
FC = 22
EPS = 1e-6
NCORE = 8
GROUPS_MAIN = [(0, 512, 0), (512, 256, 1)]
GROUPS_OTH = [(0, 512, 1), (512, 256, 1)]


class Trk:
    __slots__ = ("writer", "readers")

    def __init__(self):
        self.writer = None
        self.readers = {}


class Tile:
    __slots__ = ("ap", "trks", "split", "name")

    def __init__(self, ap, name="", split=None):
        self.ap = ap
        self.name = name
        self.split = split
        self.trks = [Trk()] if split is None else [Trk(), Trk()]

    def _sel(self, idx):
        if self.split is None:
            return self.trks
        if isinstance(idx, tuple) and len(idx) >= 2 and isinstance(idx[1], slice):
            a, b = idx[1].start, idx[1].stop
            if a is not None and b is not None:
                if b <= self.split:
                    return [self.trks[0]]
                if a >= self.split:
                    return [self.trks[1]]
        return self.trks

    def __getitem__(self, idx):
        return V(self._sel(idx), self.ap[idx])

    def v(self):
        return V(self.trks, self.ap)


class V:
    __slots__ = ("tiles", "ap")

    def __init__(self, tiles, ap):
        self.tiles = tiles
        self.ap = ap

    def __getitem__(self, idx):
        return V(self.tiles, self.ap[idx])

    def r(self):
        return V(self.tiles, self.ap.bitcast(F32R))

    def f(self):
        return V(self.tiles, self.ap.bitcast(F32))

    def re(self, s, **kw):
        return V(self.tiles, self.ap.rearrange(s, **kw))

    def bc(self, shape):
        return V(self.tiles, self.ap.to_broadcast(shape))


def _tiles(vs):
    out = []
    for v in vs:
        if v is None:
            continue
        if isinstance(v, Tile):
            out.extend(v.trks)
        else:
            out.extend(v.tiles)
    return out


class Ctx:
    def __init__(self, nc, dry=False):
        self.nc = nc
        self.dry = dry
        self.engs = {}
        self.sems = {}
        self.count = {}
        self.waited = {}
        self.stack = None
        self.n_dma_sem = 0
        self.n_ps = 0
        self.n_tp = 0
        self.log = {}

    def setup(self, stack):
        self.stack = stack
        nc = self.nc
        self.engs = {"pe": nc.tensor, "act": nc.scalar, "dve": nc.vector,
                     "pool": nc.gpsimd, "sp": nc.sync}
        for k in self.engs:
            self.sems[k] = stack.enter_context(nc.semaphore("s_" + k))
            self.count[k] = 0
            self.waited[k] = {}

    def new_dma_sem(self, name):
        key = "dma_%s_%d" % (name, self.n_dma_sem)
        self.n_dma_sem += 1
        self.sems[key] = self.stack.enter_context(self.nc.semaphore(key))
        self.count[key] = 0
        return key

    def _deps(self, ek, reads, writes):
        deps = {}

        def add(w):
            if w is None:
                return
            k, c = w
            if deps.get(k, 0) < c:
                deps[k] = c
        for t in reads:
            add(t.writer)
        for t in writes:
            add(t.writer)
            for k, c in t.readers.items():
                add((k, c))
        eng = self.engs[ek]
        for k, c in deps.items():
            if k == ek and (ek == "pe" or not SAME_ENGINE_SYNC):
                continue
            if k.startswith("dma_"):
                c = self.count[k]
            if self.waited[ek].get(k, 0) >= c:
                continue
            eng.wait_ge(self.sems[k], c)
            self.waited[ek][k] = c
            self.log.setdefault(ek, []).append(("w", k, c))

    def op(self, ek, fn, reads=(), writes=(), inc=True):
        rt = _tiles(reads)
        wt = _tiles(writes)
        if self.dry:
            return
        self._deps(ek, rt, wt)
        ins = fn(self.engs[ek])
        idx = self.count[ek] + 1
        if inc:
            ins.then_inc(self.sems[ek], 1)
            self.count[ek] = idx
        self.log.setdefault(ek, []).append(("i", ek if inc else None, 1))
        for t in wt:
            t.writer = (ek, idx)
            t.readers = {}
        for t in rt:
            if t.readers.get(ek, 0) < idx:
                t.readers[ek] = idx

    def dma(self, qk, out, in_, semkey, reads=(), writes=()):
        rt = _tiles(reads)
        wt = _tiles(writes)
        if self.dry:
            return
        self._deps(qk, rt, wt)
        self.count[semkey] += 16
        c = self.count[semkey]
        self.engs[qk].dma_start(out=out, in_=in_).then_inc(self.sems[semkey], 16)
        self.log.setdefault(qk, []).append(("i", semkey, 16))
        for t in wt:
            t.writer = (semkey, c)
            t.readers = {}
        for t in rt:
            t.readers[semkey] = c

    def group_fix(self, semkey, tiles):
        if self.dry:
            return
        for t in _tiles(tiles):
            t.writer = (semkey, self.count[semkey])

    def finish(self, ek, semkeys):
        if self.dry:
            return
        for k in semkeys:
            if self.count[k] > 0:
                self.engs[ek].wait_ge(self.sems[k], self.count[k])

    def mm(self, out, lhsT, rhs, start, stop, inc=None):
        self.op("pe", lambda e: e.matmul(out.ap, lhsT=lhsT.ap, rhs=rhs.ap, start=start, stop=stop),
                reads=[lhsT, rhs], writes=[out], inc=(True if inc is None else inc))

    def act(self, out, in_, func, bias=None, scale=1.0, eng="act"):
        rd = [in_]
        kw = {}
        if bias is not None:
            if isinstance(bias, V):
                rd.append(bias)
                kw["bias"] = bias.ap
            else:
                kw["bias"] = bias
        if isinstance(scale, V):
            rd.append(scale)
            kw["scale"] = scale.ap
        else:
            kw["scale"] = scale
        self.op("act", lambda e: e.activation(out=out.ap, in_=in_.ap, func=func, **kw), reads=rd, writes=[out])

    def tt(self, eng, out, a, b, op):
        self.op(eng, lambda e: e.tensor_tensor(out=out.ap, in0=a.ap, in1=b.ap, op=op), reads=[a, b], writes=[out])

    def ts(self, eng, out, a, s1, op0, s2=None, op1=None):
        rd = [a]
        if isinstance(s1, V):
            rd.append(s1)
        if isinstance(s2, V):
            rd.append(s2)
        a1 = s1.ap if isinstance(s1, V) else s1
        a2 = s2.ap if isinstance(s2, V) else s2
        if op1 is None:
            self.op(eng, lambda e: e.tensor_scalar(out=out.ap, in0=a.ap, scalar1=a1, scalar2=None, op0=op0),
                    reads=rd, writes=[out])
        else:
            self.op(eng, lambda e: e.tensor_scalar(out=out.ap, in0=a.ap, scalar1=a1, scalar2=a2, op0=op0, op1=op1),
                    reads=rd, writes=[out])

    def stt(self, eng, out, a, s, b, op0, op1):
        rd = [a, b]
        if isinstance(s, V):
            rd.append(s)
        sa = s.ap if isinstance(s, V) else s
        self.op(eng, lambda e: e.scalar_tensor_tensor(out=out.ap, in0=a.ap, scalar=sa, in1=b.ap, op0=op0, op1=op1),
                reads=rd, writes=[out])

    def copy(self, eng, out, in_):
        if eng == "act":
            self.op("act", lambda e: e.copy(out=out.ap, in_=in_.ap), reads=[in_], writes=[out])
        else:
            self.op(eng, lambda e: e.tensor_copy(out=out.ap, in_=in_.ap), reads=[in_], writes=[out])

    def recip(self, out, in_):
        self.op("dve", lambda e: e.reciprocal(out=out.ap, in_=in_.ap), reads=[in_], writes=[out])

    def memset(self, eng, out, val):
        self.op(eng, lambda e: e.memset(out.ap, val), writes=[out])


class WStream:
    NS = 5
    SLOT = 2048

    def __init__(self, cx, nc, stack, plan):
        self.cx = cx
        self.plan = plan if plan is not None else []
        self.record = plan is None
        self.cur = 0
        self.issued = 0
        self.hold = 1
        self.slots = []
        self.semk = []
        for i in range(self.NS):
            t = stack.enter_context(nc.sbuf_tensor("wslot%d" % i, [128, self.SLOT], F32R))
            self.slots.append(Tile(t[:], "wslot%d" % i))
            self.semk.append(cx.new_dma_sem("w%d" % i))

    def _view(self, i, shape):
        s = self.slots[i % self.NS]
        n = 1
        for d in shape[1:]:
            n *= d
        assert n <= self.SLOT, shape
        v = s[0:shape[0], 0:n]
        if len(shape) == 3:
            v = v.re("p (a b) -> p a b", a=shape[1])
        return v

    def get(self, dram_ap, shape):
        i = self.cur
        self.cur += 1
        if self.record:
            self.plan.append((dram_ap, tuple(shape)))
            return self._view(i, shape)
        assert self.plan[i][1] == tuple(shape), (i, self.plan[i][1], shape)
        lim = min(len(self.plan), i + self.NS - self.hold + 1)
        while self.issued < lim:
            j = self.issued
            ap_j, shp_j = self.plan[j]
            v = self._view(j, shp_j)
            self.cx.dma("sp", v.ap, ap_j.bitcast(F32R), self.semk[j % self.NS], writes=[v])
            self.issued += 1
        return self._view(i, shape)


class Prog:
    def __init__(self, nc, dry, plan, stage):
        self.nc = nc
        self.dry = dry
        self.plan = plan
        self.stage = stage
        self.cut = int(os.environ.get('MK_CUT', '999'))

    def dram_in(self, name, shape):
        return self.nc.dram_tensor(name, list(shape), F32, kind="ExternalInput").ap()

    def dram_out(self, name, shape):
        return self.nc.dram_tensor(name, list(shape), F32, kind="ExternalOutput").ap()

    def sb(self, name, shape, dt=F32):
        return self.st.enter_context(self.nc.sbuf_tensor("sb_" + name, list(shape), dt))

    def psum(self):
        t = self.ps[self.cx.n_ps % self.ps_rr_n]
        self.cx.n_ps += 1
        return t

    def psum_stat(self, gi):
        return self.ps[6 + gi]

    def tmp(self):
        t = self.tp[self.cx.n_tp % len(self.tp)]
        self.cx.n_tp += 1
        return t

    def tmpr(self):
        t = self.tpr[self.n_tpr % len(self.tpr)]
        self.n_tpr += 1
        return t

    def build(self):
        nc = self.nc
        with ExitStack() as st:
            self.st = st
            cx = self.cx = Ctx(nc, self.dry)
            cx.setup(st)
            self.W = WStream(cx, nc, st, self.plan)
            self.declare_io()
            self.alloc()
            self.load_consts()
            self.mod_pending = [(l, nb) for l in range(2) for nb in range(36)]
            self.prepass()
            G = GROUPS_MAIN
            self.load_x(self.d_xm)
            self.boundary(None, (0, 0), G)
            self.ffn(0, 0, G)
            self.boundary((0, 0), (0, 1), G)
            self.even_mixer()
            self.boundary((0, 1), (0, 2), G)
            self.ffn_feed_l1 = True
            self.ffn(0, 1, G)
            self.boundary((0, 2), (1, 0), G)
            self.ffn(1, 0, G)
            self.boundary((1, 0), (1, 1), G)
            self.odd_mixer()
            self.boundary((1, 1), (1, 2), G)
            self.ffn(1, 1, G)
            self.boundary((1, 2), None, G)
            self.store_out()
            cx.finish("sp", [self.dsem_out])
        return self.W.plan

    def declare_io(self):
        di = self.dram_in
        self.d_xm = di("xm", [128, KC, T])
        self.d_xo = di("xo", [128, KC, T])
        self.d_cond = di("cond", [128, KC, 2])
        self.d_wmod = di("w_mod", [2, D, 9 * D])
        self.d_bmod = di("b_mod", [128, 2 * 72])
        self.d_npre = di("npre", [128, 2 * 3 * KC])
        self.d_npost = di("npost", [128, 2 * 3 * KC])
        self.d_wg = di("ffn_w_gate", [2, 2, D, FF])
        self.d_wu = di("ffn_w_up", [2, 2, D, FF])
        self.d_wd = di("ffn_w_down", [2, 2, FF, D])
        self.d_ident = di("ident", [128, 128])
        self.d_winfm = di("w_in_fm", [D, 2688])
        self.d_wintm = di("w_in_tm", [D, 1280])
        self.d_wout = di("ev_w_out", [D, D])
        self.d_ropec = di("rope_c", [128, 512])
        self.d_ropes = di("rope_s", [128, 512])
        self.d_tri = di("tri", [128, 4 * 128])
        self.d_mm = di("gmask", [128, 4 * 128])
        self.d_fl = di("flags", [128, 6])
        self.d_ck = di("ctx_k", [128, 2 * 256])
        self.d_cv = di("ctx_v", [128, 2 * 2 * 128])
        self.d_s0f = di("s0f", [128, 256])
        self.d_s0b = di("s0b", [128, 256])
        self.d_sink = di("sinkT", [128, 4])
        self.d_gg = di("gla_g", [128, 4])
        self.d_wa = di("gla_wa", [64, 256])
        self.d_ba = di("gla_ba", [64, 256])
        self.d_cmwin = di("cm_w_in", [D, 2 * D])
        self.d_cmwout = di("cm_w_out", [D, D])
        self.d_cmgb = di("cm_gb", [128, 2 * D])
        self.d_cmws = di("cm_ws", [128, 512])
        self.d_cmbs = di("cm_bs", [128, 512])
        self.o_kv = self.dram_out("kv_out", [128, 4, 256])
        self.o_sf = self.dram_out("sf_out", [128, 2, 256])
        self.o_sb = self.dram_out("sb_out", [128, 2, 256])
        self.o_y = self.dram_out("ym", [128, KC, T])

    def alloc(self):
        nc, st, cx = self.nc, self.st, self.cx
        xt = self.sb("xT", [128, KC, T], F32)
        self.X = [Tile(xt[:, k, :], "x%d" % k, split=512) for k in range(KC)]
        ar = self.sb("arena", [128, 30, T], F32R)
        self.PG = [Tile(ar[:, i, :], "pg%d" % i, split=512) for i in range(30)]
        self.H = self.PG[0:8]
        self.ACT = self.PG[8:30]
        self.tp = [Tile(self.sb("tp%d" % i, [128, 512], F32)[:], "tp%d" % i) for i in range(3)]
        self.tpr = [Tile(self.sb("tpr%d" % i, [128, 512], F32R)[:], "tpr%d" % i) for i in range(4)]
        self.n_tpr = 0
        self.ps = [Tile(st.enter_context(nc.psum_tensor("ps%d" % i, [128, 512], F32))[:], "ps%d" % i) for i in range(8)]
        self.rstd = Tile(self.sb("rstd", [128, T], F32)[:], "rstd", split=512)
        self.cond = Tile(self.sb("cond", [128, KC, 2], F32)[:], "cond")
        self.scT = Tile(self.sb("scT", [128, KC, 2], F32R)[:], "scT")
        self.bm = Tile(self.sb("bm", [128, 2, 72], F32)[:], "bm")
        self.npre = Tile(self.sb("npre", [128, 2, 3, KC], F32)[:], "npre")
        self.npost = Tile(self.sb("npost", [128, 2, 3, KC], F32)[:], "npost")
        self.ident = Tile(self.sb("ident", [128, 128], F32)[:], "ident")
        self.onesF = Tile(self.sb("onesF", [128, 128], F32R)[:], "onesF")
        self.epsb = Tile(self.sb("epsb", [128, 1], F32)[:], "epsb")
        self.modT = Tile(self.sb("modT", [128, 2, 72, 2], F32)[:], "modT")
        mrt = self.sb("modrow", [2, 2, 256], F32)
        self.modrow = [Tile(mrt[:, i, :], "modrow%d" % i) for i in range(2)]
        self.mod_unfinished = None
        self.ffn_feed_l1 = False
        self.defer_B = None
        self.y_pending = []
        self.tab = Tile(self.sb("tab", [128, 2, 3, 2, 3, KC], F32)[:], "tab")
        self.tri = Tile(self.sb("tri", [128, 4, 128], F32)[:], "tri")
        self.MM = Tile(self.sb("gmask", [128, 4, 128], F32R)[:], "gmask")
        self.FL = Tile(self.sb("flags", [128, 6], F32)[:], "flags")
        self.CK = Tile(self.sb("ctxk", [128, 2, 256], F32R)[:], "ctxk")
        self.CV = Tile(self.sb("ctxv", [128, 2, 2, 128], F32R)[:], "ctxv")
        self.HK = Tile(self.sb("hk", [128, 2, 2, 128], F32R)[:], "hk")
        self.HV = Tile(self.sb("hv", [128, 2, 256], F32R)[:], "hv")
        self.S = [Tile(self.sb("st%d" % i, [128, 2, 128], F32R)[:], "st%d" % i) for i in range(7)]
        self.FIN = [Tile(self.sb("fin%d" % i, [128, 2, 2, 128], F32)[:], "fin%d" % i) for i in range(2)]
        self.sinkT = Tile(self.sb("sinkT", [128, 4], F32)[:], "sinkT")
        self.esink = Tile(self.sb("esink", [128, 4], F32)[:], "esink")
        self.gg = Tile(self.sb("gg", [128, 4], F32)[:], "gg")
        self.WA = Tile(self.sb("wa", [64, 256], F32R)[:], "wa")
        self.BA = Tile(self.sb("ba", [64, 256], F32R)[:], "ba")
        self.DEC = Tile(self.sb("dec", [128, 12, 2], F32)[:], "dec")
        self.DM = Tile(self.sb("dm", [128, 12, 2], F32)[:], "dm")
        lt = self.sb("lnst", [128, 6, 40], F32)
        self.lnst = [Tile(lt[:, i, :], "lnst%d" % i) for i in range(6)]
        self.lnst_all = lt[:]
        self.ps_rr_n = 6
        self.n_ev = 0
        self.dsem_c = cx.new_dma_sem("const")
        self.dsem_c2 = cx.new_dma_sem("const2")
        self.dsem_cs = cx.new_dma_sem("const_sp")
        self.dsem_x = cx.new_dma_sem("x")
        self.dsem_out = cx.new_dma_sem("out")

    def load_consts(self):
        cx = self.cx
        q = "pool"
        dsem_cond = cx.new_dma_sem("cond")
        cx.dma("sp", self.cond.ap, self.d_cond, dsem_cond, writes=[self.cond])
        cx.act(self.scT.v(), self.cond.v(), AF.Silu)
        cx.dma(q, self.npre.ap, self.d_npre.rearrange("p (l i k) -> p l i k", l=2, i=3), self.dsem_c, writes=[self.npre])
        cx.dma(q, self.npost.ap, self.d_npost.rearrange("p (l i k) -> p l i k", l=2, i=3), self.dsem_c, writes=[self.npost])
        cx.dma(q, self.ident.ap, self.d_ident, self.dsem_c, writes=[self.ident])
        cx.dma(q, self.bm.ap, self.d_bmod.rearrange("p (l c) -> p l c", l=2), self.dsem_c, writes=[self.bm])
        for (t, d, shp) in [(self.tri, self.d_tri, "p (a b) -> p a b"), (self.FL, self.d_fl, None), (self.sinkT, self.d_sink, None),
                            (self.gg, self.d_gg, None)]:
            src = d if shp is None else d.rearrange(shp, a=4)
            cx.dma(q, t.ap, src, self.dsem_c, writes=[t])
        cx.dma("sp", self.MM.ap, self.d_mm.rearrange("p (a b) -> p a b", a=4).bitcast(F32R), self.dsem_cs, writes=[self.MM])
        cx.dma("sp", self.CK.ap, self.d_ck.rearrange("p (a b) -> p a b", a=2).bitcast(F32R), self.dsem_cs, writes=[self.CK])
        cx.dma("sp", self.CV.ap, self.d_cv.rearrange("p (a b c) -> p a b c", a=2, b=2).bitcast(F32R), self.dsem_cs, writes=[self.CV])
        cx.dma("sp", self.S[0].ap, self.d_s0f.rearrange("p (a b) -> p a b", a=2).bitcast(F32R), self.dsem_cs, writes=[self.S[0]])
        cx.dma("sp", self.S[1].ap, self.d_s0b.rearrange("p (a b) -> p a b", a=2).bitcast(F32R), self.dsem_cs, writes=[self.S[1]])
        cx.dma("sp", self.WA.ap, self.d_wa.bitcast(F32R), self.dsem_cs, writes=[self.WA])
        cx.dma("sp", self.BA.ap, self.d_ba.bitcast(F32R), self.dsem_cs, writes=[self.BA])
        cx.group_fix(self.dsem_c, [self.npre, self.npost, self.ident, self.bm, self.tri, self.FL, self.sinkT, self.gg])
        cx.group_fix(self.dsem_cs, [self.MM, self.CK, self.CV, self.S[0], self.S[1], self.WA, self.BA])
        cx.memset("dve", self.FIN[0].v(), 0.0)
        cx.copy("dve", self.S[6].v(), self.FIN[0][:, 0, :, :])
        cx.act(self.esink.v(), self.sinkT.v(), AF.Exp)
        cx.memset("dve", self.tp[0][:, 0:128], 1.0)
        cx.copy("dve", self.onesF.v(), self.tp[0][:, 0:128])
        cx.memset("dve", self.epsb.v(), EPS)

    def mod_step(self, n=1):
        for _ in range(n):
            if not self.mod_pending:
                return
            l, nb = self.mod_pending.pop(0)
            self.mod_block(l, nb)
            if nb % 12 == 11:
                self.mod_tab(l, nb // 12)

    def mod_step_ffn(self):
        if self.mod_pending and (self.mod_pending[0][0] == 0 or self.ffn_feed_l1):
            self.mod_step()

    def mod_need(self, l, i):
        while self.mod_pending and (self.mod_pending[0][0] < l or
                                    (self.mod_pending[0][0] == l and self.mod_pending[0][1] < 12 * (i + 1))):
            self.mod_step()

    def mod_flush(self):
        while self.mod_pending:
            self.mod_step()

    def mod_block(self, l, nb):
        cx = self.cx
        Wt = self.W.get(self.d_wmod[l].rearrange("(kc p) n -> p kc n", p=128)[:, :, nb * 256:(nb + 1) * 256], [128, KC, 256])
        ps = self.psum()
        for kc in range(KC):
            cx.mm(ps[0:2, 0:256], self.scT[:, kc, :], Wt[:, kc, :], start=(kc == 0), stop=(kc == KC - 1), inc=(kc == KC - 1))
        mr = self.modrow[nb % 2]
        cx.copy("act", mr.v(), ps[0:2, 0:256])
        self.mod_finish()
        self.mod_unfinished = (l, nb)
        if nb % 12 == 11:
            self.mod_finish()

    def mod_finish(self):
        cx = self.cx
        if self.mod_unfinished is None:
            return
        l, nb = self.mod_unfinished
        self.mod_unfinished = None
        mr = self.modrow[nb % 2]
        pt = self.psum()
        for hf in range(2):
            cx.op("pe", lambda e, hf=hf, pt=pt: e.transpose(out=pt.ap[:, hf * 2:hf * 2 + 2],
                                                        in_=mr.ap[0:2, hf * 128:(hf + 1) * 128],
                                                        identity=self.ident.ap[0:2, 0:2]),
                  reads=[mr, self.ident], writes=[pt], inc=(hf == 1))
        for r in range(2):
            cx.tt("dve", self.modT[:, l, nb * 2:nb * 2 + 2, r], pt[:, 0:4].re("p (a b) -> p a b", a=2)[:, :, r],
                  self.bm[:, l, nb * 2:nb * 2 + 2], ALU.add)

    def mod_tab(self, l, i):
        cx = self.cx
        if True:
            coef = 1.0 if i == 1 else 0.5
            for r in range(2):
                sh = self.modT[:, l, i * 24 + 0:i * 24 + 8, r]
                sc = self.modT[:, l, i * 24 + 8:i * 24 + 16, r]
                gt = self.modT[:, l, i * 24 + 16:i * 24 + 24, r]
                cx.stt("dve", self.tab[:, l, i, r, 0, :], sc, 1.0, self.npre[:, l, i, :], ALU.add, ALU.mult)
                cx.copy("dve", self.tab[:, l, i, r, 1, :], sh)
                cx.stt("dve", self.tab[:, l, i, r, 2, :], gt, coef, self.npost[:, l, i, :], ALU.mult, ALU.mult)

    def load_x(self, d_x):
        cx = self.cx
        for k in range(KC):
            cx.dma("pool", self.X[k].ap, d_x[:, k, :], self.dsem_x, writes=[self.X[k]])
        cx.group_fix(self.dsem_x, self.X)

    def boundary(self, out_li, in_li, groups):
        if in_li is not None:
            self.mod_need(*in_li)
        self.run_deferred()
        for gi in range(2):
            self._bg_out(gi, out_li, in_li, groups)
        for gi in range(2):
            self._bg_in(gi, in_li, groups, part=0)
        for gi in range(2):
            self._bg_in(gi, in_li, groups, part=1)

    def run_deferred(self):
        pass

    def _bg_out(self, gi, out_li, in_li, groups):
        cx = self.cx
        SQ = self.PG[8:16]
        t0, tn, r = groups[gi]
        if out_li is not None:
            prev = None
            for kc in range(KC):
                tq = self.tmp()
                cx.tt("dve", tq[:, 0:tn], self.H[kc][:, t0:t0 + tn].f(), self.rstd[:, t0:t0 + tn], ALU.mult)
                if prev is not None:
                    pk, ptq = prev
                    cx.tt("dve", self.X[pk][:, t0:t0 + tn], ptq[:, 0:tn], self.X[pk][:, t0:t0 + tn], ALU.add)
                prev = (kc, tq)
            pk, ptq = prev
            cx.tt("dve", self.X[pk][:, t0:t0 + tn], ptq[:, 0:tn], self.X[pk][:, t0:t0 + tn], ALU.add)
        if in_li is not None:
            for kc in range(KC):
                cx.act(SQ[kc][:, t0:t0 + tn], self.X[kc][:, t0:t0 + tn], AF.Square)

    def _bg_in(self, gi, in_li, groups, part):
        cx = self.cx
        if in_li is None:
            return
        SQ = self.PG[8:16]
        t0, tn, r = groups[gi]
        li, ii = in_li
        pss = self.psum_stat(gi)
        if part == 0:
            for kc in range(KC):
                cx.mm(pss[:, 0:tn], self.onesF.v(), SQ[kc][:, t0:t0 + tn], start=(kc == 0), stop=(kc == KC - 1),
                      inc=(kc == KC - 1))
            self.rsqrt(self.rstd[:, t0:t0 + tn], pss[:, 0:tn], 1.0 / D, tn)
            return
        for kc in range(KC):
            tq = self.tmp()
            cx.tt("dve", tq[:, 0:tn], self.X[kc][:, t0:t0 + tn], self.rstd[:, t0:t0 + tn], ALU.mult)
            cx.act(self.H[kc][:, t0:t0 + tn], tq[:, 0:tn], AF.Identity, bias=self.tab[:, li, ii, r, 1, kc:kc + 1],
                   scale=self.tab[:, li, ii, r, 0, kc:kc + 1])

    def rsqrt(self, out, src, scale, tn):
        cx = self.cx
        tq = self.tmp()
        cx.act(tq[:, 0:tn], src, AF.Ln, bias=self.epsb[:, 0:1], scale=scale)
        cx.act(out, tq[:, 0:tn], AF.Exp, scale=-0.5)

    def ffn(self, l, s, groups):
        cx = self.cx
        self.cur_out = (l, 0 if s == 0 else 2)
        self.cur_groups = groups
        self.mod_need(l, 0 if s == 0 else 2)
        wg = self.d_wg[l, s].rearrange("(kc p) n -> p kc n", p=128)
        wu = self.d_wu[l, s].rearrange("(kc p) n -> p kc n", p=128)
        wd = self.d_wd[l, s].rearrange("(fc p) n -> p fc n", p=128)
        self.ps_rr_n = 8
        self.W.hold = 2
        for blk in range(11):
            Wg = self.W.get(wg[:, :, blk * 256:(blk + 1) * 256], [128, KC, 256])
            Wu = self.W.get(wu[:, :, blk * 256:(blk + 1) * 256], [128, KC, 256])
            for jj in range(2):
                fj = blk * 2 + jj
                for (t0, tn, _) in groups:
                    pg = self.psum()
                    pu = self.psum()
                    for kc in range(KC):
                        cx.mm(pg[:, 0:tn], Wg[:, kc, jj * 128:(jj + 1) * 128], self.H[kc][:, t0:t0 + tn],
                              start=(kc == 0), stop=(kc == KC - 1), inc=(kc == KC - 1))
                    for kc in range(KC):
                        cx.mm(pu[:, 0:tn], Wu[:, kc, jj * 128:(jj + 1) * 128], self.H[kc][:, t0:t0 + tn],
                              start=(kc == 0), stop=(kc == KC - 1), inc=(kc == KC - 1))
                    sl = self.tmp()
                    cx.act(sl[:, 0:tn], pg[:, 0:tn], AF.Silu)
                    cx.tt("dve", self.ACT[fj][:, t0:t0 + tn], sl[:, 0:tn], pu[:, 0:tn], ALU.mult)
            self.mod_step_ffn()
        self.ps_rr_n = 6
        Y = self.H
        for dc in range(KC):
            Wd0 = self.W.get(wd[:, 0:11, dc * 128:(dc + 1) * 128], [128, 11, 128])
            Wd1 = self.W.get(wd[:, 11:22, dc * 128:(dc + 1) * 128], [128, 11, 128])
            for gi, (t0, tn, _) in enumerate(groups):
                py = self.psum()
                for fc in range(FC):
                    wv_ = Wd0[:, fc, :] if fc < 11 else Wd1[:, fc - 11, :]
                    cx.mm(py[:, 0:tn], wv_, self.ACT[fc][:, t0:t0 + tn], start=(fc == 0), stop=(fc == FC - 1),
                          inc=(fc == FC - 1))
                self.y_evac(dc, t0, tn, gi, py[:, 0:tn], dc == 0, dc == KC - 1)
            self.mod_step_ffn()
        self.W.hold = 1
        self.y_finish(groups)


    def fm_linear(self, wv, col0, nchunks, src, tok_ranges, evac):
        cx = self.cx
        nk = len(src)
        j = 0
        while j < nchunks:
            nb = min(2, nchunks - j)
            Wt = self.W.get(wv[:, :, col0 + j * 128: col0 + (j + nb) * 128], [128, nk, nb * 128])
            for jj in range(nb):
                for (t0, tn) in tok_ranges:
                    ps = self.psum()
                    for kc in range(nk):
                        cx.mm(ps[:, 0:tn], Wt[:, kc, jj * 128:(jj + 1) * 128], src[kc][:, t0:t0 + tn],
                              start=(kc == 0), stop=(kc == nk - 1), inc=(kc == nk - 1))
                    evac(j + jj, t0, tn, ps[:, 0:tn])
            j += nb
            self.mod_step(1)

    def tm_linear(self, wv, col0, ncols, src, tiles, evac):
        cx = self.cx
        nk = len(src)
        c = 0
        while c < ncols:
            bw = min(256, ncols - c)
            Wt = self.W.get(wv[:, :, col0 + c: col0 + c + bw], [128, nk, bw])
            for tt in tiles:
                if tt >= 4:
                    self.run_deferred()
                ps = self.psum()
                for kc in range(nk):
                    cx.mm(ps[:, 0:bw], src[kc][:, tt * 128:(tt + 1) * 128], Wt[:, kc, :],
                          start=(kc == 0), stop=(kc == nk - 1), inc=(kc == nk - 1))
                evac(c, bw, tt, ps[:, 0:bw])
            c += bw
            self.mod_step(1)

    def ev_copy(self, dst, src):
        self.n_ev += 1
        self.cx.copy("act" if self.n_ev % 2 else "dve", dst, src)

    def y_evac(self, dc, t0, tn, gi, ps, first, last):
        cx = self.cx
        Y = self.H
        lo, io = self.cur_out
        r = self.cur_groups[gi][2]
        cx.act(Y[dc][:, t0:t0 + tn], ps, AF.Copy, scale=self.tab[:, lo, io, r, 2, dc:dc + 1])
        sq = self.tmpr()
        cx.act(sq[:, 0:tn], ps, AF.Square)
        self.y_pending.append((gi, sq, tn, first, last))
        while len(self.y_pending) > 3:
            self.y_flush1()

    def y_flush1(self):
        gi, sq, tn, first, last = self.y_pending.pop(0)
        self.cx.mm(self.psum_stat(gi)[:, 0:tn], self.onesF.v(), sq[:, 0:tn], start=first, stop=last)

    def y_finish(self, groups):
        cx = self.cx
        while self.y_pending:
            self.y_flush1()
        for gi, (t0, tn, _) in enumerate(groups):
            self.rsqrt(self.rstd[:, t0:t0 + tn], self.psum_stat(gi)[:, 0:tn], 1.0 / D, tn)

    def out_linear(self, wdram, src, l):
        wv = wdram.rearrange("(kc p) n -> p kc n", p=128)
        gidx = {512: 0, 256: 1}
        self.cur_out = (l, 1)
        self.cur_groups = GROUPS_MAIN
        self.mod_need(l, 1)

        def ev(j, t0, tn, ps):
            self.y_evac(j, t0, tn, gidx[tn], ps, j == 0, j == KC - 1)
        self.fm_linear(wv, 0, KC, src, [(0, 512), (512, 256)], ev)
        self.y_finish(GROUPS_MAIN)

    def bcast(self, v, n):
        a = v.ap
        return V(v.tiles, bass.AP(a.tensor, a.offset, [list(a.ap[0]), [0, n], list(a.ap[1])]))

    def gla_ld(self, LR, tt, ldpage):
        cx = self.cx
        for dr in range(2):
            ps = self.psum()
            rows = slice(32 * dr, 32 * dr + 16)
            r1 = slice(32 * dr, 32 * dr + 1)
            cx.mm(ps[:, 0:256], LR[rows, tt * 128:(tt + 1) * 128], self.WA[rows, :], True, False, inc=False)
            cx.mm(ps[:, 0:256], self.onesF[r1, 0:128], self.BA[r1, :], False, True)
            e = self.tmp()
            cx.act(e[:, 0:256], ps[:, 0:256], AF.Exp, scale=-1.0)
            cx.act(ldpage[:, dr * 256:(dr + 1) * 256], e[:, 0:256], AF.Ln, bias=1.0)

    def gla_prep(self, dr, kvp, ld, gp, slot, need_q, QB=None, KB=None, tok0=0, flag=None):
        cx = self.cx
        mc = 1 if dr == 0 else 3
        mb = 0 if dr == 0 else 2
        ldv = ld[:, dr * 256:(dr + 1) * 256]
        ps = self.psum()
        cx.mm(ps[:, 0:256], self.MM[:, mc, :], ldv, True, True)
        ec = self.tmp()
        cx.act(ec[:, 0:256], ps[:, 0:256], AF.Exp, scale=-1.0 / 16)
        if flag is not None:
            cx.stt("dve", gp[:, 0:256], kvp[:, 0:256].f(), flag, ec[:, 0:256], ALU.mult, ALU.mult)
        else:
            cx.tt("dve", gp[:, 0:256], kvp[:, 0:256].f(), ec[:, 0:256], ALU.mult)
        ps2 = self.psum()
        if need_q:
            for pc in range(2):
                cx.mm(ps2[:, pc * 128:(pc + 1) * 128], ldv[:, pc * 128:(pc + 1) * 128], self.MM[:, mb, :], True, True)
            eb = self.tmp()
            cx.act(eb[:, 0:256], ps2[:, 0:256], AF.Exp, scale=-1.0 / 16, bias=float(np.log(0.125)))
            for pc in range(2):
                cx.tt("dve", gp[:, 256 + pc * 128:256 + (pc + 1) * 128], QB[pc][:, tok0:tok0 + 128].f(),
                      eb[:, pc * 128:(pc + 1) * 128], ALU.mult)
            enb = self.tmp()
            cx.act(enb[:, 0:256], ps2[:, 0:256], AF.Exp, scale=1.0 / 16)
            for pc in range(2):
                cx.tt("dve", gp[:, 512 + pc * 128:512 + (pc + 1) * 128], KB[pc][:, tok0:tok0 + 128].f(),
                      enb[:, pc * 128:(pc + 1) * 128], ALU.mult)
            col = 127 if dr == 0 else 0
            cx.act(self.DEC[:, slot, :], ps2[:, 0:256].re("p (a b) -> p a b", a=2)[:, :, col], AF.Exp, scale=-1.0 / 16)
        else:
            for pc in range(2):
                cx.mm(ps2[:, pc * 2:pc * 2 + 2], ldv[:, pc * 128:(pc + 1) * 128], self.onesF[:, 0:2], True, True)
            cx.act(self.DEC[:, slot, :], ps2[:, 0:4].re("p (a b) -> p a b", a=2)[:, :, 0], AF.Exp, scale=-1.0 / 16)

    def gla_update(self, Sprev, Snew, gp, kvp, dec, flag=None, out_f32=None):
        cx = self.cx
        U = self.psum()
        for pc in range(2):
            for e in range(2):
                h = 2 * pc + e
                cx.mm(U[:, h * 128:(h + 1) * 128], gp[:, pc * 128:(pc + 1) * 128], kvp[:, 256 + h * 128:256 + (h + 1) * 128], True, True)
        for pc in range(2):
            for e in range(2):
                h = 2 * pc + e
                rows = slice(64 * e, 64 * e + 64)
                dst = Snew[rows, pc, :] if out_f32 is None else out_f32[rows, pc, :]
                cx.stt("dve", dst, Sprev[rows, pc, :].f(), dec[rows, pc:pc + 1], U[rows, h * 128:(h + 1) * 128], ALU.mult, ALU.add)

    def gla_out_a(self, gpf, gpb):
        cx = self.cx
        ats = []
        for dr, gp in ((0, gpf), (1, gpb)):
            mb = 0 if dr == 0 else 2
            Ae = [self.psum(), self.psum()]
            for h in range(4):
                pc, e = h // 2, h % 2
                rows = slice(64 * e, 64 * e + 64)
                cx.mm(Ae[e][:, pc * 128:(pc + 1) * 128], gp[rows, 512 + pc * 128:512 + (pc + 1) * 128],
                      gp[rows, 256 + pc * 128:256 + (pc + 1) * 128], True, True)
            at = self.tmpr()
            atv = at[:, 0:512].re("p (a e b) -> p a e b", a=2, e=2)
            for e in range(2):
                cx.tt("dve", atv[:, :, e, :], Ae[e][:, 0:256].re("p (a b) -> p a b", a=2),
                      self.bcast(self.MM[:, mb, :].f(), 2), ALU.mult)
            ats.append(at)
        return ats

    def gla_out(self, kvp, gpf, gpb, Sf, Sb, dst_pages, tok0, ats):
        cx = self.cx
        PI = self.psum()
        for h in range(4):
            o = PI[:, h * 128:(h + 1) * 128]
            vb = kvp[:, 256 + h * 128:256 + (h + 1) * 128]
            cx.mm(o, vb, ats[0][:, h * 128:(h + 1) * 128], True, False, inc=False)
            cx.mm(o, vb, ats[1][:, h * 128:(h + 1) * 128], False, True)
        PX = [self.psum(), self.psum()]
        for h in range(4):
            pc, e = h // 2, h % 2
            rows = slice(64 * e, 64 * e + 64)
            o = PX[e][:, pc * 128:(pc + 1) * 128]
            cx.mm(o, Sf[rows, pc, :], gpf[rows, 256 + pc * 128:256 + (pc + 1) * 128], True, False, inc=False)
            cx.mm(o, Sb[rows, pc, :], gpb[rows, 256 + pc * 128:256 + (pc + 1) * 128], False, True)
        for h in range(4):
            pc, e = h // 2, h % 2
            tc_ = self.tmp()
            cx.copy("act", tc_[:, 0:128], PI[:, h * 128:(h + 1) * 128])
            cx.tt("dve", dst_pages[h][:, tok0:tok0 + 128], tc_[:, 0:128], PX[e][:, pc * 128:(pc + 1) * 128], ALU.add)

    def load_rope(self, pc_page, ps_page):
        cx = self.cx
        cx.dma("sp", pc_page[:, 0:512].ap, self.d_ropec.bitcast(F32R), self.dsem_c2, writes=[pc_page])
        cx.dma("sp", ps_page[:, 0:512].ap, self.d_ropes.bitcast(F32R), self.dsem_c2, writes=[ps_page])
        cx.group_fix(self.dsem_c2, [pc_page, ps_page])

    def prepass(self):
        cx = self.cx
        PG = self.PG
        self.load_x(self.d_xo)
        self.boundary(None, (0, 0), GROUPS_OTH)
        self.ffn(0, 0, GROUPS_OTH)
        self.boundary((0, 0), (0, 1), GROUPS_OTH)
        if self.cut <= 1:
            return
        wfm = self.d_winfm.rearrange("(kc p) n -> p kc n", p=128)
        wtm = self.d_wintm.rearrange("(kc p) n -> p kc n", p=128)
        LR, KDh, KSh, RC, RS = PG[8], PG[9], PG[10], PG[14], PG[15]
        KVp = PG[16:22]
        LD = PG[22:28]
        GP = [PG[11], PG[12], PG[13], PG[28]]
        self.load_rope(RC, RS)
        allr = [(0, 512), (512, 256)]
        halo = [(0, 128), (640, 128)]
        self.fm_linear(wfm, 10 * 128, 1, self.H, allr, lambda j, t0, tn, ps: self.ev_copy(LR[:, t0:t0 + tn], ps))

        def ev_h(dst):
            def f(j, t0, tn, ps):
                hs = 0 if t0 == 0 else 1
                self.ev_copy(dst[:, j * 256 + hs * 128: j * 256 + (hs + 1) * 128], ps)
            return f
        self.fm_linear(wfm, 4 * 128, 2, self.H, halo, ev_h(KDh))
        self.fm_linear(wfm, 19 * 128, 2, self.H, halo, ev_h(KSh))
        if self.cut <= 2:
            return
        for kv in range(2):
            t1 = self.tmp()
            t2 = self.tmp()
            cx.tt("dve", t1[:, 0:256], KDh[:, kv * 256:(kv + 1) * 256].f(), RC[:, 256:512].f(), ALU.mult)
            cx.tt("dve", t2[:, 0:256], KSh[:, kv * 256:(kv + 1) * 256].f(), RS[:, 256:512].f(), ALU.mult)
            cx.tt("dve", self.HK[:, kv, :, :], t1[:, 0:256].re("p (a b) -> p a b", a=2), t2[:, 0:256].re("p (a b) -> p a b", a=2), ALU.add)
        self.tm_linear(wtm, 0, 256, self.H, [0, 5],
                       lambda c, bw, tt, ps: self.ev_copy(self.HV[:, 0 if tt == 0 else 1, :], ps))
        if self.cut <= 3:
            return
        self.tm_linear(wtm, 512, 768, self.H, list(range(6)),
                       lambda c, bw, tt, ps: self.ev_copy(KVp[tt][:, c:c + bw], ps))
        for tt in range(6):
            self.gla_ld(LR, tt, LD[tt])
        if self.cut <= 4:
            return
        GPs = [PG[11], PG[12], PG[13], PG[28]]

        def kslot(n):
            return GPs[n // 3][:, (n % 3) * 256:(n % 3 + 1) * 256]
        jobs = []
        for dr in range(2):
            order = [0, 1, 2, 3, 4, 5] if dr == 0 else [5, 4, 3, 2, 1, 0]
            for n, tt in enumerate(order):
                jobs.append((dr, n, tt))
        for (dr, n, tt) in jobs:
            sl = dr * 6 + n
            flag = self.FL[:, dr * 3 + tt // 2: dr * 3 + tt // 2 + 1]
            self.gla_prep(dr, KVp[tt], LD[tt], kslot(sl), sl, False, flag=flag)
            cx.ts("dve", self.DM[:, sl, :], self.DEC[:, sl, :], -1.0, ALU.add, flag, ALU.mult)
            cx.ts("dve", self.DM[:, sl, :], self.DM[:, sl, :], 1.0, ALU.add)
        for dr in range(2):
            Sprev = self.S[dr]
            pp = [self.S[4], self.S[5]]
            for (d2, n, tt) in jobs:
                if d2 != dr:
                    continue
                sl = dr * 6 + n
                Snew = self.S[2 + dr] if n == 5 else pp[n % 2]
                self.gla_update(Sprev, Snew, kslot(sl), KVp[tt], self.DM[:, sl, :])
                Sprev = Snew

    def even_mixer(self):
        cx = self.cx
        PG = self.PG
        wfm = self.d_winfm.rearrange("(kc p) n -> p kc n", p=128)
        wtm = self.d_wintm.rearrange("(kc p) n -> p kc n", p=128)
        CAT = PG[8:16]
        QA, KD, SW, VT, PT = PG[16:20], PG[20:22], PG[22:24], PG[24:26], PG[26:29]
        RC, RS = PG[14], PG[15]
        allr = [(0, 512), (512, 256)]
        samp = [(512, 256)]
        self.load_rope(RC, RS)
        self.fm_linear(wfm, 0, 4, self.H, allr, lambda j, t0, tn, ps: self.ev_copy(QA[j][:, t0:t0 + tn], ps))
        self.fm_linear(wfm, 4 * 128, 2, self.H, allr, lambda j, t0, tn, ps: self.ev_copy(KD[j][:, t0:t0 + tn], ps))

        def swv(i):
            return SW[i // 3][:, (i % 3) * 256:(i % 3 + 1) * 256]
        self.fm_linear(wfm, 15 * 128, 4, self.H, samp, lambda j, t0, tn, ps: self.ev_copy(swv(j), ps))
        self.fm_linear(wfm, 19 * 128, 2, self.H, samp, lambda j, t0, tn, ps: self.ev_copy(swv(4 + j), ps))
        if self.cut <= 10:
            return
        for i in range(6):
            dst = (QA[i] if i < 4 else KD[i - 4])[:, 512:768]
            t1 = self.tmp()
            t2 = self.tmp()
            cx.tt("dve", t1[:, 0:256], dst.f(), RC[:, 0:256].f(), ALU.mult)
            cx.tt("dve", t2[:, 0:256], swv(i).f(), RS[:, 0:256].f(), ALU.mult)
            cx.tt("dve", dst, t1[:, 0:256], t2[:, 0:256], ALU.add)

        def vtv(tt):
            return VT[tt // 3][:, (tt % 3) * 256:(tt % 3 + 1) * 256]

        def ev_tm_v(c, bw, tt, ps):
            self.ev_copy(vtv(tt), ps)
            if tt < 4:
                for kv in range(2):
                    cx.dma("sp", self.o_kv[:, tt, 128 + kv * 64:128 + (kv + 1) * 64], vtv(tt)[:, kv * 128:kv * 128 + 64].f().ap,
                           self.dsem_out, reads=[vtv(tt)])

        def ev_tm_k(c, bw, tt, ps):
            tk = self.tmp()
            self.ev_copy(tk[:, 0:128], ps[:, 0:128])
            cx.dma("sp", self.o_kv[:, tt, 0:128], tk[:, 0:128].ap, self.dsem_out, reads=[tk])
        self.tm_linear(wtm, 0, 256, self.H, list(range(6)), ev_tm_v)
        self.tm_linear(wtm, 256, 256, self.H, list(range(4)), ev_tm_k)
        if self.cut <= 12:
            return

        self.ps_rr_n = 4
        ring = [PG[26], PG[27], PG[28], PG[29], PG[12], PG[13], PG[22], PG[23]]
        rn = [0]

        def ptile():
            t = ring[rn[0] % len(ring)]
            rn[0] += 1
            return t

        def finalize(c, c0, po, den):
            for e in range(2):
                rows = slice(64 * e, 64 * e + 64)
                t1 = self.tmp()
                cx.act(t1[rows, 0:256], den[e][rows, 0:256], AF.Ln, bias=self.esink[rows, c:c + 1])
                cx.act(t1[rows, 0:256], t1[rows, 0:256], AF.Exp, scale=-1.0)
                cx.tt("dve", CAT[c][rows, c0:c0 + 256], po[e][rows, 0:256], t1[rows, 0:256], ALU.mult)

        units = [(sq, c) for sq in range(2) for c in range(4)]

        def scores(u):
            sq, c = units[u]
            c0, kv = sq * 256, c // 2
            Pk = [ptile(), ptile()]
            for kt in range(2):
                for e in range(2):
                    rows = slice(64 * e, 64 * e + 64)
                    Sp = self.psum()
                    cx.mm(Sp[:, 0:256], KD[kv][rows, c0 + kt * 128:c0 + (kt + 1) * 128], QA[c][rows, c0:c0 + 256], True, True)
                    cx.act(Pk[kt][:, e * 256:(e + 1) * 256], Sp[:, 0:256], AF.Exp, scale=0.125)
            return Pk

        def pv(u, Pk):
            sq, c = units[u]
            c0, kv = sq * 256, c // 2
            bank = self.ps[4 + 2 * (u % 2)]
            bank2 = self.ps[5 + 2 * (u % 2)]
            po = [bank[:, 0:256], bank[:, 256:512]]
            den = [bank2[:, 0:256], bank2[:, 256:512]]
            for e in range(2):
                for kt in range(2):
                    vt = c0 // 128 + kt
                    cx.mm(po[e], vtv(vt)[:, kv * 128:(kv + 1) * 128], Pk[kt][:, e * 256:(e + 1) * 256], kt == 0, kt == 1, inc=(kt == 1))
            for e in range(2):
                for kt in range(2):
                    cx.mm(den[e], self.onesF.v(), Pk[kt][:, e * 256:(e + 1) * 256], kt == 0, kt == 1, inc=(kt == 1))
            finalize(c, c0, po, den)

        pend = scores(0)
        for u in range(len(units)):
            nxt = scores(u + 1) if u + 1 < len(units) else None
            pv(u, pend)
            pend = nxt

        c0 = 512
        for c in range(4):
            kv = c // 2
            po = [self.ps[4][:, 0:256], self.ps[5][:, 0:256]]
            den = [self.ps[6][:, 0:256], self.ps[7][:, 0:256]]
            Ps = []
            for kt in range(6):
                P = ptile()
                Ps.append(P)
                q0, qn = (0, 128) if kt == 2 else ((128, 128) if kt == 5 else (0, 256))
                for e in range(2):
                    rows = slice(64 * e, 64 * e + 64)
                    if kt < 2:
                        kl = self.CK[rows, kv, kt * 128:(kt + 1) * 128]
                    elif kt == 2:
                        kl = self.HK[rows, kv, 1, :]
                    elif kt == 5:
                        kl = self.HK[rows, kv, 0, :]
                    else:
                        kl = KD[kv][rows, c0 + (kt - 3) * 128:c0 + (kt - 2) * 128]
                    Sp = self.psum()
                    cx.mm(Sp[:, 0:qn], kl, QA[c][rows, c0 + q0:c0 + q0 + qn], True, True)
                    cx.act(P[:, e * 256 + q0:e * 256 + q0 + qn], Sp[:, 0:qn], AF.Exp, scale=0.125)
                Pv = P[:, 0:512].re("p (a b) -> p a b", a=2)
                lo, hi = Pv[:, :, 0:128], Pv[:, :, 128:256]
                if kt == 2:
                    cx.tt("pool", lo, lo.f(), self.bcast(self.tri[:, 2, :], 2), ALU.mult)
                elif kt == 3:
                    cx.tt("pool", hi, hi.f(), self.bcast(self.tri[:, 0, :], 2), ALU.mult)
                elif kt == 4:
                    cx.tt("pool", lo, lo.f(), self.bcast(self.tri[:, 1, :], 2), ALU.mult)
                elif kt == 5:
                    cx.tt("pool", hi, hi.f(), self.bcast(self.tri[:, 3, :], 2), ALU.mult)
            for kt in range(6):
                P = Ps[kt]
                q0, qn = (0, 128) if kt == 2 else ((128, 128) if kt == 5 else (0, 256))
                if kt < 2:
                    vl = self.CV[:, kt, kv, :]
                elif kt == 2:
                    vl = self.HV[:, 1, kv * 128:(kv + 1) * 128]
                elif kt == 5:
                    vl = self.HV[:, 0, kv * 128:(kv + 1) * 128]
                else:
                    vl = vtv(4 + kt - 3)[:, kv * 128:(kv + 1) * 128]
                for e in range(2):
                    pe_ = P[:, e * 256 + q0:e * 256 + q0 + qn]
                    cx.mm(V(po[e].tiles, po[e].ap[:, q0:q0 + qn]), vl, pe_, kt == 0, kt == 5, inc=(kt == 5))
                    cx.mm(V(den[e].tiles, den[e].ap[:, q0:q0 + qn]), self.onesF.v(), pe_, kt == 0, kt == 5, inc=(kt == 5))
            finalize(c, c0, po, den)
        self.ps_rr_n = 6

        if self.cut <= 14:
            return
        QB, KB, LR = PG[16:18], PG[18:20], PG[20]
        KVp, LD, GP = PG[21:23], PG[23:25], PG[25:29]
        self.fm_linear(wfm, 6 * 128, 2, self.H, allr, lambda j, t0, tn, ps: self.ev_copy(QB[j][:, t0:t0 + tn], ps))
        self.fm_linear(wfm, 8 * 128, 2, self.H, allr, lambda j, t0, tn, ps: self.ev_copy(KB[j][:, t0:t0 + tn], ps))
        self.fm_linear(wfm, 10 * 128, 1, self.H, allr, lambda j, t0, tn, ps: self.ev_copy(LR[:, t0:t0 + tn], ps))
        ZS = self.S[6]
        if self.cut <= 15:
            return
        for sg in range(3):
            tiles = [2 * sg, 2 * sg + 1]
            self.tm_linear(wtm, 512, 768, self.H, tiles, lambda c, bw, tt, ps: self.ev_copy(KVp[tt % 2][:, c:c + bw], ps))
            for ci in range(2):
                self.gla_ld(LR, tiles[ci], LD[ci])
            if self.cut == 151:
                return
            for (ci, dr) in [(0, 0), (1, 0), (1, 1), (0, 1)]:
                self.gla_prep(dr, KVp[ci], LD[ci], GP[ci * 2 + dr], ci * 2 + dr, True, QB, KB, tiles[ci] * 128)
            if self.cut == 152:
                return
            Sf0 = ZS if sg < 2 else self.S[2]
            Sb0 = ZS if sg < 2 else self.S[3]
            Sf1, Sb1 = self.S[4], self.S[5]
            self.gla_update(Sf0, Sf1, GP[0], KVp[0], self.DEC[:, 0, :])
            self.gla_update(Sb0, Sb1, GP[3], KVp[1], self.DEC[:, 3, :])
            if self.cut == 153:
                return
            ats0 = self.gla_out_a(GP[0], GP[1])
            ats1 = self.gla_out_a(GP[2], GP[3])
            self.gla_out(KVp[0], GP[0], GP[1], Sf0, Sb1, CAT[4:8], tiles[0] * 128, ats0)
            self.gla_out(KVp[1], GP[2], GP[3], Sf1, Sb0, CAT[4:8], tiles[1] * 128, ats1)
            if sg < 2:
                self.gla_update(Sf1, None, GP[2], KVp[1], self.DEC[:, 2, :], out_f32=self.FIN[0][:, sg, :, :])
                self.gla_update(Sb1, None, GP[1], KVp[0], self.DEC[:, 1, :], out_f32=self.FIN[1][:, sg, :, :])
            if self.cut == 155:
                return
        cx.dma("sp", self.o_sf, self.FIN[0].ap.rearrange("p a b c -> p a (b c)"), self.dsem_out, reads=[self.FIN[0]])
        cx.dma("sp", self.o_sb, self.FIN[1].ap.rearrange("p a b c -> p a (b c)"), self.dsem_out, reads=[self.FIN[1]])
        if self.cut <= 16:
            return

        RH = PG[16:20]
        for h in range(4):
            sqp = PG[20 + h]
            cx.act(sqp.v(), CAT[4 + h].v().f(), AF.Square)
            for (t0, tn) in allr:
                ss = self.psum()
                cx.mm(ss[:, 0:tn], self.onesF.v(), sqp[:, t0:t0 + tn], True, True)
                self.rsqrt(RH[h][:, t0:t0 + tn], ss[:, 0:tn], 1.0 / 128, tn)
                oh = CAT[4 + h][:, t0:t0 + tn]
                cx.stt("dve", oh, oh.f(), self.gg[:, h:h + 1], RH[h][:, t0:t0 + tn].f(), ALU.mult, ALU.mult)

        def ev_gate(j, t0, tn, ps):
            o = CAT[4 + j][:, t0:t0 + tn]
            sgt = self.tmp()
            cx.act(sgt[:, 0:tn], ps, AF.Silu)
            cx.tt("dve", o, o.f(), sgt[:, 0:tn], ALU.mult)
        self.fm_linear(wfm, 11 * 128, 4, self.H, allr, ev_gate)
        if self.cut <= 17:
            return
        self.out_linear(self.d_wout, CAT, 0)

    def odd_mixer(self):
        cx = self.cx
        PG = self.PG
        wi = self.d_cmwin.rearrange("(kc p) n -> p kc n", p=128)
        U, VTM, GBp, WSB = PG[8:16], PG[16:24], PG[24:27], PG[27:29]
        allr = [(0, 512), (512, 256)]

        def gbv(which, cb):
            li = which * 4 + cb
            return GBp[li // 3][:, (li % 3) * 256:(li % 3 + 1) * 256]
        for which in range(2):
            for cb in range(4):
                v = gbv(which, cb)
                cx.dma("sp", v.ap, self.d_cmgb[:, which * 1024 + cb * 256: which * 1024 + (cb + 1) * 256].bitcast(F32R),
                       self.dsem_c2, writes=[v])
        WS = WSB[0][:, 0:512].re("p (g t) -> p g t", g=4)
        BS = WSB[1][:, 0:512].re("p (g t) -> p g t", g=4)
        cx.dma("sp", WSB[0][:, 0:512].ap, self.d_cmws.bitcast(F32R), self.dsem_c2, writes=[WSB[0]])
        cx.dma("sp", WSB[1][:, 0:512].ap, self.d_cmbs.bitcast(F32R), self.dsem_c2, writes=[WSB[1]])
        cx.group_fix(self.dsem_c2, list(GBp) + list(WSB))

        def vblk(tt, cb):
            li = tt * 4 + cb
            return VTM[li // 3][:, (li % 3) * 256:(li % 3 + 1) * 256]
        self.tm_linear(wi, 1024, 1024, self.H, list(range(6)), lambda c, bw, tt, ps: cx.act(vblk(tt, c // 256), ps, AF.Gelu))
        for tt in range(6):
            st = self.lnst[tt]
            for cb in range(4):
                v = vblk(tt, cb).f()
                cx.op("dve", lambda e, v=v, cb=cb, st=st: e.bn_stats(out=st.ap[:, 16 + cb * 6:16 + (cb + 1) * 6], in_=v.ap),
                      reads=[v], writes=[st])
            cx.op("dve", lambda e, st=st: e.bn_aggr(out=st.ap[:, 10:12], in_=st.ap[:, 16:40]), reads=[st], writes=[st])
        allst = V([k for t in self.lnst for k in _tiles([t])], self.lnst_all)
        cx.act(allst[:, :, 15], allst[:, :, 11], AF.Ln, bias=self.epsb[:, 0:1], scale=1.0)
        cx.act(allst[:, :, 13], allst[:, :, 15], AF.Exp, scale=-0.5)
        cx.stt("dve", allst[:, :, 14], allst[:, :, 10], -1.0, allst[:, :, 13], ALU.mult, ALU.mult)
        prev = None
        for tt in range(6):
            st = self.lnst[tt]
            for cb in range(4):
                v = vblk(tt, cb)
                t1 = self.tmp()
                cx.ts("dve", t1[:, 0:256], v.f(), st[:, 13:14], ALU.mult, st[:, 14:15], ALU.add)
                cx.tt("pool", t1[:, 0:256], t1[:, 0:256], gbv(0, cb).f(), ALU.mult)
                if prev is not None:
                    cx.tt("dve", prev[0], prev[1][:, 0:256], gbv(1, prev[2]).f(), ALU.add)
                prev = (v, t1, cb)
        cx.tt("dve", prev[0], prev[1][:, 0:256], gbv(1, prev[2]).f(), ALU.add)
        self.fm_linear(wi, 0, 8, self.H, allr, lambda j, t0, tn, ps: cx.act(U[j][:, t0:t0 + tn], ps, AF.Gelu))
        for cc in range(8):
            g = cc // 2
            for tiles in ([0, 1, 2, 3], [4, 5]):
                n = len(tiles)
                ps = self.psum()
                for ti, tt in enumerate(tiles):
                    cx.mm(ps[:, ti * 128:(ti + 1) * 128], vblk(tt, cc // 2)[:, (cc % 2) * 128:(cc % 2 + 1) * 128], WS[:, g, :], True, True)
                t1 = self.tmp()
                cx.tt("dve", t1[:, 0:n * 128].re("p (a b) -> p a b", a=n), ps[:, 0:n * 128].re("p (a b) -> p a b", a=n),
                      self.bcast(BS[:, g, :].f(), n), ALU.add)
                uu = U[cc][:, tiles[0] * 128:(tiles[-1] + 1) * 128]
                cx.tt("pool", uu, t1[:, 0:n * 128], uu.f(), ALU.mult)
        self.out_linear(self.d_cmwout, U, 1)

    def store_out(self):
        cx = self.cx
        for k in range(KC):
            cx.dma("sp", self.o_y[:, k, :], self.X[k].ap, self.dsem_out, reads=[self.X[k]])


def build_program(stage):
    nc0 = bass.Bass("TRN2", target_bir_lowering=False)
    nc0.dge_precook = False
    plan = Prog(nc0, True, None, stage).build()
    nc = bass.Bass("TRN2", target_bir_lowering=False)
    nc.dge_precook = False
    p = Prog(nc, False, plan, stage)
    p.build()
    build_program.last_log = p.cx.log
    return nc


def fm(x):
    t, d = x.shape
    return np.ascontiguousarray(x.reshape(t, d // 128, 128).transpose(2, 1, 0))


def fm_vec(v):
    sh = v.shape
    v2 = v.reshape(-1, sh[-1] // 128, 128)
    return np.ascontiguousarray(v2.transpose(2, 0, 1)).reshape(128, -1)


def _swap_idx():
    d = np.arange(64)
    a, bb, f = d // 32, (d % 32) // 16, d % 16
    return a * 32 + (1 - bb) * 16 + f


def _rope_tables(pos):
    f32 = np.float32
    half, nf = 32, 16
    inv = (f32(10000.0) ** (-np.arange(nf, dtype=f32) * f32(2.0) / f32(half))).astype(f32)
    row = (pos // 64).astype(f32)
    col = (pos % 64).astype(f32)
    d = np.arange(64)
    a, bb, f = d // 32, (d % 32) // 16, d % 16
    p = np.where(a[:, None] == 0, row[None, :], col[None, :]).astype(f32)
    ang = (p * inv[f][:, None]).astype(f32)
    cos = np.cos(ang).astype(f32)
    sin = np.sin(ang).astype(f32)
    sgn = np.where(bb == 0, -1.0, 1.0).astype(f32)[:, None]
    sins = (sin * sgn).astype(f32)
    return np.concatenate([cos, cos], 0), np.concatenate([sins, sins], 0)


def make_in_maps(inp):
    f32 = np.float32
    g = lambda k: np.asarray(inp[k], f32)
    x_prompt, x_sample, c, c_ctx = g("x_prompt"), g("x_sample"), g("c"), g("c_ctx")
    w_in = g("ev_w_in")[0]
    sw = _swap_idx()
    qa, ka, va = w_in[:, 0:512], w_in[:, 512:640], w_in[:, 640:768]
    qb, kb, vb, gb = w_in[:, 768:1024], w_in[:, 1024:1280], w_in[:, 1280:1792], w_in[:, 1792:2304]
    lrf, lrb = w_in[:, 2304:2320], w_in[:, 2320:2336]
    ka_dup = np.concatenate([ka[:, 0:64], ka[:, 0:64], ka[:, 64:128], ka[:, 64:128]], 1)
    qa_sw = qa.reshape(D, 8, 64)[:, :, sw].reshape(D, 512)
    ka_sw = ka.reshape(D, 2, 64)[:, :, sw].reshape(D, 128)
    ka_sw_dup = np.concatenate([ka_sw[:, 0:64], ka_sw[:, 0:64], ka_sw[:, 64:128], ka_sw[:, 64:128]], 1)
    lr = np.concatenate([lrf, lrf, lrb, lrb, lrf, lrf, lrb, lrb], 1)
    w_in_fm = np.ascontiguousarray(np.concatenate([qa, ka_dup, qb, kb, lr, gb, qa_sw, ka_sw_dup], 1))
    assert w_in_fm.shape == (D, 2688)
    va_dup = np.concatenate([va[:, 0:64], va[:, 0:64], va[:, 64:128], va[:, 64:128]], 1)
    w_in_tm = np.ascontiguousarray(np.concatenate([va_dup, ka, ka, kb, vb], 1))
    assert w_in_tm.shape == (D, 1280)
    jj, ii = np.meshgrid(np.arange(128), np.arange(128), indexing="ij")
    gmask = np.stack([(jj <= ii), (jj > ii), (jj >= ii), (jj < ii)], 1).astype(f32).reshape(128, 512)
    tri_prev = (jj >= ii).astype(f32)
    tri_next = (jj <= ii).astype(f32)
    wa = np.zeros((64, 256), f32)
    waf, wab = g("gla_wa_f")[0], g("gla_wa_b")[0]
    wa[0:16], wa[16:32], wa[32:48], wa[48:64] = waf, waf, wab, wab
    ba = np.zeros((64, 256), f32)
    ba[0:32] = g("gla_ba_f")[0][None, :]
    ba[32:64] = g("gla_ba_b")[0][None, :]
    sink = g("ev_sink")[0]
    sinkT = np.ascontiguousarray(np.stack([np.where(np.arange(128) < 64, sink[2 * cc], sink[2 * cc + 1]) for cc in range(4)], 1)).astype(f32)
    cm_gb = np.ascontiguousarray(np.broadcast_to(np.concatenate([g("cm_v_gain")[0], g("cm_v_bias")[0]])[None, :], (128, 2048))).astype(f32)
    cm_ws = np.ascontiguousarray(g("cm_w_s")[0].transpose(2, 0, 1).reshape(128, 512))
    cm_bs = np.ascontiguousarray(np.broadcast_to(g("cm_b_s")[0].reshape(1, 512), (128, 512))).astype(f32)
    shared = {
        "w_mod": np.ascontiguousarray(g("w_mod")),
        "b_mod": fm_vec(g("b_mod")),
        "npre": fm_vec(g("norm_pre")),
        "npost": fm_vec(g("norm_post")),
        "ffn_w_gate": np.ascontiguousarray(g("ffn_w_gate")),
        "ffn_w_up": np.ascontiguousarray(g("ffn_w_up")),
        "ffn_w_down": np.ascontiguousarray(g("ffn_w_down")),
        "ident": np.eye(128, dtype=f32),
        "w_in_fm": w_in_fm, "w_in_tm": w_in_tm,
        "ev_w_out": np.ascontiguousarray(g("ev_w_out")[0]),
        "gmask": gmask,
        "sinkT": sinkT,
        "gla_g": fm_vec(g("gla_norm")),
        "gla_wa": wa, "gla_ba": np.ascontiguousarray(ba),
        "cm_w_in": np.ascontiguousarray(g("cm_w_in")[0]),
        "cm_w_out": np.ascontiguousarray(g("cm_w_out")[0]),
        "cm_gb": cm_gb, "cm_ws": cm_ws, "cm_bs": cm_bs,
    }
    cache_k, cache_v = g("cache_k"), g("cache_v")
    sf, sbw = g("state_gla_fwd"), g("state_gla_bwd")
    maps = []
    for i in range(NCORE):
        b, q = i // 4, i % 4
        own = x_sample[b, 256 * q:256 * (q + 1)]
        main = np.concatenate([x_prompt[2 * i], x_prompt[2 * i + 1], own], axis=0)
        oth = np.concatenate([x_sample[b, 256 * ((q + r) % 4):256 * ((q + r) % 4) + 256] for r in (1, 2, 3)], axis=0)
        cond = np.stack([c_ctx, c[b]], axis=0)
        m = dict(shared)
        m["xm"] = fm(main)
        m["xo"] = fm(oth)
        m["cond"] = np.ascontiguousarray(cond.reshape(2, KC, 128).transpose(2, 1, 0))
        pos = np.concatenate([256 * q + np.arange(256), (256 * (q + 1) + np.arange(128)) % 1024,
                              (256 * q - 128 + np.arange(128)) % 1024])
        rc, rs = _rope_tables(pos)
        m["rope_c"], m["rope_s"] = np.ascontiguousarray(rc), np.ascontiguousarray(rs)
        vp, vn = f32(q > 0), f32(q < 3)
        m["tri"] = np.ascontiguousarray(np.stack([tri_prev, tri_next, tri_prev * vp, tri_next * vn], 1).reshape(128, 512))
        fl = np.zeros((128, 6), f32)
        for r in (1, 2, 3):
            fl[:, r - 1] = f32(r >= 4 - q)
            fl[:, 3 + r - 1] = f32(r <= 3 - q)
        m["flags"] = fl
        ck = cache_k[b, 0]
        ckT = ck.transpose(2, 1, 0)
        m["ctx_k"] = np.ascontiguousarray(np.concatenate([ckT, ckT], 0).reshape(128, 512))
        cv = cache_v[b, 0].reshape(2, 128, 2, 64).transpose(1, 0, 2, 3)
        m["ctx_v"] = np.ascontiguousarray(np.concatenate([cv, cv], 3).reshape(128, 512))
        for nm, stt in (("s0f", sf), ("s0b", sbw)):
            s0 = stt[b, 0]
            m[nm] = np.ascontiguousarray(s0.reshape(2, 2, 64, 128).transpose(1, 2, 0, 3).reshape(128, 256))
        maps.append(m)
    return maps


def assemble(results):
    f32 = np.float32
    y_prompt = np.zeros((16, 256, D), f32)
    y_sample = np.zeros((2, 1024, D), f32)
    nk = np.zeros((16, 1, 256, 2, 64), f32)
    nv = np.zeros((16, 1, 256, 2, 64), f32)
    nsf = np.zeros((16, 1, 4, 64, 128), f32)
    nsb = np.zeros((16, 1, 4, 64, 128), f32)
    for i, r in enumerate(results):
        b, q = i // 4, i % 4
        y = r["ym"].transpose(2, 1, 0).reshape(T, D)
        y_prompt[2 * i] = y[0:256]
        y_prompt[2 * i + 1] = y[256:512]
        y_sample[b, 256 * q:256 * (q + 1)] = y[512:768]
        kvo = r["kv_out"].transpose(1, 0, 2).reshape(512, 256)
        for sq in range(2):
            blk = kvo[sq * 256:(sq + 1) * 256]
            nk[2 * i + sq, 0] = blk[:, 0:128].reshape(256, 2, 64)
            nv[2 * i + sq, 0] = blk[:, 128:256].reshape(256, 2, 64)
        for arr, key in ((nsf, "sf_out"), (nsb, "sb_out")):
            o = r[key].reshape(128, 2, 2, 128)
            for sq in range(2):
                arr[2 * i + sq, 0] = o[:, sq].reshape(2, 64, 2, 128).transpose(2, 0, 1, 3).reshape(4, 64, 128)
    return (y_prompt, y_sample, nk, nv, nsf, nsb)


_NC_CACHE = {}


def kernel(**inputs):
    stage = int(os.environ.get("MK_STAGE", "99"))
    if stage not in _NC_CACHE:
        _NC_CACHE[stage] = build_program(stage)
    nc = _NC_CACHE[stage]
    maps = make_in_maps(inputs)
    res = run_bass_kernel_spmd(nc, maps, core_ids=list(range(NCORE)))
    kernel.last = res.results
    if stage < 99:
        return res.results
    return assemble(res.results)
```
